# Optimizing a Trainium2 kernel written in Bass

```python
import jax
import jax.numpy as jnp
from jax import lax
import numpy as np

D_MODEL = 2048
BATCH = 2
SEQ = 4096
DEPTH = 1

GRID_W = 64
CTX_LEN = 256
GLA_HEADS = 4
GLA_DK = D_MODEL // 2 // GLA_HEADS
GLA_DV = D_MODEL // GLA_HEADS
GLA_GATE_RANK = 16
GLA_GATE_TEMP = 16.0
GLA_CHUNK = 64
NA_HEADS = 16
NA_DH = 64
NA_KR = 8
NA_KC = 16
D_FF = 256 * ((8 * D_MODEL // 3 + 255) // 256)
ROPE_THETA = 10000.0
EPS = 1e-6
NEG_INF = -1e30
QK_W = GLA_HEADS * GLA_DK
V_W = GLA_HEADS * GLA_DV
NA_W = NA_HEADS * NA_DH
IN_SPLITS = (QK_W, QK_W, V_W, V_W, GLA_GATE_RANK, GLA_GATE_RANK, NA_W, NA_W, NA_W, D_MODEL, D_MODEL)
IN_WIDTH = sum(IN_SPLITS)
SPLIT_AT = [int(s) for s in np.cumsum(IN_SPLITS)[:-1]]
N_MOD = 9

kernel_name = 'hybrid_gla_natten_macaron_dit_block'


def rmsnorm(x, g):
    xf = x.astype(jnp.float32)
    y = xf * lax.rsqrt(jnp.mean(xf * xf, axis=-1, keepdims=True) + EPS)
    return (y * g.astype(jnp.float32)).astype(x.dtype)


def modulate(x, mod, i, g):
    return rmsnorm(x, g) * (1 + mod[3 * i + 1]) + mod[3 * i]


def residual(x, y, mod, i, g, w):
    return x + w * mod[3 * i + 2] * rmsnorm(y, g)


def swiglu(h, wg, wu, wd):
    return (jax.nn.silu(h @ wg) * (h @ wu)) @ wd


def to_heads(a, n):
    b, t, _ = a.shape
    return a.reshape(b, t, n, -1).transpose(0, 2, 1, 3)


def from_heads(a):
    b, n, t, d = a.shape
    return a.transpose(0, 2, 1, 3).reshape(b, t, n * d)


def flip_t(a):
    return jnp.flip(a, axis=2)


def rope_axis(x, pos):
    half = x.shape[-1] // 2
    freqs = ROPE_THETA ** (-jnp.arange(half, dtype=jnp.float32) / half)
    ang = pos.astype(jnp.float32)[:, None] * freqs
    cos, sin = jnp.cos(ang), jnp.sin(ang)
    x1, x2 = x[..., :half], x[..., half:]
    return jnp.concatenate([x1 * cos - x2 * sin, x2 * cos + x1 * sin], axis=-1)


def rope_2d(x, row, col):
    h = x.shape[-1] // 2
    return jnp.concatenate([rope_axis(x[..., :h], row), rope_axis(x[..., h:], col)], axis=-1)


def gla_scan(q, k, v, log_a, s0):
    b, h, t, dk = q.shape
    dv = v.shape[-1]
    n = t // GLA_CHUNK
    rs = lambda a: a.reshape(b, h, n, GLA_CHUNK, a.shape[-1])
    q, k, v, log_a = rs(q), rs(k), rs(v), rs(log_a)
    cum = jnp.cumsum(log_a, axis=3)
    cum_last = cum[:, :, :, -1:, :]
    q_dec = q * jnp.exp(cum)
    k_inv = k * jnp.exp(-cum)
    k_to_end = k * jnp.exp(cum_last - cum)
    causal_in_chunk = jnp.tril(jnp.ones((GLA_CHUNK, GLA_CHUNK), dtype=bool))
    att = jnp.einsum('bhnid,bhnjd->bhnij', q_dec, k_inv)
    att = jnp.where(causal_in_chunk, att, 0.0)
    o_intra = jnp.einsum('bhnij,bhnjv->bhniv', att, v)
    kv_chunk = jnp.einsum('bhncd,bhncv->bhndv', k_to_end, v)
    decay_chunk = jnp.exp(cum_last[:, :, :, 0, :])

    def step(state, xs):
        qc, kvc, dc = xs
        o = jnp.einsum('bhcd,bhdv->bhcv', qc, state)
        return dc[..., None] * state + kvc, o

    xs = (jnp.moveaxis(q_dec, 2, 0), jnp.moveaxis(kv_chunk, 2, 0), jnp.moveaxis(decay_chunk, 2, 0))
    s_fin, o_inter = lax.scan(step, s0, xs)
    o = o_intra + jnp.moveaxis(o_inter, 0, 2)
    return o.reshape(b, h, t, dv), s_fin


def gla_streams(q, k, v, gf, gb, gla_wg, gla_bg, pos):
    q = to_heads(q, GLA_HEADS).astype(jnp.float32) * (GLA_DK ** -0.5)
    k = to_heads(k, GLA_HEADS).astype(jnp.float32)
    if pos is not None:
        q, k = rope_2d(q, pos[0], pos[1]), rope_2d(k, pos[0], pos[1])
    v = to_heads(v, GLA_HEADS).astype(jnp.float32)

    def log_decay(g, j):
        z = (g @ gla_wg[j] + gla_bg[j]).astype(jnp.float32)
        return to_heads(jax.nn.log_sigmoid(z) / GLA_GATE_TEMP, GLA_HEADS)

    return q, k, v, log_decay(gf, 0), log_decay(gb, 1)


def gla_readout(o, r, g, w_o):
    o = from_heads(rmsnorm(o, g)).astype(r.dtype)
    return (o * jax.nn.silu(r)) @ w_o


def neighbourhood_attention(q, k, v, kc, vc, rpb):
    b, t, _ = q.shape
    rows = t // GRID_W
    kr = min(NA_KR, rows)
    grid = lambda a: to_heads(a, NA_HEADS).reshape(b, NA_HEADS, rows, GRID_W, NA_DH)
    qg, kg, vg = grid(q), grid(k), grid(v)
    r = jnp.arange(rows)
    row_idx = jnp.clip(r - kr // 2, 0, rows - kr)[:, None] + jnp.arange(kr)[None, :]
    k_band = kg[:, :, row_idx]
    v_band = vg[:, :, row_idx]
    j = jnp.arange(GRID_W)
    col_start = jnp.clip(j - NA_KC // 2, 0, GRID_W - NA_KC)
    col_ok = (j[None, :] >= col_start[:, None]) & (j[None, :] < col_start[:, None] + NA_KC)
    di = row_idx - r[:, None]
    dj = jnp.clip(j[None, :] - j[:, None], -(NA_KC - 1), NA_KC - 1)
    bias = rpb[:, di[:, None, :, None] + NA_KR - 1, dj[None, :, None, :] + NA_KC - 1]
    scale = NA_DH ** -0.5
    s_lat = jnp.einsum('bhrqd,bhrakd->bhrqak', qg, k_band).astype(jnp.float32) * scale + bias.astype(jnp.float32)
    s_lat = jnp.where(col_ok[:, None, :], s_lat, NEG_INF)
    kch, vch = to_heads(kc, NA_HEADS), to_heads(vc, NA_HEADS)
    s_ctx = jnp.einsum('bhrqd,bhcd->bhrqc', qg, kch).astype(jnp.float32) * scale
    n_lat = kr * GRID_W
    s = jnp.concatenate([s_lat.reshape(b, NA_HEADS, rows, GRID_W, n_lat), s_ctx], axis=-1)
    p = jax.nn.softmax(s, axis=-1).astype(v.dtype)
    p_lat = p[..., :n_lat].reshape(b, NA_HEADS, rows, GRID_W, kr, GRID_W)
    o = jnp.einsum('bhrqak,bhrakd->bhrqd', p_lat, v_band) + jnp.einsum('bhrqc,bhcd->bhrqd', p[..., n_lat:], vch)
    return from_heads(o.reshape(b, NA_HEADS, t, NA_DH))


def context_attention(q, k, v):
    qh, kh, vh = to_heads(q, NA_HEADS), to_heads(k, NA_HEADS), to_heads(v, NA_HEADS)
    s = jnp.einsum('bhqd,bhkd->bhqk', qh, kh).astype(jnp.float32) * (NA_DH ** -0.5)
    p = jax.nn.softmax(s, axis=-1).astype(v.dtype)
    return from_heads(jnp.einsum('bhqk,bhkd->bhqd', p, vh))


def token_mixer(hl, hc, w_in, gla_wg, gla_bg, gla_norm_g, w_gla_o, na_rpb, w_na_o, w_out, row, col, ctx_out):
    ql, kl, vl, rl, gfl, gbl, nql, nkl, nvl, m1l, m2l = jnp.split(hl @ w_in, SPLIT_AT, axis=-1)
    qc, kc, vc, rc, gfc, gbc, nqc, nkc, nvc, m1c, m2c = jnp.split(hc @ w_in, SPLIT_AT, axis=-1)
    qc4, kc4, vc4, afc, abc = gla_streams(qc, kc, vc, gfc, gbc, gla_wg, gla_bg, None)
    ql4, kl4, vl4, afl, abl = gla_streams(ql, kl, vl, gfl, gbl, gla_wg, gla_bg, (row, col))
    s0 = jnp.zeros((hl.shape[0], GLA_HEADS, GLA_DK, GLA_DV), jnp.float32)
    oc_f, sc_f = gla_scan(qc4, kc4, vc4, afc, s0)
    oc_b, sc_b = gla_scan(flip_t(qc4), flip_t(kc4), flip_t(vc4), flip_t(abc), s0)
    ol_f, _ = gla_scan(ql4, kl4, vl4, afl, sc_f)
    ol_b, _ = gla_scan(flip_t(ql4), flip_t(kl4), flip_t(vl4), flip_t(abl), sc_b)
    a_l = gla_readout(ol_f + flip_t(ol_b), rl, gla_norm_g, w_gla_o)
    b_l = neighbourhood_attention(nql, nkl, nvl, nkc, nvc, na_rpb) @ w_na_o
    yl = (jax.nn.sigmoid(m1l) * a_l + jax.nn.sigmoid(m2l) * b_l) @ w_out
    if not ctx_out:
        return yl, None
    a_c = gla_readout(oc_f + flip_t(oc_b), rc, gla_norm_g, w_gla_o)
    b_c = context_attention(nqc, nkc, nvc) @ w_na_o
    yc = (jax.nn.sigmoid(m1c) * a_c + jax.nn.sigmoid(m2c) * b_c) @ w_out
    return yl, yc


def setup_inputs(seed: int = 0) -> dict:
    key = jax.random.key(seed)
    ks = jax.random.split(key, 18)
    nrm = lambda k, shape, s: jax.random.normal(k, shape, jnp.float32) * s
    D, L = D_MODEL, DEPTH
    return {
        'x': nrm(ks[0], (BATCH, SEQ, D), 1.0),
        'c': nrm(ks[1], (BATCH, D), 1.0),
        'ctx': nrm(ks[2], (BATCH, CTX_LEN, D), 1.0),
        'c_ctx': nrm(ks[3], (D,), 1.0),
        'w_ada': nrm(ks[4], (L, D, N_MOD * D), D ** -0.5),
        'b_ada': nrm(ks[5], (L, N_MOD * D), 0.02),
        'norm_g': 1.0 + nrm(ks[6], (L, 6, D), 0.02),
        'ffn_wg': nrm(ks[7], (L, 2, D, D_FF), D ** -0.5),
        'ffn_wu': nrm(ks[8], (L, 2, D, D_FF), D ** -0.5),
        'ffn_wd': nrm(ks[9], (L, 2, D_FF, D), D_FF ** -0.5),
        'w_in': nrm(ks[10], (L, D, IN_WIDTH), D ** -0.5),
        'gla_wg': nrm(ks[11], (L, 2, GLA_GATE_RANK, QK_W), GLA_GATE_RANK ** -0.5),
        'gla_bg': nrm(ks[12], (L, 2, QK_W), 0.1),
        'gla_norm_g': 1.0 + nrm(ks[13], (L, GLA_DV), 0.02),
        'w_gla_o': nrm(ks[14], (L, V_W, D), V_W ** -0.5),
        'na_rpb': nrm(ks[15], (L, NA_HEADS, 2 * NA_KR - 1, 2 * NA_KC - 1), 0.1),
        'w_na_o': nrm(ks[16], (L, NA_W, D), NA_W ** -0.5),
        'w_out': nrm(ks[17], (L, D, D), D ** -0.5),
    }


def reference(x, c, ctx, c_ctx, w_ada, b_ada, norm_g, ffn_wg, ffn_wu, ffn_wd, w_in, gla_wg, gla_bg,
              gla_norm_g, w_gla_o, na_rpb, w_na_o, w_out):
    t = jnp.arange(x.shape[1])
    row, col = t // GRID_W, t % GRID_W
    silu_c, silu_cc = jax.nn.silu(c), jax.nn.silu(c_ctx)
    h, hc = x, ctx
    for l in range(DEPTH):
        ctx_out = l < DEPTH - 1
        ml = jnp.moveaxis((silu_c @ w_ada[l] + b_ada[l]).reshape(-1, N_MOD, D_MODEL), 1, 0)[:, :, None, :]
        mc = (silu_cc @ w_ada[l] + b_ada[l]).reshape(N_MOD, D_MODEL)[:, None, None, :]
        g = norm_g[l]
        h = residual(h, swiglu(modulate(h, ml, 0, g[0]), ffn_wg[l, 0], ffn_wu[l, 0], ffn_wd[l, 0]), ml, 0, g[1], 0.5)
        hc = residual(hc, swiglu(modulate(hc, mc, 0, g[0]), ffn_wg[l, 0], ffn_wu[l, 0], ffn_wd[l, 0]), mc, 0, g[1], 0.5)
        yl, yc = token_mixer(modulate(h, ml, 1, g[2]), modulate(hc, mc, 1, g[2]), w_in[l], gla_wg[l], gla_bg[l],
                             gla_norm_g[l], w_gla_o[l], na_rpb[l], w_na_o[l], w_out[l], row, col, ctx_out)
        h = residual(h, yl, ml, 1, g[3], 1.0)
        h = residual(h, swiglu(modulate(h, ml, 2, g[4]), ffn_wg[l, 1], ffn_wu[l, 1], ffn_wd[l, 1]), ml, 2, g[5], 0.5)
        if ctx_out:
            hc = residual(hc, yc, mc, 1, g[3], 1.0)
            hc = residual(hc, swiglu(modulate(hc, mc, 2, g[4]), ffn_wg[l, 1], ffn_wu[l, 1], ffn_wd[l, 1]), mc, 2, g[5], 0.5)
    return h
```

```python
import os
import contextlib
import numpy as np
import ml_dtypes
import concourse.bass as bass
import concourse.mybir as mybir
from concourse.bass_utils import run_bass_kernel_spmd

F32 = mybir.dt.float32
BF16 = mybir.dt.bfloat16
AF = mybir.ActivationFunctionType
ALU = mybir.AluOpType

D = 2048
KC = 16
DFF = 5632
FC = 44
NT = 1088
NL = 1024
GROUPS = [(0, 512), (512, 512), (1024, 64)]
EPS = 1e-6
SEQ = 4096
CTX = 256
NS = SEQ + CTX


class Prog:
    def __init__(self):
        self.ops = []
        self.lastw = {}
        self.readers = {}

    def op(self, eng, fn, r=(), w=(), dma=False, cc=False):
        i = len(self.ops)
        deps = set()
        for k in r:
            if k in self.lastw:
                deps.add(self.lastw[k])
        for k in w:
            if k in self.lastw:
                deps.add(self.lastw[k])
            deps.update(self.readers.get(k, ()))
        for k in r:
            self.readers.setdefault(k, []).append(i)
        for k in w:
            self.lastw[k] = i
            self.readers[k] = []
        self.ops.append(dict(eng=eng, fn=fn, deps=deps, dma=dma, cc=cc, barrier=False))
        return i

    def barrier(self):
        self.ops.append(dict(barrier=True))
        self.lastw = {}
        self.readers = {}

    def emit(self, nc, st):
        ops = self.ops
        engs = ('pe', 'act', 'dve', 'pool', 'sp')
        needed = set()
        for o in ops:
            if not o['barrier']:
                needed |= o['deps']
        last = {}
        for i, o in enumerate(ops):
            if o['barrier']:
                for e in engs:
                    if e in last:
                        needed.add(last[e])
                last = {}
            elif not o['dma'] and not o['cc']:
                last[o['eng']] = i
        esem = {e: st.enter_context(nc.semaphore("es_" + e)) for e in engs}
        NQ = 20
        dpool = {q: [st.enter_context(nc.semaphore("dq_%s_%d" % (q, j))) for j in range(NQ)]
                 for q in ('sp', 'pool', 'act')}
        dcnt = {}
        rr = {q: 0 for q in dpool}
        tick = {e: 0 for e in engs}
        ccs = []
        for i, o in enumerate(ops):
            if o['barrier']:
                o['ticks'] = dict(tick)
                o['dmas'] = dict(dcnt)
                o['ccs'] = list(ccs)
            elif o['cc']:
                s = st.enter_context(nc.semaphore("cc_%d" % i))
                o['sem'] = s
                o['val'] = 1
                o['prev'] = 0
                ccs.append(s)
            elif o['dma']:
                q = o['eng']
                s = dpool[q][rr[q] % NQ]
                rr[q] += 1
                o['prev'] = dcnt.get(s, 0)
                dcnt[s] = o['prev'] + 16
                o['sem'] = s
                o['val'] = dcnt[s]
            elif i in needed:
                tick[o['eng']] += 1
                o['tick'] = tick[o['eng']]

        def run(E, e):
            waited = {}

            def w(sem, val):
                key = id(sem)
                if waited.get(key, 0) < val:
                    e.wait_ge(sem, val)
                    waited[key] = val

            for i, o in enumerate(ops):
                if o['barrier']:
                    for F in engs:
                        if o['ticks'][F] > 0:
                            w(esem[F], o['ticks'][F])
                    for s, c in o['dmas'].items():
                        w(s, c)
                    for s in o['ccs']:
                        w(s, 1)
                    continue
                if o['eng'] != E:
                    continue
                for d in sorted(o['deps']):
                    od = ops[d]
                    if od['dma'] or od['cc']:
                        w(od['sem'], od['val'])
                    else:
                        if od['eng'] == E and E == 'pe':
                            continue
                        w(esem[od['eng']], od['tick'])
                if (o['dma']) and o['prev'] > 0:
                    w(o['sem'], o['prev'])
                ins = o['fn'](e)
                if o['cc']:
                    ins.then_inc(o['sem'])
                elif o['dma']:
                    ins.then_inc(o['sem'], 16)
                elif 'tick' in o:
                    ins.then_inc(esem[E], 1)

        with nc.Block() as block:
            @block.tensor
            def _(e):
                run('pe', e)

            @block.scalar
            def _(e):
                run('act', e)

            @block.vector
            def _(e):
                run('dve', e)

            @block.gpsimd
            def _(e):
                run('pool', e)

            @block.sync
            def _(e):
                run('sp', e)


class Arena:
    def __init__(self, t, nwords):
        self.t = t
        self.n = nwords
        self.off = 0

    def reset(self, off=0):
        self.off = off

    def f32(self, n, parts=128):
        a = self.t[0:parts, self.off:self.off + n]
        self.off += n
        assert self.off <= self.n, ("arena overflow", self.off)
        return a

    def bf16(self, n, parts=128):
        words = (n + 1) // 2
        a = self.t[0:parts, self.off:self.off + words].bitcast(BF16)
        self.off += words
        assert self.off <= self.n, ("arena overflow", self.off)
        return a


def build(mode='full', debug=(), ncores=8):
    nc = bass.Bass("TRN2", target_bir_lowering=False)
    P = Prog()
    st = contextlib.ExitStack()

    def din(name, shape, dt=F32):
        return nc.dram_tensor(name, list(shape), dt, kind="ExternalInput")

    def dint(name, shape, dt=F32):
        return nc.dram_tensor(name, list(shape), dt)

    full = (mode == 'full')
    xin = din("xin", [NT, D] if full else [1])
    cT = din("cT", [128, KC, 2] if full else [1])
    w_ada = din("w_ada", [D, 9 * D] if full else [1])
    bT = din("bT", [128, 144] if full else [1])
    gT = din("gT", [128, 6, KC])
    ffn_wg = din("ffn_wg", [2, D, DFF] if full else [1])
    ffn_wu = din("ffn_wu", [2, D, DFF] if full else [1])
    ffn_wd = din("ffn_wd", [2, DFF, D] if full else [1])
    identf = din("identf", [128, 128])
    groups = [[0, 1, 2, 3], [4, 5, 6, 7]] if ncores == 8 else [[0, 1, 2, 3]]
    w_inh = din("w_inh", [D, 2336])
    w_m = din("w_m", [D, 4096])
    gwg = din("gwg", [16, 2, 256])
    gbgT = din("gbgT", [128, 4])
    gng = din("gng", [1, 512])
    ropeC = din("ropeC", [2, 128, SEQ])
    ropeS = din("ropeS", [2, 128, SEQ])
    rotT_in = din("rotT", [128, 128])
    cmask_in = din("cmask", [128, 512])
    trm_in = din("trm", [64, 128])
    tb_in = din("tb", [4, 64, 15 * 64])
    w_gla_o = din("w_gla_o", [D, D])
    w_na_o = din("w_na_o", [1024, D])
    w_out = din("w_out", [D, D])
    identb_in = din("identb", [128, 128], BF16)
    hm_in = din("hm_in", [D, NT], BF16) if mode != 'full' else None
    hmT_d = dint("hmT_d", [D, NT], BF16)
    hmT_all = dint("hmT_all", [8 * 1024, NT], BF16)
    qT_d = dint("qT_d", [256, NS])
    kT_d = dint("kT_d", [256, NS])
    gT_d = dint("gT_d", [32, NS])
    v_d = dint("v_d", [NS, 512], BF16)
    sr_d = dint("sr_d", [NS, 512])
    nv_d = dint("nv_d", [NS, 256], BF16)
    nqT_d = dint("nqT_d", [256, NS], BF16)
    nkT_d = dint("nkT_d", [256, NS], BF16)
    of_d = dint("of_d", [SEQ, 512])
    mixT_d = dint("mixT_d", [4 * 768, 1024], BF16)
    mixin_d = dint("mixin_d", [8 * 1536, 1024], BF16)
    sel_in = din("sel", [128, 4])
    out = nc.dram_tensor("out", [NL, D], F32, kind="ExternalOutput")

    hT_d = dint("hT_d", [KC, 128, NT])
    yT_d = dint("yT_d", [KC, 128, NT])
    dbg = {}
    for name, shape, dt in debug:
        dbg[name] = nc.dram_tensor("dbg_" + name, list(shape), dt, kind="ExternalOutput")

    AW = 50400
    arena_t = st.enter_context(nc.sbuf_tensor("arena", [128, AW], F32))
    A = Arena(arena_t, AW)
    cst = st.enter_context(nc.sbuf_tensor("cst", [128, 768], F32))
    ident = cst[:, 0:128]
    modT = cst[:, 128:128 + 288].rearrange("p (r j) -> p r j", r=2)
    Amod = cst[:, 416:416 + 96].rearrange("p (r s k) -> p r s k", r=2, s=3)
    Gmod = cst[:, 512:512 + 96].rearrange("p (r s k) -> p r s k", r=2, s=3)
    gTs = cst[:, 608:608 + 96].rearrange("p (s k) -> p s k", s=6)
    cst_b = st.enter_context(nc.sbuf_tensor("cst_b", [128, 256], BF16))
    ones_bf = cst_b[:, 0:128]
    identb = cst_b[:, 128:256]
    cst2 = st.enter_context(nc.sbuf_tensor("cst2", [128, 1800], F32))
    rotT = cst2[:, 0:128]
    cmask = cst2[:, 128:640]
    trm = cst2[0:64, 640:768].rearrange("p (a b) -> p a b", a=2)
    nbg = cst2[:, 768:772]
    one1 = cst2[:, 772:773]
    wgT = cst2[0:16, 776:776 + 512].rearrange("p (a b) -> p a b", a=2)
    gnb = cst2[0:64, 1288:1288 + 512]
    ps = [st.enter_context(nc.psum_tensor("ps%d" % i, [128, 512], F32)) for i in range(8)]
    PSK = [("ps", i) for i in range(8)]

    P.op('sp', lambda e: e.dma_start(out=ident, in_=identf[:, :]), w=["ident"], dma=True)
    P.op('sp', lambda e: e.dma_start(out=gTs, in_=gT[:, :, :]), w=["gTs"], dma=True)
    P.op('pool', lambda e: e.memset(ones_bf, 1.0), w=["ones"])
    epsb = cst[:, 704:705]
    P.op('pool', lambda e: e.memset(epsb, EPS), w=["epsb"])
    P.op('sp', lambda e: e.dma_start(out=identb, in_=identb_in[:, :]), w=["identb"], dma=True)
    P.op('sp', lambda e: e.dma_start(out=rotT, in_=rotT_in[:, :]), w=["rotT"], dma=True)
    P.op('sp', lambda e: e.dma_start(out=cmask, in_=cmask_in[:, :]), w=["cmask"], dma=True)
    P.op('sp', lambda e: e.dma_start(out=cst2[0:64, 640:768], in_=trm_in[:, :]), w=["trm"], dma=True)
    P.op('sp', lambda e: e.dma_start(out=nbg, in_=gbgT[:, :]), w=["nbg"], dma=True)
    P.op('dve', lambda e: e.tensor_scalar(out=nbg, in0=nbg, scalar1=-1.0, scalar2=None, op0=ALU.mult),
         r=["nbg"], w=["nbg"])
    P.op('pool', lambda e: e.memset(one1, 1.0), w=["one1"])
    P.op('sp', lambda e: e.dma_start(out=cst2[0:16, 776:776 + 512], in_=gwg.ap().rearrange("k a b -> k (a b)")),
         w=["wgT"], dma=True)
    P.op('sp', lambda e: e.dma_start(out=gnb, in_=gng.ap().broadcast_to([64, 512])), w=["gnb"], dma=True)
    if mode != 'full':
        P.barrier()
    if mode == 'full':
        A.reset()
        cTs = A.f32(32).rearrange("p (k r) -> p k r", r=2)
        sTs = A.f32(32).rearrange("p (k r) -> p k r", r=2)
        bTs = A.f32(144)
        P.op('sp', lambda e: e.dma_start(out=cTs, in_=cT[:, :, :]), w=["cTs"], dma=True)
        P.op('sp', lambda e: e.dma_start(out=bTs, in_=bT[:, :]), w=["bTs"], dma=True)
        P.op('act', lambda e: e.activation(out=sTs, in_=cTs, func=AF.Silu), r=["cTs"], w=["sTs"])
        wv = w_ada.ap().rearrange("(k p) c -> p k c", p=128)
        wst = [A.f32(KC * 512).rearrange("p (k c) -> p k c", k=KC) for _ in range(2)]
        modps = ps[0][:, 0:288].rearrange("p (j r) -> p j r", r=2)
        for cg in range(36):
            wb = wst[cg % 2]
            key = ("wst", cg % 2)
            P.op('sp', lambda e, wb=wb, cg=cg: e.dma_start(out=wb, in_=wv[:, :, cg * 512:(cg + 1) * 512]),
                 w=[key], dma=True)
            for jj in range(4):
                j = cg * 4 + jj
                for k in range(KC):
                    P.op('pe', lambda e, wb=wb, jj=jj, j=j, k=k: e.matmul(
                        modps[:, j, :], lhsT=wb[:, k, jj * 128:(jj + 1) * 128], rhs=sTs[:, k, :],
                        start=(k == 0), stop=(k == KC - 1)), r=[key, "sTs"], w=[PSK[0]])
        for r in range(2):
            P.op('dve', lambda e, r=r: e.tensor_tensor(out=modT[:, r, :], in0=modps[:, :, r], in1=bTs,
                                                        op=ALU.add), r=[PSK[0], "bTs"], w=["modT"])
        WRES = (0.5, 1.0, 0.5)
        for r in range(2):
            for s in range(3):
                P.op('dve', lambda e, r=r, s=s: e.scalar_tensor_tensor(
                    out=Amod[:, r, s, :], in0=modT[:, r, (3 * s + 1) * 16:(3 * s + 2) * 16], scalar=1.0,
                    in1=gTs[:, 2 * s, :], op0=ALU.add, op1=ALU.mult), r=["modT", "gTs"], w=["Amod"])
                P.op('dve', lambda e, r=r, s=s: e.scalar_tensor_tensor(
                    out=Gmod[:, r, s, :], in0=modT[:, r, (3 * s + 2) * 16:(3 * s + 3) * 16], scalar=WRES[s],
                    in1=gTs[:, 2 * s + 1, :], op0=ALU.mult, op1=ALU.mult), r=["modT", "gTs"], w=["Gmod"])
        P.barrier()

    def Bmod(r, s, k):
        return modT[:, r, (3 * s) * 16 + k:(3 * s) * 16 + k + 1]

    def prenorm_modulate(HT, s, hm, sqkey="hm"):
        rstd = A.f32(NT)
        tmp = [A.f32(NT) for _ in range(2)]
        for k in range(KC):
            P.op('act', lambda e, k=k: e.activation(out=hm[:, k, :], in_=HT[:, k, :], func=AF.Square),
                 r=["HT"], w=["hm"])
        for gi, (c0, n) in enumerate(GROUPS):
            for k in range(KC):
                P.op('pe', lambda e, k=k, c0=c0, n=n, gi=gi: e.matmul(
                    ps[gi][:, 0:n], lhsT=ones_bf, rhs=hm[:, k, c0:c0 + n], start=(k == 0), stop=(k == KC - 1)),
                    r=["hm", "ones"], w=[PSK[gi]])
            P.op('act', lambda e, c0=c0, n=n, gi=gi: e.activation(
                out=rstd[:, c0:c0 + n], in_=ps[gi][:, 0:n], func=AF.Sqrt, bias=epsb, scale=1.0 / D),
                r=[PSK[gi], "epsb"], w=["rstd"])
        P.op('dve', lambda e: e.reciprocal(out=rstd, in_=rstd), r=["rstd"], w=["rstd"])
        for k in range(KC):
            t = tmp[k % 2]
            tk = ("tmpn", k % 2)
            P.op('dve', lambda e, k=k, t=t: e.tensor_tensor(out=t, in0=HT[:, k, :], in1=rstd, op=ALU.mult),
                 r=["HT", "rstd"], w=[tk])
            for r, (c0, n) in enumerate([(0, NL), (NL, NT - NL)]):
                P.op('act', lambda e, k=k, t=t, r=r, c0=c0, n=n: e.activation(
                    out=hm[:, k, c0:c0 + n], in_=t[:, c0:c0 + n], func=AF.Identity,
                    bias=Bmod(r, s, k), scale=Amod[:, r, s, k:k + 1]),
                    r=[tk, "Amod", "modT"], w=["hm"])

    HM_W = KC * NT // 2
    ACT_W = FC * NT // 2

    def ffn_core(s, li):
        wg = ffn_wg.ap()[li].rearrange("(k p) f -> p k f", p=128)
        wu = ffn_wu.ap()[li].rearrange("(k p) f -> p k f", p=128)
        wd = ffn_wd.ap()[li].rearrange("(f p) d -> p f d", p=128)
        A.reset(0)
        hm = A.bf16(KC * NT).rearrange("p (k t) -> p k t", k=KC)
        actT = A.bf16(FC * NT).rearrange("p (f t) -> p f t", f=FC)
        mark = A.off
        wsf = [A.f32(KC * 128).rearrange("p (k c) -> p k c", k=KC) for _ in range(4)]
        wsb = [A.bf16(KC * 128).rearrange("p (k c) -> p k c", k=KC) for _ in range(4)]
        sg = [A.f32(512) for _ in range(2)]
        slot = 0
        nsg = 0
        for f in range(FC):
            wb = []
            for gu, wsrc in enumerate((wg, wu)):
                i = (2 * f + gu) % 4
                P.op('sp', lambda e, i=i, wsrc=wsrc, f=f: e.dma_start(
                    out=wsf[i], in_=wsrc[:, :, f * 128:(f + 1) * 128]), w=[("wsf", i)], dma=True)
                P.op('pool', lambda e, i=i: e.tensor_copy(out=wsb[i], in_=wsf[i]),
                     r=[("wsf", i)], w=[("wsb", i)])
                wb.append(i)
            for (c0, n) in GROUPS:
                pg, pu = 2 + 2 * (slot % 3), 3 + 2 * (slot % 3)
                slot += 1
                for gu, pb in enumerate((pg, pu)):
                    for k in range(KC):
                        P.op('pe', lambda e, pb=pb, k=k, c0=c0, n=n, i=wb[gu]: e.matmul(
                            ps[pb][:, 0:n], lhsT=wsb[i][:, k, :], rhs=hm[:, k, c0:c0 + n],
                            start=(k == 0), stop=(k == KC - 1)),
                            r=[("wsb", wb[gu]), "hm"], w=[PSK[pb]])
                sgi = nsg % 2
                nsg += 1
                P.op('act', lambda e, pg=pg, n=n, sgi=sgi: e.activation(out=sg[sgi][:, 0:n], in_=ps[pg][:, 0:n],
                                                                        func=AF.Silu),
                     r=[PSK[pg]], w=[("sg", sgi)])
                P.op('dve', lambda e, pu=pu, n=n, sgi=sgi, f=f, c0=c0: e.tensor_tensor(
                    out=actT[:, f, c0:c0 + n], in0=sg[sgi][:, 0:n], in1=ps[pu][:, 0:n], op=ALU.mult),
                    r=[("sg", sgi), PSK[pu]], w=["act"])
        P.barrier()
        A.reset(0)
        ysb = [A.f32(NT) for _ in range(2)]
        hsb = [A.f32(NT) for _ in range(2)]
        ysq = [A.bf16(NT) for _ in range(2)]
        rstd = A.f32(NT)
        assert A.off <= HM_W
        A.reset(mark)
        wdf = [A.f32(FC * 128).rearrange("p (f c) -> p f c", f=FC) for _ in range(2)]
        wdb = [A.bf16(FC * 128).rearrange("p (f c) -> p f c", f=FC) for _ in range(2)]
        slot = 0
        for dc in range(KC):
            i = dc % 2
            P.op('sp', lambda e, i=i, dc=dc: e.dma_start(out=wdf[i], in_=wd[:, :, dc * 128:(dc + 1) * 128]),
                 w=[("wdf", i)], dma=True)
            P.op('pool', lambda e, i=i: e.tensor_copy(out=wdb[i], in_=wdf[i]), r=[("wdf", i)], w=[("wdb", i)])
            for gi, (c0, n) in enumerate(GROUPS):
                pb = 3 + (slot % 5)
                slot += 1
                for f in range(FC):
                    P.op('pe', lambda e, pb=pb, f=f, c0=c0, n=n, i=i: e.matmul(
                        ps[pb][:, 0:n], lhsT=wdb[i][:, f, :], rhs=actT[:, f, c0:c0 + n],
                        start=(f == 0), stop=(f == FC - 1)),
                        r=[("wdb", i), "act"], w=[PSK[pb]])
                P.op('act', lambda e, pb=pb, c0=c0, n=n, i=i: e.activation(
                    out=ysb[i][:, c0:c0 + n], in_=ps[pb][:, 0:n], func=AF.Copy), r=[PSK[pb]], w=[("ysb", i)])
                P.op('act', lambda e, pb=pb, c0=c0, n=n, i=i: e.activation(
                    out=ysq[i][:, c0:c0 + n], in_=ps[pb][:, 0:n], func=AF.Square), r=[PSK[pb]], w=[("ysq", i)])
            for gi, (c0, n) in enumerate(GROUPS):
                P.op('pe', lambda e, gi=gi, c0=c0, n=n, i=i, dc=dc: e.matmul(
                    ps[gi][:, 0:n], lhsT=ones_bf, rhs=ysq[i][:, c0:c0 + n], start=(dc == 0), stop=(dc == KC - 1)),
                    r=[("ysq", i), "ones"], w=[PSK[gi]])
            P.op('pool', lambda e, i=i, dc=dc: e.dma_start(out=yT_d[dc, :, :], in_=ysb[i]),
                 r=[("ysb", i)], w=[("yT_d", dc)], dma=True)
        for gi, (c0, n) in enumerate(GROUPS):
            P.op('act', lambda e, c0=c0, n=n, gi=gi: e.activation(
                out=rstd[:, c0:c0 + n], in_=ps[gi][:, 0:n], func=AF.Sqrt, bias=epsb, scale=1.0 / D),
                r=[PSK[gi], "epsb"], w=["rstd2"])
        P.op('dve', lambda e: e.reciprocal(out=rstd, in_=rstd), r=["rstd2"], w=["rstd2"])
        residual_update(s, rstd, ysb, hsb)

    def residual_update(s, rstd, ysb, hsb, ncols=NT):
        ranges = [(0, 0, NL)] + ([(1, NL, NT - NL)] if ncols == NT else [])
        for dc in range(KC):
            i = dc % 2
            yb, hb = ysb[i], hsb[i]
            P.op('sp', lambda e, yb=yb, dc=dc: e.dma_start(out=yb[:, 0:ncols], in_=yT_d[dc, :, 0:ncols]),
                 r=[("yT_d", dc)], w=[("ysb", i)], dma=True)
            P.op('sp', lambda e, hb=hb, dc=dc: e.dma_start(out=hb[:, 0:ncols], in_=hT_d[dc, :, 0:ncols]),
                 r=[("hT_d", dc)], w=[("hsb", i)], dma=True)
            P.op('dve', lambda e, yb=yb: e.tensor_tensor(out=yb[:, 0:ncols], in0=yb[:, 0:ncols], in1=rstd[:, 0:ncols],
                                                         op=ALU.mult),
                 r=[("ysb", i), "rstd2"], w=[("ysb", i)])
            for (r, c0, n) in ranges:
                P.op('dve', lambda e, yb=yb, hb=hb, dc=dc, r=r, c0=c0, n=n: e.scalar_tensor_tensor(
                    out=hb[:, c0:c0 + n], in0=yb[:, c0:c0 + n], scalar=Gmod[:, r, s, dc:dc + 1],
                    in1=hb[:, c0:c0 + n], op0=ALU.mult, op1=ALU.add),
                    r=[("ysb", i), "Gmod", ("hsb", i)], w=[("hsb", i)])
            P.op('pool', lambda e, hb=hb, dc=dc: e.dma_start(out=hT_d[dc, :, 0:ncols], in_=hb[:, 0:ncols]),
                 r=[("hsb", i)], w=[("hT_d", dc)], dma=True)
        P.barrier()

    def prenorm_stage(s, from_x):
        A.reset(0)
        hm = A.bf16(KC * NT).rearrange("p (k t) -> p k t", k=KC)
        HT = A.f32(KC * NT).rearrange("p (k t) -> p k t", k=KC)
        if from_x:
            xt = [A.f32(D) for _ in range(2)]
            tiles = [(i * 128, 128) for i in range(8)] + [(NL, 64)]
            slot = 0
            for ti, (t0, pn) in enumerate(tiles):
                xb = xt[ti % 2]
                P.op('sp', lambda e, xb=xb, t0=t0, pn=pn: e.dma_start(out=xb[0:pn, :], in_=xin[t0:t0 + pn, :]),
                     w=[("xt", ti % 2)], dma=True)
                for kq in range(4):
                    pb = 4 + (slot % 4)
                    slot += 1
                    for kk in range(4):
                        k = kq * 4 + kk
                        P.op('pe', lambda e, xb=xb, pn=pn, pb=pb, kk=kk, k=k: e.transpose(
                            ps[pb][:, kk * 128:kk * 128 + pn], xb[0:pn, k * 128:(k + 1) * 128], ident[0:pn, 0:pn]),
                            r=[("xt", ti % 2), "ident"], w=[PSK[pb]])
                    src = ps[pb][:, :].rearrange("p (a b) -> p a b", a=4)[:, :, 0:pn]
                    dst = HT[:, kq * 4:(kq + 1) * 4, t0:t0 + pn]
                    if kq % 2 == 0:
                        P.op('act', lambda e, src=src, dst=dst: e.activation(out=dst, in_=src, func=AF.Copy),
                             r=[PSK[pb]], w=["HT"])
                    else:
                        P.op('dve', lambda e, src=src, dst=dst: e.tensor_copy(out=dst, in_=src),
                             r=[PSK[pb]], w=["HT"])
            for dc in range(KC):
                P.op('pool', lambda e, dc=dc: e.dma_start(out=hT_d[dc, :, :], in_=HT[:, dc, :]),
                     r=["HT"], w=[("hT_d", dc)], dma=True)
        else:
            for dc in range(KC):
                P.op('sp', lambda e, dc=dc: e.dma_start(out=HT[:, dc, :], in_=hT_d[dc, :, :]),
                     r=[("hT_d", dc)], w=["HT"], dma=True)
        prenorm_modulate(HT, s, hm)
        P.barrier()
        return hm

    def tok_groups():
        gl = [(0, 256, [(r, 1024, 64, r * 64) for r in range(4)])]
        for g in range(1, 9):
            gl.append((256 + 512 * (g - 1), 512, [((g - 1) // 2, ((g - 1) % 2) * 512, 512, 0)]))
        return gl

    def mixer_inproj():
        A.reset(0)
        NW = 2336
        WB = A.bf16(KC * NW).rearrange("p (k c) -> p k c", k=KC)
        wstg = [A.f32(NW) for _ in range(2)]
        wv_ = w_inh.ap().rearrange("(k p) c -> p k c", p=128)
        for k in range(KC):
            P.op('sp', lambda e, k=k: e.dma_start(out=wstg[k % 2], in_=wv_[:, k, :]), w=[("wstg", k % 2)], dma=True)
            P.op('pool', lambda e, k=k: e.tensor_copy(out=WB[:, k, :], in_=wstg[k % 2]),
                 r=[("wstg", k % 2)], w=["WB"])
        X = [A.bf16(KC * 512).rearrange("p (k t) -> p k t", k=KC) for _ in range(2)]
        ev = [A.f32(512) for _ in range(4)]
        evb = [A.bf16(512) for _ in range(4)]
        cnt = dict(pb=0, ev=0, evb=0)

        def nxt(kind, mod):
            v = cnt[kind] % mod
            cnt[kind] += 1
            return v

        FM = [(0, 128, 'q', 0), (128, 128, 'q', 1), (256, 128, 'k', 0), (384, 128, 'k', 1), (512, 32, 'g', 0),
              (544, 128, 'nq', 0), (672, 128, 'nq', 1), (800, 128, 'nk', 0), (928, 128, 'nk', 1)]
        TM = [(1056, 512, 'v'), (1568, 512, 'r'), (2080, 256, 'nv')]
        for gi, (n0, n, srcs) in enumerate(tok_groups()):
            Xb = X[gi % 2]
            xk = ("X", gi % 2)
            for (r, c0, nn, x0) in srcs:
                for c in range(8):
                    P.op('sp', lambda e, Xb=Xb, r=r, c0=c0, nn=nn, x0=x0, c=c: e.dma_start(
                        out=Xb[:, 2 * c:2 * c + 2, x0:x0 + nn],
                        in_=hmT_all[c * 1024 + r * 256:c * 1024 + (r + 1) * 256, c0:c0 + nn].rearrange(
                            "(k p) t -> p k t", p=128)), w=[xk], dma=True)
            for (c0, M, kind, idx) in FM:
                if gi == 0 and kind in ('q', 'nq'):
                    continue
                pb = nxt('pb', 8)
                for k in range(KC):
                    P.op('pe', lambda e, pb=pb, M=M, n=n, k=k, c0=c0, Xb=Xb: e.matmul(
                        ps[pb][0:M, 0:n], lhsT=WB[:, k, c0:c0 + M], rhs=Xb[:, k, 0:n],
                        start=(k == 0), stop=(k == KC - 1)), r=["WB", xk], w=[PSK[pb]])
                if kind in ('q', 'k', 'g'):
                    j = nxt('ev', 4)
                    dst = {'q': qT_d, 'k': kT_d, 'g': gT_d}[kind]
                    P.op('act', lambda e, pb=pb, M=M, n=n, j=j: e.activation(out=ev[j][0:M, 0:n], in_=ps[pb][0:M, 0:n],
                                                                           func=AF.Copy), r=[PSK[pb]], w=[("ev", j)])
                    P.op('pool', lambda e, dst=dst, idx=idx, M=M, n=n, n0=n0, j=j: e.dma_start(
                        out=dst[idx * 128:idx * 128 + M, n0:n0 + n], in_=ev[j][0:M, 0:n]),
                        r=[("ev", j)], w=["qkg_d"], dma=True)
                else:
                    j = nxt('evb', 4)
                    dst = {'nq': nqT_d, 'nk': nkT_d}[kind]
                    sc = 0.125 if kind == 'nq' else 1.0
                    P.op('act', lambda e, pb=pb, n=n, j=j, sc=sc: e.activation(
                        out=evb[j][:, 0:n], in_=ps[pb][:, 0:n], func=AF.Copy, scale=sc), r=[PSK[pb]], w=[("evb", j)])
                    P.op('pool', lambda e, dst=dst, idx=idx, n=n, n0=n0, j=j: e.dma_start(
                        out=dst[idx * 128:(idx + 1) * 128, n0:n0 + n], in_=evb[j][:, 0:n]),
                        r=[("evb", j)], w=["qkg_d"], dma=True)
            for tt in range(n // 128):
                for (c0, ncol, kind) in TM:
                    pb = nxt('pb', 8)
                    for k in range(KC):
                        P.op('pe', lambda e, pb=pb, ncol=ncol, k=k, c0=c0, Xb=Xb, tt=tt: e.matmul(
                            ps[pb][:, 0:ncol], lhsT=Xb[:, k, tt * 128:(tt + 1) * 128], rhs=WB[:, k, c0:c0 + ncol],
                            start=(k == 0), stop=(k == KC - 1)), r=["WB", xk], w=[PSK[pb]])
                    r0 = n0 + tt * 128
                    if kind == 'r':
                        j = nxt('ev', 4)
                        P.op('act', lambda e, pb=pb, j=j: e.activation(out=ev[j], in_=ps[pb][:, 0:512], func=AF.Silu),
                             r=[PSK[pb]], w=[("ev", j)])
                        P.op('pool', lambda e, r0=r0, j=j: e.dma_start(out=sr_d[r0:r0 + 128, :], in_=ev[j]),
                             r=[("ev", j)], w=["qkg_d"], dma=True)
                    else:
                        j = nxt('evb', 4)
                        dst = v_d if kind == 'v' else nv_d
                        P.op('dve', lambda e, pb=pb, j=j, ncol=ncol: e.tensor_copy(out=evb[j][:, 0:ncol],
                                                                                  in_=ps[pb][:, 0:ncol]),
                             r=[PSK[pb]], w=[("evb", j)])
                        P.op('pool', lambda e, dst=dst, r0=r0, j=j, ncol=ncol: e.dma_start(
                            out=dst[r0:r0 + 128, :], in_=evb[j][:, 0:ncol]), r=[("evb", j)], w=["qkg_d"], dma=True)
        P.barrier()

    def gla():
        A.reset(0)
        S = A.f32(1024).rearrange("p (c v) -> p c v", c=2)
        Sb = A.bf16(1024).rearrange("p (c v) -> p c v", c=2)
        f3 = lambda: A.f32(1024).rearrange("p (c t) -> p c t", c=2)
        b3 = lambda: A.bf16(1024).rearrange("p (c t) -> p c t", c=2)
        qT, kT, Ct, St = f3(), f3(), f3(), f3()
        e_, sp, cum, d3, E1, E2, E3, t1, qr, kr = [f3() for _ in range(10)]
        qd, ki, ke = b3(), b3(), b3()
        gts = A.f32(512, parts=16)
        kend = A.bf16(8 * 256, parts=64).rearrange("p (c d) -> p c d", c=8)
        vb = A.bf16(8 * 512, parts=64).rearrange("p (c v) -> p c v", c=8)
        dch = A.f32(16).rearrange("p (c n) -> p c n", c=2)
        att_sb = [A.bf16(64, parts=64) for _ in range(2)]
        o_sb = [A.f32(512, parts=64) for _ in range(2)]
        of_g = A.f32(8 * 512, parts=64).rearrange("p (c v) -> p c v", c=8)
        sr_g = A.f32(8 * 512, parts=64).rearrange("p (c v) -> p c v", c=8)
        gcT = A.bf16(4 * 512).rearrange("p (a t) -> p a t", a=4)
        res = [A.bf16(512, parts=64) for _ in range(2)]
        junk = A.f32(512, parts=64)
        ssq = [A.f32(1, parts=64) for _ in range(2)]
        glist = tok_groups()
        ci = 0
        for dirn in (0, 1):
            P.op('pool', lambda e: e.memset(S, 0.0), w=["S"])
            P.op('pool', lambda e: e.memset(Sb, 0.0), w=["Sb"])
            order = [0] + (list(range(1, 9)) if dirn == 0 else list(range(8, 0, -1)))
            for g in order:
                n0, n, _ = glist[g]
                nch = n // 64
                lat = g > 0
                tok0 = n0 - 256
                P.op('sp', lambda e, n0=n0, n=n: e.dma_start(
                    out=kT[:, :, 0:n], in_=kT_d.ap().rearrange("(c p) t -> p c t", p=128)[:, :, n0:n0 + n]),
                    r=["qkg_d"], w=["kT"], dma=True)
                P.op('sp', lambda e, n0=n0, n=n, dirn=dirn: e.dma_start(
                    out=gts[:, 0:n], in_=gT_d[dirn * 16:(dirn + 1) * 16, n0:n0 + n]), r=["qkg_d"], w=["gts"], dma=True)
                P.op('sp', lambda e, n0=n0, n=n, nch=nch: e.dma_start(
                    out=vb[:, 0:nch, :], in_=v_d[n0:n0 + n, :].rearrange("(c p) v -> p c v", p=64)),
                    r=["qkg_d"], w=["vb"], dma=True)
                if lat:
                    P.op('sp', lambda e, n0=n0, n=n: e.dma_start(
                        out=qT[:, :, 0:n], in_=qT_d.ap().rearrange("(c p) t -> p c t", p=128)[:, :, n0:n0 + n]),
                        r=["qkg_d"], w=["qT"], dma=True)
                    P.op('sp', lambda e, tok0=tok0, n=n: e.dma_start(
                        out=Ct[:, :, 0:n], in_=ropeC.ap().rearrange("c p t -> p c t")[:, :, tok0:tok0 + n]),
                        w=["Ct"], dma=True)
                    P.op('sp', lambda e, tok0=tok0, n=n: e.dma_start(
                        out=St[:, :, 0:n], in_=ropeS.ap().rearrange("c p t -> p c t")[:, :, tok0:tok0 + n]),
                        w=["St"], dma=True)
                    if dirn == 1:
                        P.op('sp', lambda e, tok0=tok0, n=n: e.dma_start(
                            out=of_g, in_=of_d[tok0:tok0 + n, :].rearrange("(c p) v -> p c v", p=64)),
                            r=["of_d"], w=["of_g"], dma=True)
                        P.op('sp', lambda e, n0=n0, n=n: e.dma_start(
                            out=sr_g, in_=sr_d[n0:n0 + n, :].rearrange("(c p) v -> p c v", p=64)),
                            r=["qkg_d"], w=["sr_g"], dma=True)
                        for c in range(8):
                            P.op('dve', lambda e, c=c: e.tensor_tensor(out=sr_g[:, c, :], in0=sr_g[:, c, :], in1=gnb,
                                                                       op=ALU.mult), r=["sr_g", "gnb"], w=["sr_g"])
                for dc in range(2):
                    P.op('pe', lambda e, dc=dc, n=n, dirn=dirn: e.matmul(
                        ps[dc][:, 0:n], lhsT=wgT[:, dirn, dc * 128:(dc + 1) * 128], rhs=gts[:, 0:n],
                        start=True, stop=True), r=["gts", "wgT"], w=[PSK[dc]])
                    P.op('act', lambda e, dc=dc, n=n, dirn=dirn: e.activation(
                        out=e_[:, dc, 0:n], in_=ps[dc][:, 0:n], func=AF.Exp, scale=-1.0,
                        bias=nbg[:, dirn * 2 + dc:dirn * 2 + dc + 1]), r=[PSK[dc], "nbg"], w=["e_"])
                    P.op('act', lambda e, dc=dc, n=n: e.activation(
                        out=sp[:, dc, 0:n], in_=e_[:, dc, 0:n], func=AF.Ln, bias=one1, scale=1.0),
                        r=["e_", "one1"], w=["sp"])
                    P.op('dve', lambda e, dc=dc, n=n: e.tensor_tensor_scan(
                        out=cum[:, dc, 0:n], data0=cmask[:, 0:n], data1=sp[:, dc, 0:n], initial=0.0,
                        op0=ALU.mult, op1=ALU.add), r=["sp", "cmask"], w=["cum"])
                cum4 = cum[:, :, 0:n].rearrange("p c (h t) -> p c h t", t=64)
                tot = cum4[:, :, :, 63:64]
                for dc in range(2):
                    P.op('dve', lambda e, dc=dc, n=n, nch=nch: e.tensor_tensor(
                        out=d3[:, dc, 0:n].rearrange("p (h t) -> p h t", t=64),
                        in0=cum[:, dc, 0:n].rearrange("p (h t) -> p h t", t=64),
                        in1=cum[:, dc, 0:n].rearrange("p (h t) -> p h t", t=64)[:, :, 63:64].to_broadcast([128, nch, 64]),
                        op=ALU.subtract), r=["cum"], w=["d3"])
                    P.op('act', lambda e, dc=dc, n=n, nch=nch: e.activation(
                        out=dch[:, dc, 0:nch], in_=cum[:, dc, 0:n].rearrange("p (h t) -> p h t", t=64)[:, :, 63],
                        func=AF.Exp, scale=-1.0 / 16), r=["cum"], w=["dch"])
                if dirn == 0:
                    P.op('act', lambda e, n=n: e.activation(out=E1[:, :, 0:n], in_=cum[:, :, 0:n], func=AF.Exp,
                                                            scale=-1.0 / 16), r=["cum"], w=["E1"])
                    P.op('act', lambda e, n=n: e.activation(out=E2[:, :, 0:n], in_=cum[:, :, 0:n], func=AF.Exp,
                                                            scale=1.0 / 16), r=["cum"], w=["E2"])
                    P.op('act', lambda e, n=n: e.activation(out=E3[:, :, 0:n], in_=d3[:, :, 0:n], func=AF.Exp,
                                                            scale=1.0 / 16), r=["d3"], w=["E3"])
                else:
                    P.op('dve', lambda e, n=n: e.tensor_tensor(out=t1[:, :, 0:n], in0=sp[:, :, 0:n], in1=d3[:, :, 0:n],
                                                               op=ALU.subtract), r=["sp", "d3"], w=["t1"])
                    P.op('act', lambda e, n=n: e.activation(out=E1[:, :, 0:n], in_=t1[:, :, 0:n], func=AF.Exp,
                                                            scale=-1.0 / 16), r=["t1"], w=["E1"])
                    P.op('act', lambda e, n=n: e.activation(out=E2[:, :, 0:n], in_=t1[:, :, 0:n], func=AF.Exp,
                                                            scale=1.0 / 16), r=["t1"], w=["E2"])
                    P.op('dve', lambda e, n=n: e.tensor_tensor(out=d3[:, :, 0:n], in0=cum[:, :, 0:n], in1=sp[:, :, 0:n],
                                                               op=ALU.subtract), r=["sp", "cum", "E1"], w=["d3"])
                    P.op('act', lambda e, n=n: e.activation(out=E3[:, :, 0:n], in_=d3[:, :, 0:n], func=AF.Exp,
                                                            scale=-1.0 / 16), r=["d3"], w=["E3"])
                if lat:
                    for (src, dstr, nm) in ((qT, qr, "q"), (kT, kr, "k")):
                        for dc in range(2):
                            pb = 2 + dc
                            P.op('pe', lambda e, pb=pb, dc=dc, n=n, src=src: e.matmul(
                                ps[pb][:, 0:n], lhsT=rotT, rhs=src[:, dc, 0:n], start=True, stop=True),
                                r=[nm + "T", "rotT"], w=[PSK[pb]])
                            P.op('dve', lambda e, pb=pb, dc=dc, n=n: e.tensor_tensor(
                                out=t1[:, dc, 0:n], in0=ps[pb][:, 0:n], in1=St[:, dc, 0:n], op=ALU.mult),
                                r=[PSK[pb], "St", "E1", "E2"], w=["t1"])
                            P.op('pool', lambda e, dc=dc, n=n, src=src, dstr=dstr: e.tensor_tensor(
                                out=dstr[:, dc, 0:n], in0=src[:, dc, 0:n], in1=Ct[:, dc, 0:n], op=ALU.mult),
                                r=[nm + "T", "Ct"], w=[nm + "r"])
                            P.op('dve', lambda e, dc=dc, n=n, dstr=dstr: e.tensor_tensor(
                                out=dstr[:, dc, 0:n], in0=dstr[:, dc, 0:n], in1=t1[:, dc, 0:n], op=ALU.add),
                                r=[nm + "r", "t1"], w=[nm + "r"])
                    ksrc, kname = kr, "kr"
                    for dc in range(2):
                        P.op('dve', lambda e, dc=dc, n=n: e.scalar_tensor_tensor(
                            out=qd[:, dc, 0:n], in0=qr[:, dc, 0:n], scalar=0.0625, in1=E1[:, dc, 0:n],
                            op0=ALU.mult, op1=ALU.mult), r=["qr", "E1"], w=["qd"])
                        P.op('dve', lambda e, dc=dc, n=n: e.tensor_tensor(
                            out=ki[:, dc, 0:n], in0=kr[:, dc, 0:n], in1=E2[:, dc, 0:n], op=ALU.mult),
                            r=["kr", "E2"], w=["ki"])
                else:
                    ksrc, kname = kT, "kT"
                for dc in range(2):
                    P.op('pool', lambda e, dc=dc, n=n, ksrc=ksrc: e.tensor_tensor(
                        out=ke[:, dc, 0:n], in0=ksrc[:, dc, 0:n], in1=E3[:, dc, 0:n], op=ALU.mult),
                        r=[kname, "E3"], w=["ke"])
                for half in range((nch + 3) // 4):
                    pb = 2 + half
                    pv = ps[pb][0:64, :].bitcast(BF16).rearrange("p (c d) -> p c d", c=4)
                    for cc in range(4):
                        c = half * 4 + cc
                        for dc in range(2):
                            P.op('pe', lambda e, pv=pv, cc=cc, c=c, dc=dc: e.transpose(
                                pv[:, cc, dc * 128:(dc + 1) * 128], ke[:, dc, c * 64:(c + 1) * 64], identb),
                                r=["ke", "identb"], w=[PSK[pb]])
                    P.op('act', lambda e, pv=pv, half=half: e.activation(out=kend[:, half * 4:half * 4 + 4, :], in_=pv,
                                                                        func=AF.Copy), r=[PSK[pb]], w=["kend"])
                clist = list(range(nch)) if dirn == 0 else list(range(nch - 1, -1, -1))
                for c in clist:
                    cs = slice(c * 64, (c + 1) * 64)
                    j = ci % 2
                    ci += 1
                    pa, po = (2, 3) if j == 0 else (6, 7)
                    if lat:
                        for dc in range(2):
                            P.op('pe', lambda e, pa=pa, dc=dc, cs=cs: e.matmul(
                                ps[pa][0:64, 0:64], lhsT=ki[:, dc, cs], rhs=qd[:, dc, cs],
                                start=(dc == 0), stop=(dc == 1)), r=["ki", "qd"], w=[PSK[pa]])
                        P.op('dve', lambda e, pa=pa, j=j, dirn=dirn: e.tensor_tensor(
                            out=att_sb[j], in0=ps[pa][0:64, 0:64], in1=trm[:, dirn, :], op=ALU.mult),
                            r=[PSK[pa], "trm"], w=[("att", j)])
                        P.op('pe', lambda e, po=po, j=j, c=c: e.matmul(
                            ps[po][0:64, 0:512], lhsT=att_sb[j], rhs=vb[:, c, :], start=True, stop=False),
                            r=[("att", j), "vb"], w=[PSK[po]])
                        for dc in range(2):
                            P.op('pe', lambda e, po=po, dc=dc, cs=cs: e.matmul(
                                ps[po][0:64, 0:512], lhsT=qd[:, dc, cs], rhs=Sb[:, dc, :],
                                start=False, stop=(dc == 1)), r=["qd", "Sb"], w=[PSK[po]])
                    for dc in range(2):
                        P.op('pe', lambda e, dc=dc, c=c: e.matmul(
                            ps[4 + dc][:, 0:512], lhsT=kend[:, c, dc * 128:(dc + 1) * 128], rhs=vb[:, c, :],
                            start=True, stop=True), r=["kend", "vb"], w=[PSK[4 + dc]])
                        P.op('dve', lambda e, dc=dc, c=c: e.scalar_tensor_tensor(
                            out=S[:, dc, :], in0=S[:, dc, :], scalar=dch[:, dc, c:c + 1], in1=ps[4 + dc][:, 0:512],
                            op0=ALU.mult, op1=ALU.add), r=["S", "dch", PSK[4 + dc]], w=["S"])
                        P.op('act', lambda e, dc=dc: e.activation(out=Sb[:, dc, :], in_=S[:, dc, :], func=AF.Copy),
                             r=["S"], w=["Sb"])
                    if lat and dirn == 0:
                        P.op('act', lambda e, po=po, j=j: e.activation(out=o_sb[j], in_=ps[po][0:64, 0:512],
                                                                      func=AF.Copy), r=[PSK[po]], w=[("o_sb", j)])
                        P.op('pool', lambda e, j=j, tok0=tok0, c=c: e.dma_start(
                            out=of_d[tok0 + c * 64:tok0 + (c + 1) * 64, :], in_=o_sb[j]),
                            r=[("o_sb", j)], w=["of_d"], dma=True)
                    elif lat:
                        P.op('dve', lambda e, po=po, j=j, c=c: e.tensor_tensor(
                            out=o_sb[j], in0=ps[po][0:64, 0:512], in1=of_g[:, c, :], op=ALU.add),
                            r=[PSK[po], "of_g"], w=[("o_sb", j)])
                        P.op('act', lambda e, j=j: e.activation(out=junk, in_=o_sb[j], func=AF.Square,
                                                                accum_out=ssq[j]), r=[("o_sb", j)], w=["junk", ("ssq", j)])
                        P.op('act', lambda e, j=j: e.activation(out=ssq[j], in_=ssq[j], func=AF.Sqrt, bias=epsb[0:64, :],
                                                                scale=1.0 / 512), r=[("ssq", j), "epsb"], w=[("ssq", j)])
                        P.op('dve', lambda e, j=j: e.reciprocal(out=ssq[j], in_=ssq[j]), r=[("ssq", j)], w=[("ssq", j)])
                        P.op('dve', lambda e, j=j, c=c: e.scalar_tensor_tensor(
                            out=res[j], in0=o_sb[j], scalar=ssq[j], in1=sr_g[:, c, :], op0=ALU.mult, op1=ALU.mult),
                            r=[("o_sb", j), ("ssq", j), "sr_g"], w=[("res", j)])
                        pv = ps[pa][:, 0:128].bitcast(BF16).rearrange("p (a t) -> p a t", a=4)
                        for a in range(4):
                            P.op('pe', lambda e, pv=pv, a=a, j=j: e.transpose(
                                pv[:, a, :], res[j][:, a * 128:(a + 1) * 128], identb[0:64, 0:64]),
                                r=[("res", j), "identb"], w=[PSK[pa]])
                        P.op('act', lambda e, pv=pv, cs=cs: e.activation(out=gcT[:, :, cs], in_=pv, func=AF.Copy),
                             r=[PSK[pa]], w=["gcT"])
                if lat and dirn == 1:
                    qt, c0 = tok0 // 1024, tok0 % 1024
                    P.op('pool', lambda e, qt=qt, c0=c0: e.dma_start(
                        out=mixT_d[qt * 768:qt * 768 + 512, c0:c0 + 512].rearrange("(a p) t -> p a t", p=128),
                        in_=gcT), r=["gcT"], w=["mixT_d"], dma=True)
        P.barrier()

    def na():
        A.reset(0)
        nq = A.bf16(SEQ, parts=64)
        nk = A.bf16(NS, parts=64)
        Ve = A.bf16(34 * 64).rearrange("p (b d) -> p b d", b=34)
        Vo = A.bf16(31 * 64).rearrange("p (b d) -> p b d", b=31)
        TBh = A.f32(15 * 64, parts=64)
        naT = A.bf16(SEQ, parts=64)
        s_sb = [A.f32(512, parts=64) for _ in range(2)]
        Pm = [A.bf16(768, parts=64) for _ in range(2)]
        PT = [A.bf16(6 * 64).rearrange("p (b q) -> p b q", b=6) for _ in range(2)]
        osb = [A.bf16(64, parts=64) for _ in range(2)]
        sm = [A.f32(8, parts=64) for _ in range(2)]
        for hh in range(4):
            hs = slice(hh * 64, (hh + 1) * 64)
            P.op('sp', lambda e, hs=hs: e.dma_start(out=nq, in_=nqT_d[hs, 256:NS]), r=["qkg_d"], w=["nq"], dma=True)
            P.op('sp', lambda e, hs=hs: e.dma_start(out=nk, in_=nkT_d[hs, :]), r=["qkg_d"], w=["nk"], dma=True)
            P.op('sp', lambda e, hs=hs: e.dma_start(
                out=Ve, in_=nv_d[:, hs].rearrange("(b p) d -> p b d", p=128)), r=["qkg_d"], w=["Ve"], dma=True)
            P.op('sp', lambda e, hs=hs: e.dma_start(
                out=Vo, in_=nv_d[320:320 + 31 * 128, hs].rearrange("(b p) d -> p b d", p=128)),
                r=["qkg_d"], w=["Vo"], dma=True)
            P.op('sp', lambda e, hh=hh: e.dma_start(out=TBh, in_=tb_in[hh, :, :]), w=["TBh"], dma=True)
            for r in range(64):
                j = r % 2
                pl, pc, pt, po = (0, 1, 2, 3) if j == 0 else (4, 5, 6, 7)
                j0 = min(max(r - 4, 0), 56)
                s0 = j0 - r + 7
                qs = slice(r * 64, (r + 1) * 64)
                P.op('pe', lambda e, pl=pl, qs=qs, j0=j0: e.matmul(
                    ps[pl][0:64, 0:512], lhsT=nq[:, qs], rhs=nk[:, 256 + j0 * 64:256 + j0 * 64 + 512],
                    start=True, stop=True), r=["nq", "nk"], w=[PSK[pl]])
                P.op('pe', lambda e, pc=pc, qs=qs: e.matmul(
                    ps[pc][0:64, 0:256], lhsT=nq[:, qs], rhs=nk[:, 0:256], start=True, stop=True),
                    r=["nq", "nk"], w=[PSK[pc]])
                P.op('dve', lambda e, pl=pl, j=j, s0=s0: e.tensor_tensor(
                    out=s_sb[j], in0=ps[pl][0:64, 0:512], in1=TBh[:, s0 * 64:(s0 + 8) * 64], op=ALU.add),
                    r=[PSK[pl], "TBh"], w=[("s_sb", j)])
                P.op('dve', lambda e, j=j: e.reduce_max(out=sm[j][:, 0:1], in_=s_sb[j], axis=mybir.AxisListType.X),
                     r=[("s_sb", j)], w=[("sm", j)])
                P.op('dve', lambda e, j=j, pc=pc: e.reduce_max(out=sm[j][:, 1:2], in_=ps[pc][0:64, 0:256],
                                                               axis=mybir.AxisListType.X),
                     r=[PSK[pc]], w=[("sm", j)])
                P.op('dve', lambda e, j=j: e.tensor_tensor(out=sm[j][:, 2:3], in0=sm[j][:, 0:1], in1=sm[j][:, 1:2],
                                                           op=ALU.max), r=[("sm", j)], w=[("sm", j)])
                P.op('dve', lambda e, j=j: e.tensor_scalar(out=sm[j][:, 3:4], in0=sm[j][:, 2:3], scalar1=-1.0,
                                                           scalar2=None, op0=ALU.mult), r=[("sm", j)], w=[("sm", j)])
                P.op('act', lambda e, j=j: e.activation(out=Pm[j][:, 0:512], in_=s_sb[j], func=AF.Exp,
                                                        bias=sm[j][:, 3:4], scale=1.0, accum_out=sm[j][:, 4:5]),
                     r=[("s_sb", j), ("sm", j)], w=[("Pm", j), ("sm2", j)])
                P.op('act', lambda e, j=j, pc=pc: e.activation(out=Pm[j][:, 512:768], in_=ps[pc][0:64, 0:256], func=AF.Exp,
                                                               bias=sm[j][:, 3:4], scale=1.0, accum_out=sm[j][:, 5:6]),
                     r=[PSK[pc], ("sm", j)], w=[("Pm", j), ("sm2", j)])
                pv = ps[pt][:, 0:192].bitcast(BF16).rearrange("p (b q) -> p b q", b=6)
                for blk in range(6):
                    P.op('pe', lambda e, pv=pv, blk=blk, j=j: e.transpose(
                        pv[:, blk, :], Pm[j][:, blk * 128:(blk + 1) * 128], identb[0:64, 0:64]),
                        r=[("Pm", j), "identb"], w=[PSK[pt]])
                P.op('dve', lambda e, pv=pv, j=j: e.tensor_copy(out=PT[j], in_=pv), r=[PSK[pt]], w=[("PT", j)])
                for blk in range(6):
                    if blk < 4:
                        vsrc = Ve[:, 2 + j0 // 2 + blk, :] if j0 % 2 == 0 else Vo[:, (j0 - 1) // 2 + blk, :]
                    else:
                        vsrc = Ve[:, blk - 4, :]
                    P.op('pe', lambda e, po=po, blk=blk, j=j, vsrc=vsrc: e.matmul(
                        ps[po][0:64, 0:64], lhsT=PT[j][:, blk, :], rhs=vsrc, start=(blk == 0), stop=(blk == 5)),
                        r=[("PT", j), "Ve", "Vo"], w=[PSK[po]])
                P.op('dve', lambda e, j=j: e.tensor_tensor(out=sm[j][:, 6:7], in0=sm[j][:, 4:5], in1=sm[j][:, 5:6],
                                                           op=ALU.add), r=[("sm2", j)], w=[("sm3", j)])
                P.op('dve', lambda e, j=j: e.reciprocal(out=sm[j][:, 7:8], in_=sm[j][:, 6:7]),
                     r=[("sm3", j)], w=[("sm3", j)])
                P.op('dve', lambda e, j=j, po=po: e.tensor_scalar(out=osb[j], in0=ps[po][0:64, 0:64],
                                                                  scalar1=sm[j][:, 7:8], scalar2=None, op0=ALU.mult),
                     r=[PSK[po], ("sm3", j)], w=[("osb", j)])
                pv2 = ps[po][0:64, 256:288].bitcast(BF16)
                P.op('pe', lambda e, pv2=pv2, j=j: e.transpose(pv2, osb[j], identb[0:64, 0:64]),
                     r=[("osb", j), "identb"], w=[PSK[po]])
                P.op('act', lambda e, pv2=pv2, qs=qs: e.activation(out=naT[:, qs], in_=pv2, func=AF.Copy),
                     r=[PSK[po]], w=["naT"])
            for qt in range(4):
                P.op('pool', lambda e, qt=qt, hh=hh: e.dma_start(
                    out=mixT_d[qt * 768 + 512 + hh * 64:qt * 768 + 512 + (hh + 1) * 64, :],
                    in_=naT[:, qt * 1024:(qt + 1) * 1024]), r=["naT"], w=["mixT_d"], dma=True)
        P.barrier()

    def merge_out():
        A.reset(0)
        MX = A.bf16(24 * NL).rearrange("p (c t) -> p c t", c=24)
        hmo = A.bf16(KC * NL).rearrange("p (k t) -> p k t", k=KC)
        MG = A.bf16(KC * NL).rearrange("p (k t) -> p k t", k=KC)
        mark = A.off
        selS = A.f32(4)
        tmpx = [A.bf16(6 * NL) for _ in range(2)]
        P.op('sp', lambda e: e.dma_start(out=selS, in_=sel_in[:, :]), w=["selS"], dma=True)
        P.op('sp', lambda e: e.dma_start(
            out=hmo, in_=hmT_d.ap().rearrange("(k p) t -> p k t", p=128)[:, :, 0:NL]), w=["hmo"], dma=True)
        n = 0
        for r in range(4):
            dst = MX[:, r * 6:(r + 1) * 6, :].rearrange("p c t -> p (c t)")
            for q in range(4):
                tb_ = tmpx[n % 2]
                tk = ("tmpx", n % 2)
                n += 1
                for hf in range(2):
                    r0 = (2 * q + hf) * 1536 + r * 384
                    P.op('sp', lambda e, tb_=tb_, r0=r0, hf=hf: e.dma_start(
                        out=tb_.rearrange("p (c t) -> p c t", c=6)[:, 3 * hf:3 * hf + 3, :],
                        in_=mixin_d[r0:r0 + 384, :].rearrange("(c p) t -> p c t", p=128)), w=[tk], dma=True)
                if q == 0:
                    P.op('dve', lambda e, dst=dst, tb_=tb_, q=q: e.tensor_scalar(
                        out=dst, in0=tb_, scalar1=selS[:, q:q + 1], scalar2=None, op0=ALU.mult),
                        r=[tk, "selS"], w=["MX"])
                else:
                    P.op('dve', lambda e, dst=dst, tb_=tb_, q=q: e.scalar_tensor_tensor(
                        out=dst, in0=tb_, scalar=selS[:, q:q + 1], in1=dst, op0=ALU.mult, op1=ALU.add),
                        r=[tk, "selS", "MX"], w=["MX"])
        P.barrier()
        A.reset(mark)
        wsf = [A.f32(KC * 128).rearrange("p (k c) -> p k c", k=KC) for _ in range(4)]
        wsb = [A.bf16(KC * 128).rearrange("p (k c) -> p k c", k=KC) for _ in range(4)]
        sgm = [A.f32(512) for _ in range(4)]
        wgo = w_gla_o.ap().rearrange("(i p) d -> p i d", p=128)
        wno = w_na_o.ap().rearrange("(i p) d -> p i d", p=128)
        wmv = w_m.ap().rearrange("(k p) c -> p k c", p=128)
        it = 0
        for dc in range(KC):
            dsl = slice(dc * 128, (dc + 1) * 128)
            srcs = [(wgo[:, :, dsl], 16), (wno[:, :, dsl], 8), (wmv[:, :, dsl], 16),
                    (wmv[:, :, 2048 + dc * 128:2048 + (dc + 1) * 128], 16)]
            for wi, (src, ni) in enumerate(srcs):
                P.op('sp', lambda e, wi=wi, src=src, ni=ni: e.dma_start(out=wsf[wi][:, 0:ni, :], in_=src),
                     w=[("wsf", wi)], dma=True)
                P.op('pool', lambda e, wi=wi, ni=ni: e.tensor_copy(out=wsb[wi][:, 0:ni, :], in_=wsf[wi][:, 0:ni, :]),
                     r=[("wsf", wi)], w=[("wsb", wi)])
            for grp in range(2):
                gs = slice(grp * 512, (grp + 1) * 512)
                base = 4 * (it % 2)
                it += 1
                pa, pbk, p1, p2 = base, base + 1, base + 2, base + 3
                for i in range(16):
                    P.op('pe', lambda e, pa=pa, i=i, gs=gs: e.matmul(
                        ps[pa][:, :], lhsT=wsb[0][:, i, :], rhs=MX[:, (i // 4) * 6 + i % 4, gs],
                        start=(i == 0), stop=(i == 15)), r=[("wsb", 0), "MX"], w=[PSK[pa]])
                for i in range(8):
                    P.op('pe', lambda e, pbk=pbk, i=i, gs=gs: e.matmul(
                        ps[pbk][:, :], lhsT=wsb[1][:, i, :], rhs=MX[:, (i // 2) * 6 + 4 + i % 2, gs],
                        start=(i == 0), stop=(i == 7)), r=[("wsb", 1), "MX"], w=[PSK[pbk]])
                for (pm, wi) in ((p1, 2), (p2, 3)):
                    for k in range(KC):
                        P.op('pe', lambda e, pm=pm, wi=wi, k=k, gs=gs: e.matmul(
                            ps[pm][:, :], lhsT=wsb[wi][:, k, :], rhs=hmo[:, k, gs],
                            start=(k == 0), stop=(k == KC - 1)), r=[("wsb", wi), "hmo"], w=[PSK[pm]])
                P.op('act', lambda e, p1=p1: e.activation(out=sgm[0], in_=ps[p1][:, :], func=AF.Sigmoid),
                     r=[PSK[p1]], w=[("sgm", 0)])
                P.op('act', lambda e, p2=p2: e.activation(out=sgm[1], in_=ps[p2][:, :], func=AF.Sigmoid),
                     r=[PSK[p2]], w=[("sgm", 1)])
                P.op('dve', lambda e, pa=pa: e.tensor_tensor(out=sgm[2], in0=sgm[0], in1=ps[pa][:, :], op=ALU.mult),
                     r=[("sgm", 0), PSK[pa]], w=[("sgm", 2)])
                P.op('dve', lambda e, pbk=pbk: e.tensor_tensor(out=sgm[3], in0=sgm[1], in1=ps[pbk][:, :], op=ALU.mult),
                     r=[("sgm", 1), PSK[pbk]], w=[("sgm", 3)])
                P.op('pool', lambda e, dc=dc, gs=gs: e.tensor_tensor(out=MG[:, dc, gs], in0=sgm[2], in1=sgm[3],
                                                                      op=ALU.add),
                     r=[("sgm", 2), ("sgm", 3)], w=["MG"])
        P.barrier()
        A.reset(mark)
        wsf2 = [A.f32(KC * 128).rearrange("p (k c) -> p k c", k=KC) for _ in range(2)]
        wsb2 = [A.bf16(KC * 128).rearrange("p (k c) -> p k c", k=KC) for _ in range(2)]
        ysb = [A.f32(NT) for _ in range(2)]
        hsb = [A.f32(NT) for _ in range(2)]
        ysq = [A.bf16(NT) for _ in range(2)]
        rstd = A.f32(NT)
        wov = w_out.ap().rearrange("(k p) d -> p k d", p=128)
        it = 0
        for dc in range(KC):
            i = dc % 2
            P.op('sp', lambda e, i=i, dc=dc: e.dma_start(out=wsf2[i], in_=wov[:, :, dc * 128:(dc + 1) * 128]),
                 w=[("wsf", i)], dma=True)
            P.op('pool', lambda e, i=i: e.tensor_copy(out=wsb2[i], in_=wsf2[i]), r=[("wsf", i)], w=[("wsb", i)])
            for grp in range(2):
                gs = slice(grp * 512, (grp + 1) * 512)
                pb = 2 + (it % 6)
                it += 1
                for k in range(KC):
                    P.op('pe', lambda e, pb=pb, k=k, gs=gs, i=i: e.matmul(
                        ps[pb][:, :], lhsT=wsb2[i][:, k, :], rhs=MG[:, k, gs], start=(k == 0), stop=(k == KC - 1)),
                        r=[("wsb", i), "MG"], w=[PSK[pb]])
                P.op('act', lambda e, pb=pb, gs=gs, i=i: e.activation(out=ysb[i][:, gs], in_=ps[pb][:, :], func=AF.Copy),
                     r=[PSK[pb]], w=[("ysb", i)])
                P.op('act', lambda e, pb=pb, gs=gs, i=i: e.activation(out=ysq[i][:, gs], in_=ps[pb][:, :], func=AF.Square),
                     r=[PSK[pb]], w=[("ysq", i)])
            for grp in range(2):
                gs = slice(grp * 512, (grp + 1) * 512)
                P.op('pe', lambda e, grp=grp, gs=gs, i=i, dc=dc: e.matmul(
                    ps[grp][:, :], lhsT=ones_bf, rhs=ysq[i][:, gs], start=(dc == 0), stop=(dc == KC - 1)),
                    r=[("ysq", i), "ones"], w=[PSK[grp]])
            P.op('pool', lambda e, i=i, dc=dc: e.dma_start(out=yT_d[dc, :, 0:NL], in_=ysb[i][:, 0:NL]),
                 r=[("ysb", i)], w=[("yT_d", dc)], dma=True)
        for grp in range(2):
            gs = slice(grp * 512, (grp + 1) * 512)
            P.op('act', lambda e, grp=grp, gs=gs: e.activation(
                out=rstd[:, gs], in_=ps[grp][:, :], func=AF.Sqrt, bias=epsb, scale=1.0 / D),
                r=[PSK[grp], "epsb"], w=["rstd2"])
        P.op('dve', lambda e: e.reciprocal(out=rstd[:, 0:NL], in_=rstd[:, 0:NL]), r=["rstd2"], w=["rstd2"])
        residual_update(1, rstd, ysb, hsb, ncols=NL)

    def final_out():
        A.reset(0)
        HT = A.f32(KC * NL).rearrange("p (k t) -> p k t", k=KC)
        xo = [A.f32(D) for _ in range(2)]
        for dc in range(KC):
            P.op('sp', lambda e, dc=dc: e.dma_start(out=HT[:, dc, :], in_=hT_d[dc, :, 0:NL]),
                 r=[("hT_d", dc)], w=["HT"], dma=True)
        it = 0
        for tt in range(8):
            xb = xo[tt % 2]
            xk = ("xo", tt % 2)
            for kq in range(4):
                pb = it % 8
                it += 1
                for kk in range(4):
                    k = kq * 4 + kk
                    P.op('pe', lambda e, pb=pb, kk=kk, k=k, tt=tt: e.transpose(
                        ps[pb][:, kk * 128:(kk + 1) * 128], HT[:, k, tt * 128:(tt + 1) * 128], ident),
                        r=["HT", "ident"], w=[PSK[pb]])
                if kq % 2 == 0:
                    P.op('act', lambda e, pb=pb, xb=xb, kq=kq: e.activation(
                        out=xb[:, kq * 512:(kq + 1) * 512], in_=ps[pb][:, :], func=AF.Copy), r=[PSK[pb]], w=[xk])
                else:
                    P.op('dve', lambda e, pb=pb, xb=xb, kq=kq: e.tensor_copy(
                        out=xb[:, kq * 512:(kq + 1) * 512], in_=ps[pb][:, :]), r=[PSK[pb]], w=[xk])
            P.op('sp', lambda e, xb=xb, tt=tt: e.dma_start(out=out[tt * 128:(tt + 1) * 128, :], in_=xb),
                 r=[xk], w=["out"], dma=True)

    if mode == 'full':
        prenorm_stage(0, True)
        ffn_core(0, 0)

    if 'h1T' in dbg:
        for dc in range(KC):
            P.op('sp', lambda e, dc=dc: e.dma_start(out=dbg['h1T'][dc, :, :], in_=hT_d[dc, :, :]),
                 r=[("hT_d", dc)], dma=True)

    if mode == 'full':
        hm = prenorm_stage(1, False)
    for c in range(8):
        if mode == 'full':
            P.op('sp', lambda e, c=c: e.dma_start(
                out=hmT_d[c * 256:(c + 1) * 256, :].rearrange("(k p) t -> p k t", p=128), in_=hm[:, 2 * c:2 * c + 2, :]),
                r=["hm"], w=[("hmT_d", c)], dma=True)
        else:
            P.op('sp', lambda e, c=c: e.dma_start(out=hmT_d[c * 256:(c + 1) * 256, :],
                                                   in_=hm_in[c * 256:(c + 1) * 256, :]), w=[("hmT_d", c)], dma=True)
        P.op('pool', lambda e, c=c: e.collective_compute(
            "AllGather", ALU.bypass, replica_groups=groups,
            ins=[hmT_d[c * 256:(c + 1) * 256, :].opt()], outs=[hmT_all[c * 1024:(c + 1) * 1024, :].opt()]),
            r=[("hmT_d", c)], w=["hmT_all"], cc=True)
    P.barrier()
    STG = os.environ.get("K_STAGES", "inproj,gla,na,ag2").split(",")
    if "inproj" in STG:
        mixer_inproj()
    if "gla" in STG:
        gla()
    if "na" in STG:
        na()
    if "ag2" in STG:
        for c in range(8):
            P.op('pool', lambda e, c=c: e.collective_compute(
                "AllGather", ALU.bypass, replica_groups=groups,
                ins=[mixT_d[c * 384:(c + 1) * 384, :].opt()], outs=[mixin_d[c * 1536:(c + 1) * 1536, :].opt()]),
                w=["mixin"], cc=True)
    P.barrier()
    if 'mixT' in dbg:
        P.op('sp', lambda e: e.dma_start(out=dbg['mixT'][:, :], in_=mixT_d[:, :]), dma=True)
        P.barrier()
    if mode == 'full':
        merge_out()
        prenorm_stage(2, False)
        ffn_core(2, 1)
        final_out()
    P.barrier()
    P.emit(nc, st)
    return nc, st


def _consts():
    th = 10000.0
    freqs = th ** (-np.arange(64, dtype=np.float32) / 64.0)
    t = np.arange(SEQ)
    row, col = (t // 64).astype(np.float32), (t % 64).astype(np.float32)
    ropeC = np.zeros((2, 128, SEQ), np.float32)
    ropeS = np.zeros((2, 128, SEQ), np.float32)
    for c, pos in enumerate((row, col)):
        ang = pos[None, :] * freqs[:, None]
        ropeC[c, :64], ropeC[c, 64:] = np.cos(ang), np.cos(ang)
        ropeS[c, :64], ropeS[c, 64:] = np.sin(ang), np.sin(ang)
    rotT = np.zeros((128, 128), np.float32)
    for m in range(64):
        rotT[m + 64, m] = -1.0
        rotT[m, m + 64] = 1.0
    cmask = np.ones((128, 512), np.float32)
    cmask[:, ::64] = 0.0
    ii = np.arange(64)
    trm = np.zeros((64, 2, 64), np.float32)
    trm[:, 0, :] = (ii[None, :] >= ii[:, None])
    trm[:, 1, :] = (ii[None, :] <= ii[:, None])
    return dict(ropeC=ropeC, ropeS=ropeS, rotT=rotT, cmask=cmask, trm=np.ascontiguousarray(trm.reshape(64, 128)),
                identf=np.eye(128, dtype=np.float32), identb=np.eye(128).astype(ml_dtypes.bfloat16))


def _prep_inputs(inputs, ncores=8, mode='full', hm_in=None):
    f = lambda k: np.asarray(inputs[k], np.float32)
    x, c, ctx, c_ctx = f('x'), f('c'), f('ctx'), f('c_ctx')
    w_in = f('w_in')[0]
    gla_wg, gla_bg, gng = f('gla_wg')[0], f('gla_bg')[0], f('gla_norm_g')[0]
    rpb = f('na_rpb')[0]
    shared = _consts()
    shared['w_m'] = np.ascontiguousarray(w_in[:, 9248:13344])
    shared['gng'] = np.ascontiguousarray(gng.reshape(1, 512))
    shared['w_gla_o'] = np.ascontiguousarray(f('w_gla_o')[0])
    shared['w_na_o'] = np.ascontiguousarray(f('w_na_o')[0])
    shared['w_out'] = np.ascontiguousarray(f('w_out')[0])
    shared['w_ada'] = np.ascontiguousarray(f('w_ada')[0])
    shared['bT'] = np.ascontiguousarray(f('b_ada')[0].reshape(144, 128).T)
    shared['gT'] = np.ascontiguousarray(f('norm_g')[0].reshape(6, KC, 128).transpose(2, 0, 1))
    shared['ffn_wg'] = np.ascontiguousarray(f('ffn_wg')[0])
    shared['ffn_wu'] = np.ascontiguousarray(f('ffn_wu')[0])
    shared['ffn_wd'] = np.ascontiguousarray(f('ffn_wd')[0])
    q = np.arange(64)
    k = np.arange(64)
    dj = np.clip(k[None, :] - q[:, None], -15, 15) + 15
    cs = np.clip(q - 8, 0, 48)
    ok = (k[None, :] >= cs[:, None]) & (k[None, :] < cs[:, None] + 16)
    in_maps = []
    for core in range(ncores):
        b, h = core // 4, core % 4
        s = h
        m = dict(shared)
        xin = np.concatenate([x[b, s * 1024:(s + 1) * 1024], ctx[b, s * 64:(s + 1) * 64]], axis=0)
        m['xin'] = np.ascontiguousarray(xin)
        selv = np.zeros((128, 4), np.float32)
        selv[:, s] = 1.0
        m['sel'] = selv
        cin = np.stack([c[b], c_ctx], axis=0)
        m['cT'] = np.ascontiguousarray(cin.reshape(2, KC, 128).transpose(2, 1, 0))
        cols = np.concatenate([
            np.arange(h * 256, (h + 1) * 256), 1024 + np.arange(h * 256, (h + 1) * 256),
            np.arange(6144, 6176),
            6176 + np.arange(h * 256, (h + 1) * 256), 7200 + np.arange(h * 256, (h + 1) * 256),
            2048 + np.arange(h * 512, (h + 1) * 512), 4096 + np.arange(h * 512, (h + 1) * 512),
            8224 + np.arange(h * 256, (h + 1) * 256)])
        m['w_inh'] = np.ascontiguousarray(w_in[:, cols])
        m['gwg'] = np.ascontiguousarray(gla_wg[:, :, h * 256:(h + 1) * 256].transpose(1, 0, 2))
        m['gbgT'] = np.ascontiguousarray(gla_bg[:, h * 256:(h + 1) * 256].reshape(2, 2, 128).transpose(2, 0, 1).reshape(128, 4))
        tbl = np.empty((4, 64, 15, 64), np.float32)
        for hh in range(4):
            g = rpb[4 * h + hh][:, dj]
            tbl[hh] = np.where(ok[None], g, np.float32(-1e30)).transpose(1, 0, 2)
        m['tb'] = np.ascontiguousarray(tbl.reshape(4, 64, 15 * 64))
        if mode != 'full':
            m['hm_in'] = hm_in[core]
            for kk in ('xin', 'cT', 'w_ada', 'bT', 'ffn_wg', 'ffn_wu', 'ffn_wd'):
                m[kk] = np.zeros((1,), np.float32)
        in_maps.append(m)
    return in_maps


def kernel(**inputs):
    nc, st = build()
    in_maps = _prep_inputs(inputs)
    res = run_bass_kernel_spmd(nc, in_maps, core_ids=list(range(8)))
    outp = np.zeros((2, SEQ, D), np.float32)
    for core in range(8):
        b, s = core // 4, core % 4
        outp[b, s * 1024:(s + 1) * 1024] = res.results[core]["out"]
    return outp
```

```python
import os
import contextlib
import numpy as np
import ml_dtypes
import concourse.bass as bass
import concourse.mybir as mybir
from concourse.bass_utils import run_bass_kernel_spmd

F32 = mybir.dt.float32
BF16 = mybir.dt.bfloat16
AF = mybir.ActivationFunctionType
ALU = mybir.AluOpType

D = 2048
KC = 16
DFF = 5632
FC = 44
NT = 1088
NL = 1024
GROUPS = [(0, 512), (512, 512), (1024, 64)]
EPS = 1e-6
SEQ = 4096
CTX = 256
NS = SEQ + CTX


class Prog:
    def __init__(self):
        self.ops = []
        self.lastw = {}
        self.readers = {}

    def op(self, eng, fn, r=(), w=(), dma=False, cc=False):
        i = len(self.ops)
        deps = set()
        for k in r:
            if k in self.lastw:
                deps.add(self.lastw[k])
        for k in w:
            if k in self.lastw:
                deps.add(self.lastw[k])
            deps.update(self.readers.get(k, ()))
        for k in r:
            self.readers.setdefault(k, []).append(i)
        for k in w:
            self.lastw[k] = i
            self.readers[k] = []
        self.ops.append(dict(eng=eng, fn=fn, deps=deps, dma=dma, cc=cc, barrier=False))
        return i

    def barrier(self):
        self.ops.append(dict(barrier=True))
        self.lastw = {}
        self.readers = {}

    def emit(self, nc, st):
        ops = self.ops
        engs = ('pe', 'act', 'dve', 'pool', 'sp')
        needed = set()
        for o in ops:
            if not o['barrier']:
                needed |= o['deps']
        last = {}
        for i, o in enumerate(ops):
            if o['barrier']:
                for e in engs:
                    if e in last:
                        needed.add(last[e])
                last = {}
            elif not o['dma'] and not o['cc']:
                last[o['eng']] = i
        esem = {e: st.enter_context(nc.semaphore("es_" + e)) for e in engs}
        NQ = 20
        dpool = {q: [st.enter_context(nc.semaphore("dq_%s_%d" % (q, j))) for j in range(NQ)]
                 for q in ('sp', 'pool', 'act')}
        dcnt = {}
        rr = {q: 0 for q in dpool}
        tick = {e: 0 for e in engs}
        ccs = []
        for i, o in enumerate(ops):
            if o['barrier']:
                o['ticks'] = dict(tick)
                o['dmas'] = dict(dcnt)
                o['ccs'] = list(ccs)
            elif o['cc']:
                s = st.enter_context(nc.semaphore("cc_%d" % i))
                o['sem'] = s
                o['val'] = 1
                o['prev'] = 0
                ccs.append(s)
            elif o['dma']:
                q = o['eng']
                s = dpool[q][rr[q] % NQ]
                rr[q] += 1
                o['prev'] = dcnt.get(s, 0)
                dcnt[s] = o['prev'] + 16
                o['sem'] = s
                o['val'] = dcnt[s]
            elif i in needed:
                tick[o['eng']] += 1
                o['tick'] = tick[o['eng']]

        def run(E, e):
            waited = {}

            def w(sem, val):
                key = id(sem)
                if waited.get(key, 0) < val:
                    e.wait_ge(sem, val)
                    waited[key] = val

            for i, o in enumerate(ops):
                if o['barrier']:
                    for F in engs:
                        if o['ticks'][F] > 0:
                            w(esem[F], o['ticks'][F])
                    for s, c in o['dmas'].items():
                        w(s, c)
                    for s in o['ccs']:
                        w(s, 1)
                    continue
                if o['eng'] != E:
                    continue
                for d in sorted(o['deps']):
                    od = ops[d]
                    if od['dma'] or od['cc']:
                        w(od['sem'], od['val'])
                    else:
                        if od['eng'] == E and E == 'pe':
                            continue
                        w(esem[od['eng']], od['tick'])
                if (o['dma']) and o['prev'] > 0:
                    w(o['sem'], o['prev'])
                ins = o['fn'](e)
                if o['cc']:
                    ins.then_inc(o['sem'])
                elif o['dma']:
                    ins.then_inc(o['sem'], 16)
                elif 'tick' in o:
                    ins.then_inc(esem[E], 1)

        with nc.Block() as block:
            @block.tensor
            def _(e):
                run('pe', e)

            @block.scalar
            def _(e):
                run('act', e)

            @block.vector
            def _(e):
                run('dve', e)

            @block.gpsimd
            def _(e):
                run('pool', e)

            @block.sync
            def _(e):
                run('sp', e)


class Arena:
    def __init__(self, t, nwords):
        self.t = t
        self.n = nwords
        self.off = 0

    def reset(self, off=0):
        self.off = off

    def f32(self, n, parts=128):
        a = self.t[0:parts, self.off:self.off + n]
        self.off += n
        assert self.off <= self.n, ("arena overflow", self.off)
        return a

    def bf16(self, n, parts=128):
        words = (n + 1) // 2
        a = self.t[0:parts, self.off:self.off + words].bitcast(BF16)
        self.off += words
        assert self.off <= self.n, ("arena overflow", self.off)
        return a


def build(mode='full', debug=(), ncores=8):
    nc = bass.Bass("TRN2", target_bir_lowering=False)
    P = Prog()
    st = contextlib.ExitStack()

    def din(name, shape, dt=F32):
        return nc.dram_tensor(name, list(shape), dt, kind="ExternalInput")

    def dint(name, shape, dt=F32):
        return nc.dram_tensor(name, list(shape), dt)

    full = (mode == 'full')
    xin = din("xin", [NT, D] if full else [1])
    cT = din("cT", [128, KC, 2] if full else [1])
    w_ada = din("w_ada", [D, 4608] if full else [1])
    bT = din("bT", [128, 36] if full else [1])
    gT = din("gT", [128, 6, KC])
    ffn_wg = din("ffn_wg", [2, D, DFF] if full else [1])
    ffn_wu = din("ffn_wu", [2, D, DFF] if full else [1])
    ffn_wd = din("ffn_wd", [2, DFF, D] if full else [1])
    identf = din("identf", [128, 128])
    groups = [[0, 1, 2, 3], [4, 5, 6, 7]] if ncores == 8 else [[0, 1, 2, 3]]
    w_inh = din("w_inh", [D, 2336])
    w_m = din("w_m", [D, 4096])
    gwg = din("gwg", [16, 2, 256])
    gbgT = din("gbgT", [128, 4])
    gng = din("gng", [1, 512])
    ropeC = din("ropeC", [2, 128, SEQ])
    ropeS = din("ropeS", [2, 128, SEQ])
    rotT_in = din("rotT", [128, 128])
    cmask_in = din("cmask", [128, 512])
    trm_in = din("trm", [64, 128])
    tb_in = din("tb", [4, 64, 15 * 64])
    w_gla_o = din("w_gla_o", [D, D])
    w_na_o = din("w_na_o", [1024, D])
    w_out = din("w_out", [D, D])
    identb_in = din("identb", [128, 128], BF16)
    hm_in = din("hm_in", [D, NT], BF16) if mode != 'full' else None
    hmT_d = dint("hmT_d", [D, NT], BF16)
    modp_d = dint("modp_d", [128, 72])
    modall_d = dint("modall_d", [512, 72])
    hmT_all = dint("hmT_all", [8 * 1024, NT], BF16)
    qT_d = dint("qT_d", [256, NS])
    kT_d = dint("kT_d", [256, NS])
    gT_d = dint("gT_d", [32, NS])
    v_d = dint("v_d", [NS, 512], BF16)
    sr_d = dint("sr_d", [NS, 512])
    nv_d = dint("nv_d", [NS, 256], BF16)
    nqT_d = dint("nqT_d", [256, NS], BF16)
    nkT_d = dint("nkT_d", [256, NS], BF16)
    of_d = dint("of_d", [SEQ, 512])
    mixT_d = dint("mixT_d", [4 * 768, 1024], BF16)
    mixin_d = dint("mixin_d", [8 * 1536, 1024], BF16)
    sel_in = din("sel", [128, 4])
    out = nc.dram_tensor("out", [NL, D], F32, kind="ExternalOutput")

    hT_d = dint("hT_d", [KC, 128, NT])
    yT_d = dint("yT_d", [KC, 128, NT])
    dbg = {}
    for name, shape, dt in debug:
        dbg[name] = nc.dram_tensor("dbg_" + name, list(shape), dt, kind="ExternalOutput")

    AW = 50400
    arena_t = st.enter_context(nc.sbuf_tensor("arena", [128, AW], F32))
    A = Arena(arena_t, AW)
    cst = st.enter_context(nc.sbuf_tensor("cst", [128, 768], F32))
    ident = cst[:, 0:128]
    modT = cst[:, 128:128 + 288].rearrange("p (r j) -> p r j", r=2)
    Amod = cst[:, 416:416 + 96].rearrange("p (r s k) -> p r s k", r=2, s=3)
    Gmod = cst[:, 512:512 + 96].rearrange("p (r s k) -> p r s k", r=2, s=3)
    gTs = cst[:, 608:608 + 96].rearrange("p (s k) -> p s k", s=6)
    cst_b = st.enter_context(nc.sbuf_tensor("cst_b", [128, 256], BF16))
    ones_bf = cst_b[:, 0:128]
    identb = cst_b[:, 128:256]
    cst2 = st.enter_context(nc.sbuf_tensor("cst2", [128, 1800], F32))
    rotT = cst2[:, 0:128]
    cmask = cst2[:, 128:640]
    trm = cst2[0:64, 640:768].rearrange("p (a b) -> p a b", a=2)
    nbg = cst2[:, 768:772]
    one1 = cst2[:, 772:773]
    wgT = cst2[0:16, 776:776 + 512].rearrange("p (a b) -> p a b", a=2)
    gnb = cst2[0:64, 1288:1288 + 512]
    ps = [st.enter_context(nc.psum_tensor("ps%d" % i, [128, 512], F32)) for i in range(8)]
    PSK = [("ps", i) for i in range(8)]

    P.op('sp', lambda e: e.dma_start(out=ident, in_=identf[:, :]), w=["ident"], dma=True)
    P.op('sp', lambda e: e.dma_start(out=gTs, in_=gT[:, :, :]), w=["gTs"], dma=True)
    P.op('pool', lambda e: e.memset(ones_bf, 1.0), w=["ones"])
    epsb = cst[:, 704:705]
    P.op('pool', lambda e: e.memset(epsb, EPS), w=["epsb"])
    P.op('sp', lambda e: e.dma_start(out=identb, in_=identb_in[:, :]), w=["identb"], dma=True)
    P.op('sp', lambda e: e.dma_start(out=rotT, in_=rotT_in[:, :]), w=["rotT"], dma=True)
    P.op('sp', lambda e: e.dma_start(out=cmask, in_=cmask_in[:, :]), w=["cmask"], dma=True)
    P.op('sp', lambda e: e.dma_start(out=cst2[0:64, 640:768], in_=trm_in[:, :]), w=["trm"], dma=True)
    P.op('sp', lambda e: e.dma_start(out=nbg, in_=gbgT[:, :]), w=["nbg"], dma=True)
    P.op('dve', lambda e: e.tensor_scalar(out=nbg, in0=nbg, scalar1=-1.0, scalar2=None, op0=ALU.mult),
         r=["nbg"], w=["nbg"])
    P.op('pool', lambda e: e.memset(one1, 1.0), w=["one1"])
    P.op('sp', lambda e: e.dma_start(out=cst2[0:16, 776:776 + 512], in_=gwg.ap().rearrange("k a b -> k (a b)")),
         w=["wgT"], dma=True)
    P.op('sp', lambda e: e.dma_start(out=gnb, in_=gng.ap().broadcast_to([64, 512])), w=["gnb"], dma=True)
    if mode != 'full':
        P.barrier()
    if mode == 'full':
        A.reset()
        cTs = A.f32(32).rearrange("p (k r) -> p k r", r=2)
        sTs = A.f32(32).rearrange("p (k r) -> p k r", r=2)
        bTs = A.f32(36)
        modp = A.f32(72).rearrange("p (r j) -> p r j", r=2)
        P.op('sp', lambda e: e.dma_start(out=cTs, in_=cT[:, :, :]), w=["cTs"], dma=True)
        P.op('sp', lambda e: e.dma_start(out=bTs, in_=bT[:, :]), w=["bTs"], dma=True)
        P.op('act', lambda e: e.activation(out=sTs, in_=cTs, func=AF.Silu), r=["cTs"], w=["sTs"])
        wv = w_ada.ap().rearrange("(k p) c -> p k c", p=128)
        wst = [A.f32(KC * 512).rearrange("p (k c) -> p k c", k=KC) for _ in range(3)]
        modps = ps[0][:, 0:72].rearrange("p (j r) -> p j r", r=2)
        for cg in range(9):
            wb = wst[cg % 3]
            key = ("wst", cg % 3)
            P.op('sp', lambda e, wb=wb, cg=cg: e.dma_start(out=wb, in_=wv[:, :, cg * 512:(cg + 1) * 512]),
                 w=[key], dma=True)
            for jj in range(4):
                j = cg * 4 + jj
                for k in range(KC):
                    P.op('pe', lambda e, wb=wb, jj=jj, j=j, k=k: e.matmul(
                        modps[:, j, :], lhsT=wb[:, k, jj * 128:(jj + 1) * 128], rhs=sTs[:, k, :],
                        start=(k == 0), stop=(k == KC - 1)), r=[key, "sTs"], w=[PSK[0]])
        for r in range(2):
            P.op('dve', lambda e, r=r: e.tensor_tensor(out=modp[:, r, :], in0=modps[:, :, r], in1=bTs,
                                                        op=ALU.add), r=[PSK[0], "bTs"], w=["modp"])
        P.op('sp', lambda e: e.dma_start(out=modp_d[:, :], in_=modp.rearrange("p r j -> p (r j)")),
             r=["modp"], w=["modp_d"], dma=True)
        P.op('pool', lambda e: e.collective_compute("AllGather", ALU.bypass, replica_groups=groups,
                                                    ins=[modp_d.ap().opt()], outs=[modall_d.ap().opt()]),
             r=["modp_d"], w=["modall_d"], cc=True)
        for q in range(4):
            P.op('sp', lambda e, q=q: e.dma_start(
                out=modT[:, :, q * 36:(q + 1) * 36],
                in_=modall_d[q * 128:(q + 1) * 128, :].rearrange("p (r j) -> p r j", r=2)),
                r=["modall_d"], w=["modT"], dma=True)
        WRES = (0.5, 1.0, 0.5)
        for r in range(2):
            for s in range(3):
                P.op('dve', lambda e, r=r, s=s: e.scalar_tensor_tensor(
                    out=Amod[:, r, s, :], in0=modT[:, r, (3 * s + 1) * 16:(3 * s + 2) * 16], scalar=1.0,
                    in1=gTs[:, 2 * s, :], op0=ALU.add, op1=ALU.mult), r=["modT", "gTs"], w=["Amod"])
                P.op('dve', lambda e, r=r, s=s: e.scalar_tensor_tensor(
                    out=Gmod[:, r, s, :], in0=modT[:, r, (3 * s + 2) * 16:(3 * s + 3) * 16], scalar=WRES[s],
                    in1=gTs[:, 2 * s + 1, :], op0=ALU.mult, op1=ALU.mult), r=["modT", "gTs"], w=["Gmod"])
        P.barrier()

    def Bmod(r, s, k):
        return modT[:, r, (3 * s) * 16 + k:(3 * s) * 16 + k + 1]

    def prenorm_modulate(HT, s, hm, sqkey="hm"):
        rstd = A.f32(NT)
        tmp = [A.f32(NT) for _ in range(2)]
        for k in range(KC):
            P.op('act', lambda e, k=k: e.activation(out=hm[:, k, :], in_=HT[:, k, :], func=AF.Square),
                 r=["HT"], w=["hm"])
        for gi, (c0, n) in enumerate(GROUPS):
            for k in range(KC):
                P.op('pe', lambda e, k=k, c0=c0, n=n, gi=gi: e.matmul(
                    ps[gi][:, 0:n], lhsT=ones_bf, rhs=hm[:, k, c0:c0 + n], start=(k == 0), stop=(k == KC - 1)),
                    r=["hm", "ones"], w=[PSK[gi]])
            P.op('act', lambda e, c0=c0, n=n, gi=gi: e.activation(
                out=rstd[:, c0:c0 + n], in_=ps[gi][:, 0:n], func=AF.Sqrt, bias=epsb, scale=1.0 / D),
                r=[PSK[gi], "epsb"], w=["rstd"])
        P.op('dve', lambda e: e.reciprocal(out=rstd, in_=rstd), r=["rstd"], w=["rstd"])
        for k in range(KC):
            t = tmp[k % 2]
            tk = ("tmpn", k % 2)
            P.op('dve', lambda e, k=k, t=t: e.tensor_tensor(out=t, in0=HT[:, k, :], in1=rstd, op=ALU.mult),
                 r=["HT", "rstd"], w=[tk])
            for r, (c0, n) in enumerate([(0, NL), (NL, NT - NL)]):
                P.op('act', lambda e, k=k, t=t, r=r, c0=c0, n=n: e.activation(
                    out=hm[:, k, c0:c0 + n], in_=t[:, c0:c0 + n], func=AF.Identity,
                    bias=Bmod(r, s, k), scale=Amod[:, r, s, k:k + 1]),
                    r=[tk, "Amod", "modT"], w=["hm"])

    HM_W = KC * NT // 2
    ACT_W = FC * NT // 2

    def ffn_core(s, li):
        wg = ffn_wg.ap()[li].rearrange("(k p) f -> p k f", p=128)
        wu = ffn_wu.ap()[li].rearrange("(k p) f -> p k f", p=128)
        wd = ffn_wd.ap()[li].rearrange("(f p) d -> p f d", p=128)
        A.reset(0)
        hm = A.bf16(KC * NT).rearrange("p (k t) -> p k t", k=KC)
        actT = A.bf16(FC * NT).rearrange("p (f t) -> p f t", f=FC)
        mark = A.off
        wsf = [A.f32(KC * 256).rearrange("p (k c) -> p k c", k=KC) for _ in range(2)]
        wsb = [A.bf16(KC * 256).rearrange("p (k c) -> p k c", k=KC) for _ in range(4)]
        sg = [A.f32(512) for _ in range(2)]
        slot = 0
        nsg = 0
        nld = 0
        for fg in range(FC // 2):
            wb = []
            for gu, wsrc in enumerate((wg, wu)):
                i2 = nld % 2
                i = nld % 4
                nld += 1
                P.op('sp', lambda e, i2=i2, wsrc=wsrc, fg=fg: e.dma_start(
                    out=wsf[i2], in_=wsrc[:, :, fg * 256:(fg + 1) * 256]), w=[("wsf", i2)], dma=True)
                for hk in range(2):
                    P.op('pool', lambda e, i=i, i2=i2, hk=hk: e.tensor_copy(
                        out=wsb[i][:, hk * 8:(hk + 1) * 8, :], in_=wsf[i2][:, hk * 8:(hk + 1) * 8, :]),
                        r=[("wsf", i2)], w=[("wsb", i)])
                wb.append(i)
            for f2 in range(2):
                f = fg * 2 + f2
                for (c0, n) in GROUPS:
                    pg, pu = 2 + 2 * (slot % 3), 3 + 2 * (slot % 3)
                    slot += 1
                    for gu, pb in enumerate((pg, pu)):
                        for k in range(KC):
                            P.op('pe', lambda e, pb=pb, k=k, c0=c0, n=n, i=wb[gu], f2=f2: e.matmul(
                                ps[pb][:, 0:n], lhsT=wsb[i][:, k, f2 * 128:(f2 + 1) * 128], rhs=hm[:, k, c0:c0 + n],
                                start=(k == 0), stop=(k == KC - 1)),
                                r=[("wsb", wb[gu]), "hm"], w=[PSK[pb]])
                    sgi = nsg % 2
                    nsg += 1
                    P.op('act', lambda e, pg=pg, n=n, sgi=sgi: e.activation(out=sg[sgi][:, 0:n], in_=ps[pg][:, 0:n],
                                                                            func=AF.Silu),
                         r=[PSK[pg]], w=[("sg", sgi)])
                    P.op('dve', lambda e, pu=pu, n=n, sgi=sgi, f=f, c0=c0: e.tensor_tensor(
                        out=actT[:, f, c0:c0 + n], in0=sg[sgi][:, 0:n], in1=ps[pu][:, 0:n], op=ALU.mult),
                        r=[("sg", sgi), PSK[pu]], w=["act"])
        P.barrier()
        A.reset(0)
        ysb = [A.f32(NT) for _ in range(2)]
        hsb = [A.f32(NT) for _ in range(2)]
        ysq = [A.bf16(NT) for _ in range(2)]
        rstd = A.f32(NT)
        assert A.off <= HM_W
        A.reset(mark)
        wdf = [A.f32(11 * 256).rearrange("p (f c) -> p f c", f=11) for _ in range(2)]
        wdb = [A.bf16(FC * 256).rearrange("p (f c) -> p f c", f=FC) for _ in range(2)]
        slot = 0
        nst = 0
        for dcg in range(KC // 2):
            t = dcg % 2
            for qtr in range(4):
                qi = nst % 2
                nst += 1
                P.op('sp', lambda e, qi=qi, dcg=dcg, qtr=qtr: e.dma_start(
                    out=wdf[qi], in_=wd[:, qtr * 11:(qtr + 1) * 11, dcg * 256:(dcg + 1) * 256]),
                    w=[("wdf", qi)], dma=True)
                P.op('pool', lambda e, qi=qi, t=t, qtr=qtr: e.tensor_copy(
                    out=wdb[t][:, qtr * 11:(qtr + 1) * 11, :], in_=wdf[qi]), r=[("wdf", qi)], w=[("wdb", t)])
            for dc2 in range(2):
                dc = dcg * 2 + dc2
                i = dc % 2
                for gi, (c0, n) in enumerate(GROUPS):
                    pb = 3 + (slot % 5)
                    slot += 1
                    for f in range(FC):
                        P.op('pe', lambda e, pb=pb, f=f, c0=c0, n=n, t=t, dc2=dc2: e.matmul(
                            ps[pb][:, 0:n], lhsT=wdb[t][:, f, dc2 * 128:(dc2 + 1) * 128], rhs=actT[:, f, c0:c0 + n],
                            start=(f == 0), stop=(f == FC - 1)),
                            r=[("wdb", t), "act"], w=[PSK[pb]])
                    P.op('act', lambda e, pb=pb, c0=c0, n=n, i=i: e.activation(
                        out=ysb[i][:, c0:c0 + n], in_=ps[pb][:, 0:n], func=AF.Copy), r=[PSK[pb]], w=[("ysb", i)])
                    P.op('act', lambda e, pb=pb, c0=c0, n=n, i=i: e.activation(
                        out=ysq[i][:, c0:c0 + n], in_=ps[pb][:, 0:n], func=AF.Square), r=[PSK[pb]], w=[("ysq", i)])
                for gi, (c0, n) in enumerate(GROUPS):
                    P.op('pe', lambda e, gi=gi, c0=c0, n=n, i=i, dc=dc: e.matmul(
                        ps[gi][:, 0:n], lhsT=ones_bf, rhs=ysq[i][:, c0:c0 + n], start=(dc == 0), stop=(dc == KC - 1)),
                        r=[("ysq", i), "ones"], w=[PSK[gi]])
                P.op('pool', lambda e, i=i, dc=dc: e.dma_start(out=yT_d[dc, :, :], in_=ysb[i]),
                     r=[("ysb", i)], w=[("yT_d", dc)], dma=True)
        for gi, (c0, n) in enumerate(GROUPS):
            P.op('act', lambda e, c0=c0, n=n, gi=gi: e.activation(
                out=rstd[:, c0:c0 + n], in_=ps[gi][:, 0:n], func=AF.Sqrt, bias=epsb, scale=1.0 / D),
                r=[PSK[gi], "epsb"], w=["rstd2"])
        P.op('dve', lambda e: e.reciprocal(out=rstd, in_=rstd), r=["rstd2"], w=["rstd2"])
        residual_update(s, rstd, ysb, hsb)

    def residual_update(s, rstd, ysb, hsb, ncols=NT):
        ranges = [(0, 0, NL)] + ([(1, NL, NT - NL)] if ncols == NT else [])
        for dc in range(KC):
            i = dc % 2
            yb, hb = ysb[i], hsb[i]
            P.op('sp', lambda e, yb=yb, dc=dc: e.dma_start(out=yb[:, 0:ncols], in_=yT_d[dc, :, 0:ncols]),
                 r=[("yT_d", dc)], w=[("ysb", i)], dma=True)
            P.op('sp', lambda e, hb=hb, dc=dc: e.dma_start(out=hb[:, 0:ncols], in_=hT_d[dc, :, 0:ncols]),
                 r=[("hT_d", dc)], w=[("hsb", i)], dma=True)
            P.op('dve', lambda e, yb=yb: e.tensor_tensor(out=yb[:, 0:ncols], in0=yb[:, 0:ncols], in1=rstd[:, 0:ncols],
                                                         op=ALU.mult),
                 r=[("ysb", i), "rstd2"], w=[("ysb", i)])
            for (r, c0, n) in ranges:
                P.op('dve', lambda e, yb=yb, hb=hb, dc=dc, r=r, c0=c0, n=n: e.scalar_tensor_tensor(
                    out=hb[:, c0:c0 + n], in0=yb[:, c0:c0 + n], scalar=Gmod[:, r, s, dc:dc + 1],
                    in1=hb[:, c0:c0 + n], op0=ALU.mult, op1=ALU.add),
                    r=[("ysb", i), "Gmod", ("hsb", i)], w=[("hsb", i)])
            P.op('pool', lambda e, hb=hb, dc=dc: e.dma_start(out=hT_d[dc, :, 0:ncols], in_=hb[:, 0:ncols]),
                 r=[("hsb", i)], w=[("hT_d", dc)], dma=True)
        P.barrier()

    def prenorm_stage(s, from_x):
        A.reset(0)
        hm = A.bf16(KC * NT).rearrange("p (k t) -> p k t", k=KC)
        HT = A.f32(KC * NT).rearrange("p (k t) -> p k t", k=KC)
        if from_x:
            xt = [A.f32(D) for _ in range(2)]
            tiles = [(i * 128, 128) for i in range(8)] + [(NL, 64)]
            slot = 0
            for ti, (t0, pn) in enumerate(tiles):
                xb = xt[ti % 2]
                P.op('sp', lambda e, xb=xb, t0=t0, pn=pn: e.dma_start(out=xb[0:pn, :], in_=xin[t0:t0 + pn, :]),
                     w=[("xt", ti % 2)], dma=True)
                for kq in range(4):
                    pb = 4 + (slot % 4)
                    slot += 1
                    for kk in range(4):
                        k = kq * 4 + kk
                        P.op('pe', lambda e, xb=xb, pn=pn, pb=pb, kk=kk, k=k: e.transpose(
                            ps[pb][:, kk * 128:kk * 128 + pn], xb[0:pn, k * 128:(k + 1) * 128], ident[0:pn, 0:pn]),
                            r=[("xt", ti % 2), "ident"], w=[PSK[pb]])
                    src = ps[pb][:, :].rearrange("p (a b) -> p a b", a=4)[:, :, 0:pn]
                    dst = HT[:, kq * 4:(kq + 1) * 4, t0:t0 + pn]
                    if kq % 2 == 0:
                        P.op('act', lambda e, src=src, dst=dst: e.activation(out=dst, in_=src, func=AF.Copy),
                             r=[PSK[pb]], w=["HT"])
                    else:
                        P.op('dve', lambda e, src=src, dst=dst: e.tensor_copy(out=dst, in_=src),
                             r=[PSK[pb]], w=["HT"])
            for dc in range(KC):
                P.op('pool', lambda e, dc=dc: e.dma_start(out=hT_d[dc, :, :], in_=HT[:, dc, :]),
                     r=["HT"], w=[("hT_d", dc)], dma=True)
        else:
            for dc in range(KC):
                P.op('sp', lambda e, dc=dc: e.dma_start(out=HT[:, dc, :], in_=hT_d[dc, :, :]),
                     r=[("hT_d", dc)], w=["HT"], dma=True)
        prenorm_modulate(HT, s, hm)
        P.barrier()
        return hm

    def tok_groups():
        gl = [(0, 256, [(r, 1024, 64, r * 64) for r in range(4)])]
        for g in range(1, 9):
            gl.append((256 + 512 * (g - 1), 512, [((g - 1) // 2, ((g - 1) % 2) * 512, 512, 0)]))
        return gl

    def mixer_inproj():
        A.reset(0)
        NW = 2336
        WB = A.bf16(KC * NW).rearrange("p (k c) -> p k c", k=KC)
        wstg = [A.f32(NW) for _ in range(2)]
        wv_ = w_inh.ap().rearrange("(k p) c -> p k c", p=128)
        for k in range(KC):
            P.op('sp', lambda e, k=k: e.dma_start(out=wstg[k % 2], in_=wv_[:, k, :]), w=[("wstg", k % 2)], dma=True)
            P.op('pool', lambda e, k=k: e.tensor_copy(out=WB[:, k, :], in_=wstg[k % 2]),
                 r=[("wstg", k % 2)], w=["WB"])
        X = [A.bf16(KC * 512).rearrange("p (k t) -> p k t", k=KC) for _ in range(2)]
        ev = [A.f32(512) for _ in range(4)]
        evb = [A.bf16(512) for _ in range(4)]
        cnt = dict(pb=0, ev=0, evb=0)

        def nxt(kind, mod):
            v = cnt[kind] % mod
            cnt[kind] += 1
            return v

        FM = [(0, 128, 'q', 0), (128, 128, 'q', 1), (256, 128, 'k', 0), (384, 128, 'k', 1), (512, 32, 'g', 0),
              (544, 128, 'nq', 0), (672, 128, 'nq', 1), (800, 128, 'nk', 0), (928, 128, 'nk', 1)]
        TM = [(1056, 512, 'v'), (1568, 512, 'r'), (2080, 256, 'nv')]
        for gi, (n0, n, srcs) in enumerate(tok_groups()):
            Xb = X[gi % 2]
            xk = ("X", gi % 2)
            for (r, c0, nn, x0) in srcs:
                for c in range(8):
                    P.op('sp', lambda e, Xb=Xb, r=r, c0=c0, nn=nn, x0=x0, c=c: e.dma_start(
                        out=Xb[:, 2 * c:2 * c + 2, x0:x0 + nn],
                        in_=hmT_all[c * 1024 + r * 256:c * 1024 + (r + 1) * 256, c0:c0 + nn].rearrange(
                            "(k p) t -> p k t", p=128)), w=[xk], dma=True)
            for (c0, M, kind, idx) in FM:
                if gi == 0 and kind in ('q', 'nq'):
                    continue
                pb = nxt('pb', 8)
                for k in range(KC):
                    P.op('pe', lambda e, pb=pb, M=M, n=n, k=k, c0=c0, Xb=Xb: e.matmul(
                        ps[pb][0:M, 0:n], lhsT=WB[:, k, c0:c0 + M], rhs=Xb[:, k, 0:n],
                        start=(k == 0), stop=(k == KC - 1)), r=["WB", xk], w=[PSK[pb]])
                if kind in ('q', 'k', 'g'):
                    j = nxt('ev', 4)
                    dst = {'q': qT_d, 'k': kT_d, 'g': gT_d}[kind]
                    P.op('act', lambda e, pb=pb, M=M, n=n, j=j: e.activation(out=ev[j][0:M, 0:n], in_=ps[pb][0:M, 0:n],
                                                                           func=AF.Copy), r=[PSK[pb]], w=[("ev", j)])
                    P.op('pool', lambda e, dst=dst, idx=idx, M=M, n=n, n0=n0, j=j: e.dma_start(
                        out=dst[idx * 128:idx * 128 + M, n0:n0 + n], in_=ev[j][0:M, 0:n]),
                        r=[("ev", j)], w=["qkg_d"], dma=True)
                else:
                    j = nxt('evb', 4)
                    dst = {'nq': nqT_d, 'nk': nkT_d}[kind]
                    sc = 0.125 if kind == 'nq' else 1.0
                    P.op('act', lambda e, pb=pb, n=n, j=j, sc=sc: e.activation(
                        out=evb[j][:, 0:n], in_=ps[pb][:, 0:n], func=AF.Copy, scale=sc), r=[PSK[pb]], w=[("evb", j)])
                    P.op('pool', lambda e, dst=dst, idx=idx, n=n, n0=n0, j=j: e.dma_start(
                        out=dst[idx * 128:(idx + 1) * 128, n0:n0 + n], in_=evb[j][:, 0:n]),
                        r=[("evb", j)], w=["qkg_d"], dma=True)
            for tt in range(n // 128):
                for (c0, ncol, kind) in TM:
                    pb = nxt('pb', 8)
                    for k in range(KC):
                        P.op('pe', lambda e, pb=pb, ncol=ncol, k=k, c0=c0, Xb=Xb, tt=tt: e.matmul(
                            ps[pb][:, 0:ncol], lhsT=Xb[:, k, tt * 128:(tt + 1) * 128], rhs=WB[:, k, c0:c0 + ncol],
                            start=(k == 0), stop=(k == KC - 1)), r=["WB", xk], w=[PSK[pb]])
                    r0 = n0 + tt * 128
                    if kind == 'r':
                        j = nxt('ev', 4)
                        P.op('act', lambda e, pb=pb, j=j: e.activation(out=ev[j], in_=ps[pb][:, 0:512], func=AF.Silu),
                             r=[PSK[pb]], w=[("ev", j)])
                        P.op('pool', lambda e, r0=r0, j=j: e.dma_start(out=sr_d[r0:r0 + 128, :], in_=ev[j]),
                             r=[("ev", j)], w=["qkg_d"], dma=True)
                    else:
                        j = nxt('evb', 4)
                        dst = v_d if kind == 'v' else nv_d
                        P.op('dve', lambda e, pb=pb, j=j, ncol=ncol: e.tensor_copy(out=evb[j][:, 0:ncol],
                                                                                  in_=ps[pb][:, 0:ncol]),
                             r=[PSK[pb]], w=[("evb", j)])
                        P.op('pool', lambda e, dst=dst, r0=r0, j=j, ncol=ncol: e.dma_start(
                            out=dst[r0:r0 + 128, :], in_=evb[j][:, 0:ncol]), r=[("evb", j)], w=["qkg_d"], dma=True)
        P.barrier()

    def gla():
        A.reset(0)
        S = A.f32(1024).rearrange("p (c v) -> p c v", c=2)
        Sb = A.bf16(1024).rearrange("p (c v) -> p c v", c=2)
        f3 = lambda: A.f32(1024).rearrange("p (c t) -> p c t", c=2)
        b3 = lambda: A.bf16(1024).rearrange("p (c t) -> p c t", c=2)
        qT, kT, Ct, St = f3(), f3(), f3(), f3()
        e_, sp, cum, d3, E1, E2, E3, t1, qr, kr = [f3() for _ in range(10)]
        qd, ki, ke = b3(), b3(), b3()
        gts = A.f32(512, parts=16)
        kend = A.bf16(8 * 256, parts=64).rearrange("p (c d) -> p c d", c=8)
        vb = A.bf16(8 * 512, parts=64).rearrange("p (c v) -> p c v", c=8)
        dch = A.f32(16).rearrange("p (c n) -> p c n", c=2)
        att_sb = [A.bf16(64, parts=64) for _ in range(2)]
        o_sb = [A.f32(512, parts=64) for _ in range(2)]
        of_g = A.f32(8 * 512, parts=64).rearrange("p (c v) -> p c v", c=8)
        sr_g = A.f32(8 * 512, parts=64).rearrange("p (c v) -> p c v", c=8)
        gcT = A.bf16(4 * 512).rearrange("p (a t) -> p a t", a=4)
        res = [A.bf16(512, parts=64) for _ in range(2)]
        junk = A.f32(512, parts=64)
        ssq = [A.f32(1, parts=64) for _ in range(2)]
        glist = tok_groups()
        ci = 0
        for dirn in (0, 1):
            P.op('pool', lambda e: e.memset(S, 0.0), w=["S"])
            P.op('pool', lambda e: e.memset(Sb, 0.0), w=["Sb"])
            order = [0] + (list(range(1, 9)) if dirn == 0 else list(range(8, 0, -1)))
            for g in order:
                n0, n, _ = glist[g]
                nch = n // 64
                lat = g > 0
                tok0 = n0 - 256
                P.op('sp', lambda e, n0=n0, n=n: e.dma_start(
                    out=kT[:, :, 0:n], in_=kT_d.ap().rearrange("(c p) t -> p c t", p=128)[:, :, n0:n0 + n]),
                    r=["qkg_d"], w=["kT"], dma=True)
                P.op('sp', lambda e, n0=n0, n=n, dirn=dirn: e.dma_start(
                    out=gts[:, 0:n], in_=gT_d[dirn * 16:(dirn + 1) * 16, n0:n0 + n]), r=["qkg_d"], w=["gts"], dma=True)
                P.op('sp', lambda e, n0=n0, n=n, nch=nch: e.dma_start(
                    out=vb[:, 0:nch, :], in_=v_d[n0:n0 + n, :].rearrange("(c p) v -> p c v", p=64)),
                    r=["qkg_d"], w=["vb"], dma=True)
                if lat:
                    P.op('sp', lambda e, n0=n0, n=n: e.dma_start(
                        out=qT[:, :, 0:n], in_=qT_d.ap().rearrange("(c p) t -> p c t", p=128)[:, :, n0:n0 + n]),
                        r=["qkg_d"], w=["qT"], dma=True)
                    P.op('sp', lambda e, tok0=tok0, n=n: e.dma_start(
                        out=Ct[:, :, 0:n], in_=ropeC.ap().rearrange("c p t -> p c t")[:, :, tok0:tok0 + n]),
                        w=["Ct"], dma=True)
                    P.op('sp', lambda e, tok0=tok0, n=n: e.dma_start(
                        out=St[:, :, 0:n], in_=ropeS.ap().rearrange("c p t -> p c t")[:, :, tok0:tok0 + n]),
                        w=["St"], dma=True)
                    if dirn == 1:
                        P.op('sp', lambda e, tok0=tok0, n=n: e.dma_start(
                            out=of_g, in_=of_d[tok0:tok0 + n, :].rearrange("(c p) v -> p c v", p=64)),
                            r=["of_d"], w=["of_g"], dma=True)
                        P.op('sp', lambda e, n0=n0, n=n: e.dma_start(
                            out=sr_g, in_=sr_d[n0:n0 + n, :].rearrange("(c p) v -> p c v", p=64)),
                            r=["qkg_d"], w=["sr_g"], dma=True)
                        for c in range(8):
                            P.op('dve', lambda e, c=c: e.tensor_tensor(out=sr_g[:, c, :], in0=sr_g[:, c, :], in1=gnb,
                                                                       op=ALU.mult), r=["sr_g", "gnb"], w=["sr_g"])
                for dc in range(2):
                    P.op('pe', lambda e, dc=dc, n=n, dirn=dirn: e.matmul(
                        ps[dc][:, 0:n], lhsT=wgT[:, dirn, dc * 128:(dc + 1) * 128], rhs=gts[:, 0:n],
                        start=True, stop=True), r=["gts", "wgT"], w=[PSK[dc]])
                    P.op('act', lambda e, dc=dc, n=n, dirn=dirn: e.activation(
                        out=e_[:, dc, 0:n], in_=ps[dc][:, 0:n], func=AF.Exp, scale=-1.0,
                        bias=nbg[:, dirn * 2 + dc:dirn * 2 + dc + 1]), r=[PSK[dc], "nbg"], w=["e_"])
                    P.op('act', lambda e, dc=dc, n=n: e.activation(
                        out=sp[:, dc, 0:n], in_=e_[:, dc, 0:n], func=AF.Ln, bias=one1, scale=1.0),
                        r=["e_", "one1"], w=["sp"])
                    P.op('dve', lambda e, dc=dc, n=n: e.tensor_tensor_scan(
                        out=cum[:, dc, 0:n], data0=cmask[:, 0:n], data1=sp[:, dc, 0:n], initial=0.0,
                        op0=ALU.mult, op1=ALU.add), r=["sp", "cmask"], w=["cum"])
                cum4 = cum[:, :, 0:n].rearrange("p c (h t) -> p c h t", t=64)
                tot = cum4[:, :, :, 63:64]
                for dc in range(2):
                    P.op('dve', lambda e, dc=dc, n=n, nch=nch: e.tensor_tensor(
                        out=d3[:, dc, 0:n].rearrange("p (h t) -> p h t", t=64),
                        in0=cum[:, dc, 0:n].rearrange("p (h t) -> p h t", t=64),
                        in1=cum[:, dc, 0:n].rearrange("p (h t) -> p h t", t=64)[:, :, 63:64].to_broadcast([128, nch, 64]),
                        op=ALU.subtract), r=["cum"], w=["d3"])
                    P.op('act', lambda e, dc=dc, n=n, nch=nch: e.activation(
                        out=dch[:, dc, 0:nch], in_=cum[:, dc, 0:n].rearrange("p (h t) -> p h t", t=64)[:, :, 63],
                        func=AF.Exp, scale=-1.0 / 16), r=["cum"], w=["dch"])
                if dirn == 0:
                    P.op('act', lambda e, n=n: e.activation(out=E1[:, :, 0:n], in_=cum[:, :, 0:n], func=AF.Exp,
                                                            scale=-1.0 / 16), r=["cum"], w=["E1"])
                    P.op('act', lambda e, n=n: e.activation(out=E2[:, :, 0:n], in_=cum[:, :, 0:n], func=AF.Exp,
                                                            scale=1.0 / 16), r=["cum"], w=["E2"])
                    P.op('act', lambda e, n=n: e.activation(out=E3[:, :, 0:n], in_=d3[:, :, 0:n], func=AF.Exp,
                                                            scale=1.0 / 16), r=["d3"], w=["E3"])
                else:
                    P.op('dve', lambda e, n=n: e.tensor_tensor(out=t1[:, :, 0:n], in0=sp[:, :, 0:n], in1=d3[:, :, 0:n],
                                                               op=ALU.subtract), r=["sp", "d3"], w=["t1"])
                    P.op('act', lambda e, n=n: e.activation(out=E1[:, :, 0:n], in_=t1[:, :, 0:n], func=AF.Exp,
                                                            scale=-1.0 / 16), r=["t1"], w=["E1"])
                    P.op('act', lambda e, n=n: e.activation(out=E2[:, :, 0:n], in_=t1[:, :, 0:n], func=AF.Exp,
                                                            scale=1.0 / 16), r=["t1"], w=["E2"])
                    P.op('dve', lambda e, n=n: e.tensor_tensor(out=d3[:, :, 0:n], in0=cum[:, :, 0:n], in1=sp[:, :, 0:n],
                                                               op=ALU.subtract), r=["sp", "cum", "E1"], w=["d3"])
                    P.op('act', lambda e, n=n: e.activation(out=E3[:, :, 0:n], in_=d3[:, :, 0:n], func=AF.Exp,
                                                            scale=-1.0 / 16), r=["d3"], w=["E3"])
                if lat:
                    for (src, dstr, nm) in ((qT, qr, "q"), (kT, kr, "k")):
                        for dc in range(2):
                            pb = 2 + dc
                            P.op('pe', lambda e, pb=pb, dc=dc, n=n, src=src: e.matmul(
                                ps[pb][:, 0:n], lhsT=rotT, rhs=src[:, dc, 0:n], start=True, stop=True),
                                r=[nm + "T", "rotT"], w=[PSK[pb]])
                            P.op('dve', lambda e, pb=pb, dc=dc, n=n: e.tensor_tensor(
                                out=t1[:, dc, 0:n], in0=ps[pb][:, 0:n], in1=St[:, dc, 0:n], op=ALU.mult),
                                r=[PSK[pb], "St", "E1", "E2"], w=["t1"])
                            P.op('pool', lambda e, dc=dc, n=n, src=src, dstr=dstr: e.tensor_tensor(
                                out=dstr[:, dc, 0:n], in0=src[:, dc, 0:n], in1=Ct[:, dc, 0:n], op=ALU.mult),
                                r=[nm + "T", "Ct"], w=[nm + "r"])
                            P.op('dve', lambda e, dc=dc, n=n, dstr=dstr: e.tensor_tensor(
                                out=dstr[:, dc, 0:n], in0=dstr[:, dc, 0:n], in1=t1[:, dc, 0:n], op=ALU.add),
                                r=[nm + "r", "t1"], w=[nm + "r"])
                    ksrc, kname = kr, "kr"
                    for dc in range(2):
                        P.op('dve', lambda e, dc=dc, n=n: e.scalar_tensor_tensor(
                            out=qd[:, dc, 0:n], in0=qr[:, dc, 0:n], scalar=0.0625, in1=E1[:, dc, 0:n],
                            op0=ALU.mult, op1=ALU.mult), r=["qr", "E1"], w=["qd"])
                        P.op('dve', lambda e, dc=dc, n=n: e.tensor_tensor(
                            out=ki[:, dc, 0:n], in0=kr[:, dc, 0:n], in1=E2[:, dc, 0:n], op=ALU.mult),
                            r=["kr", "E2"], w=["ki"])
                else:
                    ksrc, kname = kT, "kT"
                for dc in range(2):
                    P.op('pool', lambda e, dc=dc, n=n, ksrc=ksrc: e.tensor_tensor(
                        out=ke[:, dc, 0:n], in0=ksrc[:, dc, 0:n], in1=E3[:, dc, 0:n], op=ALU.mult),
                        r=[kname, "E3"], w=["ke"])
                for half in range((nch + 3) // 4):
                    pb = 2 + half
                    pv = ps[pb][0:64, :].bitcast(BF16).rearrange("p (c d) -> p c d", c=4)
                    for cc in range(4):
                        c = half * 4 + cc
                        for dc in range(2):
                            P.op('pe', lambda e, pv=pv, cc=cc, c=c, dc=dc: e.transpose(
                                pv[:, cc, dc * 128:(dc + 1) * 128], ke[:, dc, c * 64:(c + 1) * 64], identb),
                                r=["ke", "identb"], w=[PSK[pb]])
                    P.op('act', lambda e, pv=pv, half=half: e.activation(out=kend[:, half * 4:half * 4 + 4, :], in_=pv,
                                                                        func=AF.Copy), r=[PSK[pb]], w=["kend"])
                clist = list(range(nch)) if dirn == 0 else list(range(nch - 1, -1, -1))
                for c in clist:
                    cs = slice(c * 64, (c + 1) * 64)
                    j = ci % 2
                    ci += 1
                    pa, po = (2, 3) if j == 0 else (6, 7)
                    if lat:
                        for dc in range(2):
                            P.op('pe', lambda e, pa=pa, dc=dc, cs=cs: e.matmul(
                                ps[pa][0:64, 0:64], lhsT=ki[:, dc, cs], rhs=qd[:, dc, cs],
                                start=(dc == 0), stop=(dc == 1)), r=["ki", "qd"], w=[PSK[pa]])
                        P.op('dve', lambda e, pa=pa, j=j, dirn=dirn: e.tensor_tensor(
                            out=att_sb[j], in0=ps[pa][0:64, 0:64], in1=trm[:, dirn, :], op=ALU.mult),
                            r=[PSK[pa], "trm"], w=[("att", j)])
                        P.op('pe', lambda e, po=po, j=j, c=c: e.matmul(
                            ps[po][0:64, 0:512], lhsT=att_sb[j], rhs=vb[:, c, :], start=True, stop=False),
                            r=[("att", j), "vb"], w=[PSK[po]])
                        for dc in range(2):
                            P.op('pe', lambda e, po=po, dc=dc, cs=cs: e.matmul(
                                ps[po][0:64, 0:512], lhsT=qd[:, dc, cs], rhs=Sb[:, dc, :],
                                start=False, stop=(dc == 1)), r=["qd", "Sb"], w=[PSK[po]])
                    for dc in range(2):
                        P.op('pe', lambda e, dc=dc, c=c: e.matmul(
                            ps[4 + dc][:, 0:512], lhsT=kend[:, c, dc * 128:(dc + 1) * 128], rhs=vb[:, c, :],
                            start=True, stop=True), r=["kend", "vb"], w=[PSK[4 + dc]])
                        P.op('dve', lambda e, dc=dc, c=c: e.scalar_tensor_tensor(
                            out=S[:, dc, :], in0=S[:, dc, :], scalar=dch[:, dc, c:c + 1], in1=ps[4 + dc][:, 0:512],
                            op0=ALU.mult, op1=ALU.add), r=["S", "dch", PSK[4 + dc]], w=["S"])
                        P.op('act', lambda e, dc=dc: e.activation(out=Sb[:, dc, :], in_=S[:, dc, :], func=AF.Copy),
                             r=["S"], w=["Sb"])
                    if lat and dirn == 0:
                        P.op('act', lambda e, po=po, j=j: e.activation(out=o_sb[j], in_=ps[po][0:64, 0:512],
                                                                      func=AF.Copy), r=[PSK[po]], w=[("o_sb", j)])
                        P.op('pool', lambda e, j=j, tok0=tok0, c=c: e.dma_start(
                            out=of_d[tok0 + c * 64:tok0 + (c + 1) * 64, :], in_=o_sb[j]),
                            r=[("o_sb", j)], w=["of_d"], dma=True)
                    elif lat:
                        P.op('dve', lambda e, po=po, j=j, c=c: e.tensor_tensor(
                            out=o_sb[j], in0=ps[po][0:64, 0:512], in1=of_g[:, c, :], op=ALU.add),
                            r=[PSK[po], "of_g"], w=[("o_sb", j)])
                        P.op('act', lambda e, j=j: e.activation(out=junk, in_=o_sb[j], func=AF.Square,
                                                                accum_out=ssq[j]), r=[("o_sb", j)], w=["junk", ("ssq", j)])
                        P.op('act', lambda e, j=j: e.activation(out=ssq[j], in_=ssq[j], func=AF.Sqrt, bias=epsb[0:64, :],
                                                                scale=1.0 / 512), r=[("ssq", j), "epsb"], w=[("ssq", j)])
                        P.op('dve', lambda e, j=j: e.reciprocal(out=ssq[j], in_=ssq[j]), r=[("ssq", j)], w=[("ssq", j)])
                        P.op('dve', lambda e, j=j, c=c: e.scalar_tensor_tensor(
                            out=res[j], in0=o_sb[j], scalar=ssq[j], in1=sr_g[:, c, :], op0=ALU.mult, op1=ALU.mult),
                            r=[("o_sb", j), ("ssq", j), "sr_g"], w=[("res", j)])
                        pv = ps[pa][:, 0:128].bitcast(BF16).rearrange("p (a t) -> p a t", a=4)
                        for a in range(4):
                            P.op('pe', lambda e, pv=pv, a=a, j=j: e.transpose(
                                pv[:, a, :], res[j][:, a * 128:(a + 1) * 128], identb[0:64, 0:64]),
                                r=[("res", j), "identb"], w=[PSK[pa]])
                        P.op('act', lambda e, pv=pv, cs=cs: e.activation(out=gcT[:, :, cs], in_=pv, func=AF.Copy),
                             r=[PSK[pa]], w=["gcT"])
                if lat and dirn == 1:
                    qt, c0 = tok0 // 1024, tok0 % 1024
                    P.op('pool', lambda e, qt=qt, c0=c0: e.dma_start(
                        out=mixT_d[qt * 768:qt * 768 + 512, c0:c0 + 512].rearrange("(a p) t -> p a t", p=128),
                        in_=gcT), r=["gcT"], w=["mixT_d"], dma=True)
        P.barrier()

    def na():
        A.reset(0)
        nq = A.bf16(SEQ, parts=64)
        nk = A.bf16(NS, parts=64)
        Ve = A.bf16(34 * 64).rearrange("p (b d) -> p b d", b=34)
        Vo = A.bf16(31 * 64).rearrange("p (b d) -> p b d", b=31)
        TBh = A.f32(15 * 64, parts=64)
        naT = A.bf16(SEQ, parts=64)
        s_sb = [A.f32(512, parts=64) for _ in range(2)]
        Pm = [A.bf16(768, parts=64) for _ in range(2)]
        PT = [A.bf16(6 * 64).rearrange("p (b q) -> p b q", b=6) for _ in range(2)]
        osb = [A.bf16(64, parts=64) for _ in range(2)]
        sm = [A.f32(8, parts=64) for _ in range(2)]
        for hh in range(4):
            hs = slice(hh * 64, (hh + 1) * 64)
            P.op('sp', lambda e, hs=hs: e.dma_start(out=nq, in_=nqT_d[hs, 256:NS]), r=["qkg_d"], w=["nq"], dma=True)
            P.op('sp', lambda e, hs=hs: e.dma_start(out=nk, in_=nkT_d[hs, :]), r=["qkg_d"], w=["nk"], dma=True)
            P.op('sp', lambda e, hs=hs: e.dma_start(
                out=Ve, in_=nv_d[:, hs].rearrange("(b p) d -> p b d", p=128)), r=["qkg_d"], w=["Ve"], dma=True)
            P.op('sp', lambda e, hs=hs: e.dma_start(
                out=Vo, in_=nv_d[320:320 + 31 * 128, hs].rearrange("(b p) d -> p b d", p=128)),
                r=["qkg_d"], w=["Vo"], dma=True)
            P.op('sp', lambda e, hh=hh: e.dma_start(out=TBh, in_=tb_in[hh, :, :]), w=["TBh"], dma=True)
            for r in range(64):
                j = r % 2
                pl, pc, pt, po = (0, 1, 2, 3) if j == 0 else (4, 5, 6, 7)
                j0 = min(max(r - 4, 0), 56)
                s0 = j0 - r + 7
                qs = slice(r * 64, (r + 1) * 64)
                P.op('pe', lambda e, pl=pl, qs=qs, j0=j0: e.matmul(
                    ps[pl][0:64, 0:512], lhsT=nq[:, qs], rhs=nk[:, 256 + j0 * 64:256 + j0 * 64 + 512],
                    start=True, stop=True), r=["nq", "nk"], w=[PSK[pl]])
                P.op('pe', lambda e, pc=pc, qs=qs: e.matmul(
                    ps[pc][0:64, 0:256], lhsT=nq[:, qs], rhs=nk[:, 0:256], start=True, stop=True),
                    r=["nq", "nk"], w=[PSK[pc]])
                P.op('dve', lambda e, pl=pl, j=j, s0=s0: e.tensor_tensor(
                    out=s_sb[j], in0=ps[pl][0:64, 0:512], in1=TBh[:, s0 * 64:(s0 + 8) * 64], op=ALU.add),
                    r=[PSK[pl], "TBh"], w=[("s_sb", j)])
                P.op('dve', lambda e, j=j: e.reduce_max(out=sm[j][:, 0:1], in_=s_sb[j], axis=mybir.AxisListType.X),
                     r=[("s_sb", j)], w=[("sm", j)])
                P.op('dve', lambda e, j=j, pc=pc: e.reduce_max(out=sm[j][:, 1:2], in_=ps[pc][0:64, 0:256],
                                                               axis=mybir.AxisListType.X),
                     r=[PSK[pc]], w=[("sm", j)])
                P.op('dve', lambda e, j=j: e.tensor_tensor(out=sm[j][:, 2:3], in0=sm[j][:, 0:1], in1=sm[j][:, 1:2],
                                                           op=ALU.max), r=[("sm", j)], w=[("sm", j)])
                P.op('dve', lambda e, j=j: e.tensor_scalar(out=sm[j][:, 3:4], in0=sm[j][:, 2:3], scalar1=-1.0,
                                                           scalar2=None, op0=ALU.mult), r=[("sm", j)], w=[("sm", j)])
                P.op('act', lambda e, j=j: e.activation(out=Pm[j][:, 0:512], in_=s_sb[j], func=AF.Exp,
                                                        bias=sm[j][:, 3:4], scale=1.0, accum_out=sm[j][:, 4:5]),
                     r=[("s_sb", j), ("sm", j)], w=[("Pm", j), ("sm2", j)])
                P.op('act', lambda e, j=j, pc=pc: e.activation(out=Pm[j][:, 512:768], in_=ps[pc][0:64, 0:256], func=AF.Exp,
                                                               bias=sm[j][:, 3:4], scale=1.0, accum_out=sm[j][:, 5:6]),
                     r=[PSK[pc], ("sm", j)], w=[("Pm", j), ("sm2", j)])
                pv = ps[pt][:, 0:192].bitcast(BF16).rearrange("p (b q) -> p b q", b=6)
                for blk in range(6):
                    P.op('pe', lambda e, pv=pv, blk=blk, j=j: e.transpose(
                        pv[:, blk, :], Pm[j][:, blk * 128:(blk + 1) * 128], identb[0:64, 0:64]),
                        r=[("Pm", j), "identb"], w=[PSK[pt]])
                P.op('dve', lambda e, pv=pv, j=j: e.tensor_copy(out=PT[j], in_=pv), r=[PSK[pt]], w=[("PT", j)])
                for blk in range(6):
                    if blk < 4:
                        vsrc = Ve[:, 2 + j0 // 2 + blk, :] if j0 % 2 == 0 else Vo[:, (j0 - 1) // 2 + blk, :]
                    else:
                        vsrc = Ve[:, blk - 4, :]
                    P.op('pe', lambda e, po=po, blk=blk, j=j, vsrc=vsrc: e.matmul(
                        ps[po][0:64, 0:64], lhsT=PT[j][:, blk, :], rhs=vsrc, start=(blk == 0), stop=(blk == 5)),
                        r=[("PT", j), "Ve", "Vo"], w=[PSK[po]])
                P.op('dve', lambda e, j=j: e.tensor_tensor(out=sm[j][:, 6:7], in0=sm[j][:, 4:5], in1=sm[j][:, 5:6],
                                                           op=ALU.add), r=[("sm2", j)], w=[("sm3", j)])
                P.op('dve', lambda e, j=j: e.reciprocal(out=sm[j][:, 7:8], in_=sm[j][:, 6:7]),
                     r=[("sm3", j)], w=[("sm3", j)])
                P.op('dve', lambda e, j=j, po=po: e.tensor_scalar(out=osb[j], in0=ps[po][0:64, 0:64],
                                                                  scalar1=sm[j][:, 7:8], scalar2=None, op0=ALU.mult),
                     r=[PSK[po], ("sm3", j)], w=[("osb", j)])
                pv2 = ps[po][0:64, 256:288].bitcast(BF16)
                P.op('pe', lambda e, pv2=pv2, j=j: e.transpose(pv2, osb[j], identb[0:64, 0:64]),
                     r=[("osb", j), "identb"], w=[PSK[po]])
                P.op('act', lambda e, pv2=pv2, qs=qs: e.activation(out=naT[:, qs], in_=pv2, func=AF.Copy),
                     r=[PSK[po]], w=["naT"])
            for qt in range(4):
                P.op('pool', lambda e, qt=qt, hh=hh: e.dma_start(
                    out=mixT_d[qt * 768 + 512 + hh * 64:qt * 768 + 512 + (hh + 1) * 64, :],
                    in_=naT[:, qt * 1024:(qt + 1) * 1024]), r=["naT"], w=["mixT_d"], dma=True)
        P.barrier()

    def merge_out():
        A.reset(0)
        MX = A.bf16(24 * NL).rearrange("p (c t) -> p c t", c=24)
        hmo = A.bf16(KC * NL).rearrange("p (k t) -> p k t", k=KC)
        MG = A.bf16(KC * NL).rearrange("p (k t) -> p k t", k=KC)
        mark = A.off
        selS = A.f32(4)
        tmpx = [A.bf16(6 * NL) for _ in range(2)]
        P.op('sp', lambda e: e.dma_start(out=selS, in_=sel_in[:, :]), w=["selS"], dma=True)
        P.op('sp', lambda e: e.dma_start(
            out=hmo, in_=hmT_d.ap().rearrange("(k p) t -> p k t", p=128)[:, :, 0:NL]), w=["hmo"], dma=True)
        n = 0
        for r in range(4):
            dst = MX[:, r * 6:(r + 1) * 6, :].rearrange("p c t -> p (c t)")
            for q in range(4):
                tb_ = tmpx[n % 2]
                tk = ("tmpx", n % 2)
                n += 1
                for hf in range(2):
                    r0 = (2 * q + hf) * 1536 + r * 384
                    P.op('sp', lambda e, tb_=tb_, r0=r0, hf=hf: e.dma_start(
                        out=tb_.rearrange("p (c t) -> p c t", c=6)[:, 3 * hf:3 * hf + 3, :],
                        in_=mixin_d[r0:r0 + 384, :].rearrange("(c p) t -> p c t", p=128)), w=[tk], dma=True)
                if q == 0:
                    P.op('dve', lambda e, dst=dst, tb_=tb_, q=q: e.tensor_scalar(
                        out=dst, in0=tb_, scalar1=selS[:, q:q + 1], scalar2=None, op0=ALU.mult),
                        r=[tk, "selS"], w=["MX"])
                else:
                    P.op('dve', lambda e, dst=dst, tb_=tb_, q=q: e.scalar_tensor_tensor(
                        out=dst, in0=tb_, scalar=selS[:, q:q + 1], in1=dst, op0=ALU.mult, op1=ALU.add),
                        r=[tk, "selS", "MX"], w=["MX"])
        P.barrier()
        A.reset(mark)
        wsf = [A.f32(KC * 128).rearrange("p (k c) -> p k c", k=KC) for _ in range(4)]
        wsb = [A.bf16(KC * 128).rearrange("p (k c) -> p k c", k=KC) for _ in range(4)]
        sgm = [A.f32(512) for _ in range(4)]
        wgo = w_gla_o.ap().rearrange("(i p) d -> p i d", p=128)
        wno = w_na_o.ap().rearrange("(i p) d -> p i d", p=128)
        wmv = w_m.ap().rearrange("(k p) c -> p k c", p=128)
        it = 0
        for dc in range(KC):
            dsl = slice(dc * 128, (dc + 1) * 128)
            srcs = [(wgo[:, :, dsl], 16), (wno[:, :, dsl], 8), (wmv[:, :, dsl], 16),
                    (wmv[:, :, 2048 + dc * 128:2048 + (dc + 1) * 128], 16)]
            for wi, (src, ni) in enumerate(srcs):
                P.op('sp', lambda e, wi=wi, src=src, ni=ni: e.dma_start(out=wsf[wi][:, 0:ni, :], in_=src),
                     w=[("wsf", wi)], dma=True)
                P.op('pool', lambda e, wi=wi, ni=ni: e.tensor_copy(out=wsb[wi][:, 0:ni, :], in_=wsf[wi][:, 0:ni, :]),
                     r=[("wsf", wi)], w=[("wsb", wi)])
            for grp in range(2):
                gs = slice(grp * 512, (grp + 1) * 512)
                base = 4 * (it % 2)
                it += 1
                pa, pbk, p1, p2 = base, base + 1, base + 2, base + 3
                for i in range(16):
                    P.op('pe', lambda e, pa=pa, i=i, gs=gs: e.matmul(
                        ps[pa][:, :], lhsT=wsb[0][:, i, :], rhs=MX[:, (i // 4) * 6 + i % 4, gs],
                        start=(i == 0), stop=(i == 15)), r=[("wsb", 0), "MX"], w=[PSK[pa]])
                for i in range(8):
                    P.op('pe', lambda e, pbk=pbk, i=i, gs=gs: e.matmul(
                        ps[pbk][:, :], lhsT=wsb[1][:, i, :], rhs=MX[:, (i // 2) * 6 + 4 + i % 2, gs],
                        start=(i == 0), stop=(i == 7)), r=[("wsb", 1), "MX"], w=[PSK[pbk]])
                for (pm, wi) in ((p1, 2), (p2, 3)):
                    for k in range(KC):
                        P.op('pe', lambda e, pm=pm, wi=wi, k=k, gs=gs: e.matmul(
                            ps[pm][:, :], lhsT=wsb[wi][:, k, :], rhs=hmo[:, k, gs],
                            start=(k == 0), stop=(k == KC - 1)), r=[("wsb", wi), "hmo"], w=[PSK[pm]])
                P.op('act', lambda e, p1=p1: e.activation(out=sgm[0], in_=ps[p1][:, :], func=AF.Sigmoid),
                     r=[PSK[p1]], w=[("sgm", 0)])
                P.op('act', lambda e, p2=p2: e.activation(out=sgm[1], in_=ps[p2][:, :], func=AF.Sigmoid),
                     r=[PSK[p2]], w=[("sgm", 1)])
                P.op('dve', lambda e, pa=pa: e.tensor_tensor(out=sgm[2], in0=sgm[0], in1=ps[pa][:, :], op=ALU.mult),
                     r=[("sgm", 0), PSK[pa]], w=[("sgm", 2)])
                P.op('dve', lambda e, pbk=pbk: e.tensor_tensor(out=sgm[3], in0=sgm[1], in1=ps[pbk][:, :], op=ALU.mult),
                     r=[("sgm", 1), PSK[pbk]], w=[("sgm", 3)])
                P.op('pool', lambda e, dc=dc, gs=gs: e.tensor_tensor(out=MG[:, dc, gs], in0=sgm[2], in1=sgm[3],
                                                                      op=ALU.add),
                     r=[("sgm", 2), ("sgm", 3)], w=["MG"])
        P.barrier()
        A.reset(mark)
        wsf2 = [A.f32(KC * 128).rearrange("p (k c) -> p k c", k=KC) for _ in range(2)]
        wsb2 = [A.bf16(KC * 128).rearrange("p (k c) -> p k c", k=KC) for _ in range(2)]
        ysb = [A.f32(NT) for _ in range(2)]
        hsb = [A.f32(NT) for _ in range(2)]
        ysq = [A.bf16(NT) for _ in range(2)]
        rstd = A.f32(NT)
        wov = w_out.ap().rearrange("(k p) d -> p k d", p=128)
        it = 0
        for dc in range(KC):
            i = dc % 2
            P.op('sp', lambda e, i=i, dc=dc: e.dma_start(out=wsf2[i], in_=wov[:, :, dc * 128:(dc + 1) * 128]),
                 w=[("wsf", i)], dma=True)
            P.op('pool', lambda e, i=i: e.tensor_copy(out=wsb2[i], in_=wsf2[i]), r=[("wsf", i)], w=[("wsb", i)])
            for grp in range(2):
                gs = slice(grp * 512, (grp + 1) * 512)
                pb = 2 + (it % 6)
                it += 1
                for k in range(KC):
                    P.op('pe', lambda e, pb=pb, k=k, gs=gs, i=i: e.matmul(
                        ps[pb][:, :], lhsT=wsb2[i][:, k, :], rhs=MG[:, k, gs], start=(k == 0), stop=(k == KC - 1)),
                        r=[("wsb", i), "MG"], w=[PSK[pb]])
                P.op('act', lambda e, pb=pb, gs=gs, i=i: e.activation(out=ysb[i][:, gs], in_=ps[pb][:, :], func=AF.Copy),
                     r=[PSK[pb]], w=[("ysb", i)])
                P.op('act', lambda e, pb=pb, gs=gs, i=i: e.activation(out=ysq[i][:, gs], in_=ps[pb][:, :], func=AF.Square),
                     r=[PSK[pb]], w=[("ysq", i)])
            for grp in range(2):
                gs = slice(grp * 512, (grp + 1) * 512)
                P.op('pe', lambda e, grp=grp, gs=gs, i=i, dc=dc: e.matmul(
                    ps[grp][:, :], lhsT=ones_bf, rhs=ysq[i][:, gs], start=(dc == 0), stop=(dc == KC - 1)),
                    r=[("ysq", i), "ones"], w=[PSK[grp]])
            P.op('pool', lambda e, i=i, dc=dc: e.dma_start(out=yT_d[dc, :, 0:NL], in_=ysb[i][:, 0:NL]),
                 r=[("ysb", i)], w=[("yT_d", dc)], dma=True)
        for grp in range(2):
            gs = slice(grp * 512, (grp + 1) * 512)
            P.op('act', lambda e, grp=grp, gs=gs: e.activation(
                out=rstd[:, gs], in_=ps[grp][:, :], func=AF.Sqrt, bias=epsb, scale=1.0 / D),
                r=[PSK[grp], "epsb"], w=["rstd2"])
        P.op('dve', lambda e: e.reciprocal(out=rstd[:, 0:NL], in_=rstd[:, 0:NL]), r=["rstd2"], w=["rstd2"])
        residual_update(1, rstd, ysb, hsb, ncols=NL)

    def final_out():
        A.reset(0)
        HT = A.f32(KC * NL).rearrange("p (k t) -> p k t", k=KC)
        xo = [A.f32(D) for _ in range(2)]
        for dc in range(KC):
            P.op('sp', lambda e, dc=dc: e.dma_start(out=HT[:, dc, :], in_=hT_d[dc, :, 0:NL]),
                 r=[("hT_d", dc)], w=["HT"], dma=True)
        it = 0
        for tt in range(8):
            xb = xo[tt % 2]
            xk = ("xo", tt % 2)
            for kq in range(4):
                pb = it % 8
                it += 1
                for kk in range(4):
                    k = kq * 4 + kk
                    P.op('pe', lambda e, pb=pb, kk=kk, k=k, tt=tt: e.transpose(
                        ps[pb][:, kk * 128:(kk + 1) * 128], HT[:, k, tt * 128:(tt + 1) * 128], ident),
                        r=["HT", "ident"], w=[PSK[pb]])
                if kq % 2 == 0:
                    P.op('act', lambda e, pb=pb, xb=xb, kq=kq: e.activation(
                        out=xb[:, kq * 512:(kq + 1) * 512], in_=ps[pb][:, :], func=AF.Copy), r=[PSK[pb]], w=[xk])
                else:
                    P.op('dve', lambda e, pb=pb, xb=xb, kq=kq: e.tensor_copy(
                        out=xb[:, kq * 512:(kq + 1) * 512], in_=ps[pb][:, :]), r=[PSK[pb]], w=[xk])
            P.op('sp', lambda e, xb=xb, tt=tt: e.dma_start(out=out[tt * 128:(tt + 1) * 128, :], in_=xb),
                 r=[xk], w=["out"], dma=True)

    if mode == 'full':
        prenorm_stage(0, True)
        ffn_core(0, 0)

    if 'h1T' in dbg:
        for dc in range(KC):
            P.op('sp', lambda e, dc=dc: e.dma_start(out=dbg['h1T'][dc, :, :], in_=hT_d[dc, :, :]),
                 r=[("hT_d", dc)], dma=True)

    if mode == 'full':
        hm = prenorm_stage(1, False)
    for c in range(8):
        if mode == 'full':
            P.op('sp', lambda e, c=c: e.dma_start(
                out=hmT_d[c * 256:(c + 1) * 256, :].rearrange("(k p) t -> p k t", p=128), in_=hm[:, 2 * c:2 * c + 2, :]),
                r=["hm"], w=[("hmT_d", c)], dma=True)
        else:
            P.op('sp', lambda e, c=c: e.dma_start(out=hmT_d[c * 256:(c + 1) * 256, :],
                                                   in_=hm_in[c * 256:(c + 1) * 256, :]), w=[("hmT_d", c)], dma=True)
        P.op('pool', lambda e, c=c: e.collective_compute(
            "AllGather", ALU.bypass, replica_groups=groups,
            ins=[hmT_d[c * 256:(c + 1) * 256, :].opt()], outs=[hmT_all[c * 1024:(c + 1) * 1024, :].opt()]),
            r=[("hmT_d", c)], w=["hmT_all"], cc=True)
    P.barrier()
    STG = os.environ.get("K_STAGES", "inproj,gla,na,ag2").split(",")
    if "inproj" in STG:
        mixer_inproj()
    if "gla" in STG:
        gla()
    if "na" in STG:
        na()
    if "ag2" in STG:
        for c in range(8):
            P.op('pool', lambda e, c=c: e.collective_compute(
                "AllGather", ALU.bypass, replica_groups=groups,
                ins=[mixT_d[c * 384:(c + 1) * 384, :].opt()], outs=[mixin_d[c * 1536:(c + 1) * 1536, :].opt()]),
                w=["mixin"], cc=True)
    P.barrier()
    if 'mixT' in dbg:
        P.op('sp', lambda e: e.dma_start(out=dbg['mixT'][:, :], in_=mixT_d[:, :]), dma=True)
        P.barrier()
    if mode == 'full':
        merge_out()
        prenorm_stage(2, False)
        ffn_core(2, 1)
        final_out()
    P.barrier()
    P.emit(nc, st)
    return nc, st


def _consts():
    th = 10000.0
    freqs = th ** (-np.arange(64, dtype=np.float32) / 64.0)
    t = np.arange(SEQ)
    row, col = (t // 64).astype(np.float32), (t % 64).astype(np.float32)
    ropeC = np.zeros((2, 128, SEQ), np.float32)
    ropeS = np.zeros((2, 128, SEQ), np.float32)
    for c, pos in enumerate((row, col)):
        ang = pos[None, :] * freqs[:, None]
        ropeC[c, :64], ropeC[c, 64:] = np.cos(ang), np.cos(ang)
        ropeS[c, :64], ropeS[c, 64:] = np.sin(ang), np.sin(ang)
    rotT = np.zeros((128, 128), np.float32)
    for m in range(64):
        rotT[m + 64, m] = -1.0
        rotT[m, m + 64] = 1.0
    cmask = np.ones((128, 512), np.float32)
    cmask[:, ::64] = 0.0
    ii = np.arange(64)
    trm = np.zeros((64, 2, 64), np.float32)
    trm[:, 0, :] = (ii[None, :] >= ii[:, None])
    trm[:, 1, :] = (ii[None, :] <= ii[:, None])
    return dict(ropeC=ropeC, ropeS=ropeS, rotT=rotT, cmask=cmask, trm=np.ascontiguousarray(trm.reshape(64, 128)),
                identf=np.eye(128, dtype=np.float32), identb=np.eye(128).astype(ml_dtypes.bfloat16))


def _prep_inputs(inputs, ncores=8, mode='full', hm_in=None):
    f = lambda k: np.asarray(inputs[k], np.float32)
    x, c, ctx, c_ctx = f('x'), f('c'), f('ctx'), f('c_ctx')
    w_in = f('w_in')[0]
    gla_wg, gla_bg, gng = f('gla_wg')[0], f('gla_bg')[0], f('gla_norm_g')[0]
    rpb = f('na_rpb')[0]
    shared = _consts()
    shared['w_m'] = np.ascontiguousarray(w_in[:, 9248:13344])
    shared['gng'] = np.ascontiguousarray(gng.reshape(1, 512))
    shared['w_gla_o'] = np.ascontiguousarray(f('w_gla_o')[0])
    shared['w_na_o'] = np.ascontiguousarray(f('w_na_o')[0])
    shared['w_out'] = np.ascontiguousarray(f('w_out')[0])
    w_ada_full = f('w_ada')[0]
    bT_full = np.ascontiguousarray(f('b_ada')[0].reshape(144, 128).T)
    shared['gT'] = np.ascontiguousarray(f('norm_g')[0].reshape(6, KC, 128).transpose(2, 0, 1))
    shared['ffn_wg'] = np.ascontiguousarray(f('ffn_wg')[0])
    shared['ffn_wu'] = np.ascontiguousarray(f('ffn_wu')[0])
    shared['ffn_wd'] = np.ascontiguousarray(f('ffn_wd')[0])
    q = np.arange(64)
    k = np.arange(64)
    dj = np.clip(k[None, :] - q[:, None], -15, 15) + 15
    cs = np.clip(q - 8, 0, 48)
    ok = (k[None, :] >= cs[:, None]) & (k[None, :] < cs[:, None] + 16)
    in_maps = []
    for core in range(ncores):
        b, h = core // 4, core % 4
        s = h
        m = dict(shared)
        xin = np.concatenate([x[b, s * 1024:(s + 1) * 1024], ctx[b, s * 64:(s + 1) * 64]], axis=0)
        m['xin'] = np.ascontiguousarray(xin)
        m['w_ada'] = np.ascontiguousarray(w_ada_full[:, s * 4608:(s + 1) * 4608])
        m['bT'] = np.ascontiguousarray(bT_full[:, s * 36:(s + 1) * 36])
        selv = np.zeros((128, 4), np.float32)
        selv[:, s] = 1.0
        m['sel'] = selv
        cin = np.stack([c[b], c_ctx], axis=0)
        m['cT'] = np.ascontiguousarray(cin.reshape(2, KC, 128).transpose(2, 1, 0))
        cols = np.concatenate([
            np.arange(h * 256, (h + 1) * 256), 1024 + np.arange(h * 256, (h + 1) * 256),
            np.arange(6144, 6176),
            6176 + np.arange(h * 256, (h + 1) * 256), 7200 + np.arange(h * 256, (h + 1) * 256),
            2048 + np.arange(h * 512, (h + 1) * 512), 4096 + np.arange(h * 512, (h + 1) * 512),
            8224 + np.arange(h * 256, (h + 1) * 256)])
        m['w_inh'] = np.ascontiguousarray(w_in[:, cols])
        m['gwg'] = np.ascontiguousarray(gla_wg[:, :, h * 256:(h + 1) * 256].transpose(1, 0, 2))
        m['gbgT'] = np.ascontiguousarray(gla_bg[:, h * 256:(h + 1) * 256].reshape(2, 2, 128).transpose(2, 0, 1).reshape(128, 4))
        tbl = np.empty((4, 64, 15, 64), np.float32)
        for hh in range(4):
            g = rpb[4 * h + hh][:, dj]
            tbl[hh] = np.where(ok[None], g, np.float32(-1e30)).transpose(1, 0, 2)
        m['tb'] = np.ascontiguousarray(tbl.reshape(4, 64, 15 * 64))
        if mode != 'full':
            m['hm_in'] = hm_in[core]
            for kk in ('xin', 'cT', 'w_ada', 'bT', 'ffn_wg', 'ffn_wu', 'ffn_wd'):
                m[kk] = np.zeros((1,), np.float32)
        in_maps.append(m)
    return in_maps


def kernel(**inputs):
    nc, st = build()
    in_maps = _prep_inputs(inputs)
    res = run_bass_kernel_spmd(nc, in_maps, core_ids=list(range(8)))
    outp = np.zeros((2, SEQ, D), np.float32)
    for core in range(8):
        b, s = core // 4, core % 4
        outp[b, s * 1024:(s + 1) * 1024] = res.results[core]["out"]
    return outp
```

```python
import os
import contextlib
import numpy as np
import ml_dtypes
import concourse.bass as bass
import concourse.mybir as mybir
from concourse.bass_utils import run_bass_kernel_spmd

F32 = mybir.dt.float32
BF16 = mybir.dt.bfloat16
AF = mybir.ActivationFunctionType
ALU = mybir.AluOpType

D = 2048
KC = 16
DFF = 5632
FC = 44
NT = 1088
NL = 1024
GROUPS = [(0, 512), (512, 512), (1024, 64)]
EPS = 1e-6
SEQ = 4096
CTX = 256
NS = SEQ + CTX


class Prog:
    def __init__(self):
        self.ops = []
        self.lastw = {}
        self.readers = {}

    def op(self, eng, fn, r=(), w=(), dma=False, cc=False):
        i = len(self.ops)
        deps = set()
        for k in r:
            if k in self.lastw:
                deps.add(self.lastw[k])
        for k in w:
            if k in self.lastw:
                deps.add(self.lastw[k])
            deps.update(self.readers.get(k, ()))
        for k in r:
            self.readers.setdefault(k, []).append(i)
        for k in w:
            self.lastw[k] = i
            self.readers[k] = []
        self.ops.append(dict(eng=eng, fn=fn, deps=deps, dma=dma, cc=cc, barrier=False))
        return i

    def barrier(self):
        self.ops.append(dict(barrier=True))
        self.lastw = {}
        self.readers = {}

    def emit(self, nc, st):
        ops = self.ops
        engs = ('pe', 'act', 'dve', 'pool', 'sp')
        needed = set()
        for o in ops:
            if not o['barrier']:
                needed |= o['deps']
        last = {}
        for i, o in enumerate(ops):
            if o['barrier']:
                for e in engs:
                    if e in last:
                        needed.add(last[e])
                last = {}
            elif not o['dma'] and not o['cc']:
                last[o['eng']] = i
        esem = {e: st.enter_context(nc.semaphore("es_" + e)) for e in engs}
        NQ = 20
        dpool = {q: [st.enter_context(nc.semaphore("dq_%s_%d" % (q, j))) for j in range(NQ)]
                 for q in ('sp', 'pool', 'act')}
        dcnt = {}
        rr = {q: 0 for q in dpool}
        tick = {e: 0 for e in engs}
        ccs = []
        for i, o in enumerate(ops):
            if o['barrier']:
                o['ticks'] = dict(tick)
                o['dmas'] = dict(dcnt)
                o['ccs'] = list(ccs)
            elif o['cc']:
                s = st.enter_context(nc.semaphore("cc_%d" % i))
                o['sem'] = s
                o['val'] = 1
                o['prev'] = 0
                ccs.append(s)
            elif o['dma']:
                q = o['eng']
                s = dpool[q][rr[q] % NQ]
                rr[q] += 1
                o['prev'] = dcnt.get(s, 0)
                dcnt[s] = o['prev'] + 16
                o['sem'] = s
                o['val'] = dcnt[s]
            elif i in needed:
                tick[o['eng']] += 1
                o['tick'] = tick[o['eng']]

        def run(E, e):
            waited = {}

            def w(sem, val):
                key = id(sem)
                if waited.get(key, 0) < val:
                    e.wait_ge(sem, val)
                    waited[key] = val

            for i, o in enumerate(ops):
                if o['barrier']:
                    for F in engs:
                        if o['ticks'][F] > 0:
                            w(esem[F], o['ticks'][F])
                    for s, c in o['dmas'].items():
                        w(s, c)
                    for s in o['ccs']:
                        w(s, 1)
                    continue
                if o['eng'] != E:
                    continue
                for d in sorted(o['deps']):
                    od = ops[d]
                    if od['dma'] or od['cc']:
                        w(od['sem'], od['val'])
                    else:
                        if od['eng'] == E and E == 'pe':
                            continue
                        w(esem[od['eng']], od['tick'])
                if (o['dma']) and o['prev'] > 0:
                    w(o['sem'], o['prev'])
                ins = o['fn'](e)
                if o['cc']:
                    ins.then_inc(o['sem'])
                elif o['dma']:
                    ins.then_inc(o['sem'], 16)
                elif 'tick' in o:
                    ins.then_inc(esem[E], 1)

        with nc.Block() as block:
            @block.tensor
            def _(e):
                run('pe', e)

            @block.scalar
            def _(e):
                run('act', e)

            @block.vector
            def _(e):
                run('dve', e)

            @block.gpsimd
            def _(e):
                run('pool', e)

            @block.sync
            def _(e):
                run('sp', e)


class Arena:
    def __init__(self, t, nwords):
        self.t = t
        self.n = nwords
        self.off = 0

    def reset(self, off=0):
        self.off = off

    def f32(self, n, parts=128):
        a = self.t[0:parts, self.off:self.off + n]
        self.off += n
        assert self.off <= self.n, ("arena overflow", self.off)
        return a

    def bf16(self, n, parts=128):
        words = (n + 1) // 2
        a = self.t[0:parts, self.off:self.off + words].bitcast(BF16)
        self.off += words
        assert self.off <= self.n, ("arena overflow", self.off)
        return a


def build(mode='full', debug=(), ncores=8):
    nc = bass.Bass("TRN2", target_bir_lowering=False)
    P = Prog()
    st = contextlib.ExitStack()

    def din(name, shape, dt=F32):
        return nc.dram_tensor(name, list(shape), dt, kind="ExternalInput")

    def dint(name, shape, dt=F32):
        return nc.dram_tensor(name, list(shape), dt)

    full = (mode == 'full')
    xin = din("xin", [NT, D] if full else [1])
    cT = din("cT", [128, KC, 2] if full else [1])
    w_ada = din("w_ada", [D, 4608] if full else [1])
    bT = din("bT", [128, 36] if full else [1])
    gT = din("gT", [128, 6, KC])
    ffn_wg = din("ffn_wg", [2, D, DFF] if full else [1])
    ffn_wu = din("ffn_wu", [2, D, DFF] if full else [1])
    ffn_wd = din("ffn_wd", [2, DFF, D] if full else [1])
    identf = din("identf", [128, 128])
    groups = [[0, 1, 2, 3], [4, 5, 6, 7]] if ncores == 8 else [[0, 1, 2, 3]]
    w_inh = din("w_inh", [D, 2336])
    w_m = din("w_m", [D, 4096])
    gwg = din("gwg", [16, 2, 256])
    gbgT = din("gbgT", [128, 4])
    gng = din("gng", [1, 512])
    ropeC = din("ropeC", [2, 128, SEQ])
    ropeS = din("ropeS", [2, 128, SEQ])
    rotT_in = din("rotT", [128, 128])
    cmask_in = din("cmask", [128, 512])
    trm_in = din("trm", [64, 128])
    tb_in = din("tb", [4, 64, 15 * 64])
    w_gla_o = din("w_gla_o", [D, D])
    w_na_o = din("w_na_o", [1024, D])
    w_out = din("w_out", [D, D])
    identb_in = din("identb", [128, 128], BF16)
    hm_in = din("hm_in", [D, NT], BF16) if mode != 'full' else None
    hmT_d = dint("hmT_d", [D, NT], BF16)
    modp_d = dint("modp_d", [128, 72])
    modall_d = dint("modall_d", [512, 72])
    hmT_all = dint("hmT_all", [8 * 1024, NT], BF16)
    qT_d = dint("qT_d", [256, NS])
    kT_d = dint("kT_d", [256, NS])
    gT_d = dint("gT_d", [32, NS])
    v_d = dint("v_d", [NS, 512], BF16)
    sr_d = dint("sr_d", [NS, 512])
    nv_d = dint("nv_d", [NS, 256], BF16)
    nqT_d = dint("nqT_d", [256, NS], BF16)
    nkT_d = dint("nkT_d", [256, NS], BF16)
    of_d = dint("of_d", [SEQ, 512])
    mixT_d = dint("mixT_d", [4 * 768, 1024], BF16)
    mixin_d = dint("mixin_d", [8 * 1536, 1024], BF16)
    sel_in = din("sel", [128, 4])
    out = nc.dram_tensor("out", [NL, D], F32, kind="ExternalOutput")

    hT_d = dint("hT_d", [KC, 128, NT])
    yT_d = dint("yT_d", [KC, 128, NT])
    dbg = {}
    for name, shape, dt in debug:
        dbg[name] = nc.dram_tensor("dbg_" + name, list(shape), dt, kind="ExternalOutput")

    AW = 50400
    arena_t = st.enter_context(nc.sbuf_tensor("arena", [128, AW], F32))
    A = Arena(arena_t, AW)
    cst = st.enter_context(nc.sbuf_tensor("cst", [128, 768], F32))
    ident = cst[:, 0:128]
    modT = cst[:, 128:128 + 288].rearrange("p (r j) -> p r j", r=2)
    Amod = cst[:, 416:416 + 96].rearrange("p (r s k) -> p r s k", r=2, s=3)
    Gmod = cst[:, 512:512 + 96].rearrange("p (r s k) -> p r s k", r=2, s=3)
    gTs = cst[:, 608:608 + 96].rearrange("p (s k) -> p s k", s=6)
    cst_b = st.enter_context(nc.sbuf_tensor("cst_b", [128, 256], BF16))
    ones_bf = cst_b[:, 0:128]
    identb = cst_b[:, 128:256]
    cst2 = st.enter_context(nc.sbuf_tensor("cst2", [128, 1800], F32))
    rotT = cst2[:, 0:128]
    cmask = cst2[:, 128:640]
    trm = cst2[0:64, 640:768].rearrange("p (a b) -> p a b", a=2)
    nbg = cst2[:, 768:772]
    one1 = cst2[:, 772:773]
    wgT = cst2[0:16, 776:776 + 512].rearrange("p (a b) -> p a b", a=2)
    gnb = cst2[0:64, 1288:1288 + 512]
    ps = [st.enter_context(nc.psum_tensor("ps%d" % i, [128, 512], F32)) for i in range(8)]
    PSK = [("ps", i) for i in range(8)]

    P.op('sp', lambda e: e.dma_start(out=ident, in_=identf[:, :]), w=["ident"], dma=True)
    P.op('sp', lambda e: e.dma_start(out=gTs, in_=gT[:, :, :]), w=["gTs"], dma=True)
    P.op('pool', lambda e: e.memset(ones_bf, 1.0), w=["ones"])
    epsb = cst[:, 704:705]
    P.op('pool', lambda e: e.memset(epsb, EPS), w=["epsb"])
    P.op('sp', lambda e: e.dma_start(out=identb, in_=identb_in[:, :]), w=["identb"], dma=True)
    P.op('sp', lambda e: e.dma_start(out=rotT, in_=rotT_in[:, :]), w=["rotT"], dma=True)
    P.op('sp', lambda e: e.dma_start(out=cmask, in_=cmask_in[:, :]), w=["cmask"], dma=True)
    P.op('sp', lambda e: e.dma_start(out=cst2[0:64, 640:768], in_=trm_in[:, :]), w=["trm"], dma=True)
    P.op('sp', lambda e: e.dma_start(out=nbg, in_=gbgT[:, :]), w=["nbg"], dma=True)
    P.op('dve', lambda e: e.tensor_scalar(out=nbg, in0=nbg, scalar1=-1.0, scalar2=None, op0=ALU.mult),
         r=["nbg"], w=["nbg"])
    P.op('pool', lambda e: e.memset(one1, 1.0), w=["one1"])
    P.op('sp', lambda e: e.dma_start(out=cst2[0:16, 776:776 + 512], in_=gwg.ap().rearrange("k a b -> k (a b)")),
         w=["wgT"], dma=True)
    P.op('sp', lambda e: e.dma_start(out=gnb, in_=gng.ap().broadcast_to([64, 512])), w=["gnb"], dma=True)
    if mode != 'full':
        P.barrier()
    if mode == 'full':
        A.reset()
        cTs = A.f32(32).rearrange("p (k r) -> p k r", r=2)
        sTs = A.f32(32).rearrange("p (k r) -> p k r", r=2)
        bTs = A.f32(36)
        modp = A.f32(72).rearrange("p (r j) -> p r j", r=2)
        P.op('sp', lambda e: e.dma_start(out=cTs, in_=cT[:, :, :]), w=["cTs"], dma=True)
        P.op('sp', lambda e: e.dma_start(out=bTs, in_=bT[:, :]), w=["bTs"], dma=True)
        P.op('act', lambda e: e.activation(out=sTs, in_=cTs, func=AF.Silu), r=["cTs"], w=["sTs"])
        wv = w_ada.ap().rearrange("(k p) c -> p k c", p=128)
        wst = [A.f32(KC * 512).rearrange("p (k c) -> p k c", k=KC) for _ in range(3)]
        modps = ps[0][:, 0:72].rearrange("p (j r) -> p j r", r=2)
        for cg in range(9):
            wb = wst[cg % 3]
            key = ("wst", cg % 3)
            P.op('sp', lambda e, wb=wb, cg=cg: e.dma_start(out=wb, in_=wv[:, :, cg * 512:(cg + 1) * 512]),
                 w=[key], dma=True)
            for jj in range(4):
                j = cg * 4 + jj
                for k in range(KC):
                    P.op('pe', lambda e, wb=wb, jj=jj, j=j, k=k: e.matmul(
                        modps[:, j, :], lhsT=wb[:, k, jj * 128:(jj + 1) * 128], rhs=sTs[:, k, :],
                        start=(k == 0), stop=(k == KC - 1)), r=[key, "sTs"], w=[PSK[0]])
        for r in range(2):
            P.op('dve', lambda e, r=r: e.tensor_tensor(out=modp[:, r, :], in0=modps[:, :, r], in1=bTs,
                                                        op=ALU.add), r=[PSK[0], "bTs"], w=["modp"])
        P.op('sp', lambda e: e.dma_start(out=modp_d[:, :], in_=modp.rearrange("p r j -> p (r j)")),
             r=["modp"], w=["modp_d"], dma=True)
        P.op('pool', lambda e: e.collective_compute("AllGather", ALU.bypass, replica_groups=groups,
                                                    ins=[modp_d.ap().opt()], outs=[modall_d.ap().opt()]),
             r=["modp_d"], w=["modall_d"], cc=True)
        for q in range(4):
            P.op('sp', lambda e, q=q: e.dma_start(
                out=modT[:, :, q * 36:(q + 1) * 36],
                in_=modall_d[q * 128:(q + 1) * 128, :].rearrange("p (r j) -> p r j", r=2)),
                r=["modall_d"], w=["modT"], dma=True)
        WRES = (0.5, 1.0, 0.5)
        for r in range(2):
            for s in range(3):
                P.op('dve', lambda e, r=r, s=s: e.scalar_tensor_tensor(
                    out=Amod[:, r, s, :], in0=modT[:, r, (3 * s + 1) * 16:(3 * s + 2) * 16], scalar=1.0,
                    in1=gTs[:, 2 * s, :], op0=ALU.add, op1=ALU.mult), r=["modT", "gTs"], w=["Amod"])
                P.op('dve', lambda e, r=r, s=s: e.scalar_tensor_tensor(
                    out=Gmod[:, r, s, :], in0=modT[:, r, (3 * s + 2) * 16:(3 * s + 3) * 16], scalar=WRES[s],
                    in1=gTs[:, 2 * s + 1, :], op0=ALU.mult, op1=ALU.mult), r=["modT", "gTs"], w=["Gmod"])
        P.barrier()

    def Bmod(r, s, k):
        return modT[:, r, (3 * s) * 16 + k:(3 * s) * 16 + k + 1]

    def prenorm_modulate(HT, s, hm, sqkey="hm"):
        rstd = A.f32(NT)
        tmp = [A.f32(NT) for _ in range(2)]
        for k in range(KC):
            P.op('act', lambda e, k=k: e.activation(out=hm[:, k, :], in_=HT[:, k, :], func=AF.Square),
                 r=["HT"], w=["hm"])
        for gi, (c0, n) in enumerate(GROUPS):
            for k in range(KC):
                P.op('pe', lambda e, k=k, c0=c0, n=n, gi=gi: e.matmul(
                    ps[gi][:, 0:n], lhsT=ones_bf, rhs=hm[:, k, c0:c0 + n], start=(k == 0), stop=(k == KC - 1)),
                    r=["hm", "ones"], w=[PSK[gi]])
            P.op('act', lambda e, c0=c0, n=n, gi=gi: e.activation(
                out=rstd[:, c0:c0 + n], in_=ps[gi][:, 0:n], func=AF.Sqrt, bias=epsb, scale=1.0 / D),
                r=[PSK[gi], "epsb"], w=["rstd"])
        P.op('dve', lambda e: e.reciprocal(out=rstd, in_=rstd), r=["rstd"], w=["rstd"])
        for k in range(KC):
            t = tmp[k % 2]
            tk = ("tmpn", k % 2)
            P.op('dve', lambda e, k=k, t=t: e.tensor_tensor(out=t, in0=HT[:, k, :], in1=rstd, op=ALU.mult),
                 r=["HT", "rstd"], w=[tk])
            for r, (c0, n) in enumerate([(0, NL), (NL, NT - NL)]):
                P.op('act', lambda e, k=k, t=t, r=r, c0=c0, n=n: e.activation(
                    out=hm[:, k, c0:c0 + n], in_=t[:, c0:c0 + n], func=AF.Identity,
                    bias=Bmod(r, s, k), scale=Amod[:, r, s, k:k + 1]),
                    r=[tk, "Amod", "modT"], w=["hm"])

    HM_W = KC * NT // 2
    ACT_W = FC * NT // 2

    def ffn_core(s, li):
        wg = ffn_wg.ap()[li].rearrange("(k p) f -> p k f", p=128)
        wu = ffn_wu.ap()[li].rearrange("(k p) f -> p k f", p=128)
        wd = ffn_wd.ap()[li].rearrange("(f p) d -> p f d", p=128)
        A.reset(0)
        hm = A.bf16(KC * NT).rearrange("p (k t) -> p k t", k=KC)
        actT = A.bf16(FC * NT).rearrange("p (f t) -> p f t", f=FC)
        mark = A.off
        wsf = [A.f32(KC * 256).rearrange("p (k c) -> p k c", k=KC) for _ in range(2)]
        wsb = [A.bf16(KC * 256).rearrange("p (k c) -> p k c", k=KC) for _ in range(4)]
        sg = [A.f32(512) for _ in range(2)]
        slot = 0
        nsg = 0
        nld = 0
        for fg in range(FC // 2):
            wb = []
            for gu, wsrc in enumerate((wg, wu)):
                i2 = nld % 2
                i = nld % 4
                nld += 1
                P.op('sp', lambda e, i2=i2, wsrc=wsrc, fg=fg: e.dma_start(
                    out=wsf[i2], in_=wsrc[:, :, fg * 256:(fg + 1) * 256]), w=[("wsf", i2)], dma=True)
                for hk in range(2):
                    P.op('pool', lambda e, i=i, i2=i2, hk=hk: e.tensor_copy(
                        out=wsb[i][:, hk * 8:(hk + 1) * 8, :], in_=wsf[i2][:, hk * 8:(hk + 1) * 8, :]),
                        r=[("wsf", i2)], w=[("wsb", i)])
                wb.append(i)
            for f2 in range(2):
                f = fg * 2 + f2
                for (c0, n) in GROUPS:
                    pg, pu = 2 + 2 * (slot % 3), 3 + 2 * (slot % 3)
                    slot += 1
                    for gu, pb in enumerate((pg, pu)):
                        for k in range(KC):
                            P.op('pe', lambda e, pb=pb, k=k, c0=c0, n=n, i=wb[gu], f2=f2: e.matmul(
                                ps[pb][:, 0:n], lhsT=wsb[i][:, k, f2 * 128:(f2 + 1) * 128], rhs=hm[:, k, c0:c0 + n],
                                start=(k == 0), stop=(k == KC - 1)),
                                r=[("wsb", wb[gu]), "hm"], w=[PSK[pb]])
                    sgi = nsg % 2
                    nsg += 1
                    P.op('act', lambda e, pg=pg, n=n, sgi=sgi: e.activation(out=sg[sgi][:, 0:n], in_=ps[pg][:, 0:n],
                                                                            func=AF.Silu),
                         r=[PSK[pg]], w=[("sg", sgi)])
                    P.op('dve', lambda e, pu=pu, n=n, sgi=sgi, f=f, c0=c0: e.tensor_tensor(
                        out=actT[:, f, c0:c0 + n], in0=sg[sgi][:, 0:n], in1=ps[pu][:, 0:n], op=ALU.mult),
                        r=[("sg", sgi), PSK[pu]], w=["act"])
        P.barrier()
        A.reset(0)
        ysb = [A.f32(NT) for _ in range(2)]
        hsb = [A.f32(NT) for _ in range(2)]
        ysq = [A.bf16(NT) for _ in range(2)]
        rstd = A.f32(NT)
        assert A.off <= HM_W
        A.reset(mark)
        wdf = [A.f32(11 * 256).rearrange("p (f c) -> p f c", f=11) for _ in range(2)]
        wdb = [A.bf16(FC * 256).rearrange("p (f c) -> p f c", f=FC) for _ in range(2)]
        slot = 0
        nst = 0
        for dcg in range(KC // 2):
            t = dcg % 2
            for qtr in range(4):
                qi = nst % 2
                nst += 1
                P.op('sp', lambda e, qi=qi, dcg=dcg, qtr=qtr: e.dma_start(
                    out=wdf[qi], in_=wd[:, qtr * 11:(qtr + 1) * 11, dcg * 256:(dcg + 1) * 256]),
                    w=[("wdf", qi)], dma=True)
                P.op('pool', lambda e, qi=qi, t=t, qtr=qtr: e.tensor_copy(
                    out=wdb[t][:, qtr * 11:(qtr + 1) * 11, :], in_=wdf[qi]), r=[("wdf", qi)], w=[("wdb", t)])
            for dc2 in range(2):
                dc = dcg * 2 + dc2
                i = dc % 2
                for gi, (c0, n) in enumerate(GROUPS):
                    pb = 3 + (slot % 5)
                    slot += 1
                    for f in range(FC):
                        P.op('pe', lambda e, pb=pb, f=f, c0=c0, n=n, t=t, dc2=dc2: e.matmul(
                            ps[pb][:, 0:n], lhsT=wdb[t][:, f, dc2 * 128:(dc2 + 1) * 128], rhs=actT[:, f, c0:c0 + n],
                            start=(f == 0), stop=(f == FC - 1)),
                            r=[("wdb", t), "act"], w=[PSK[pb]])
                    P.op('act', lambda e, pb=pb, c0=c0, n=n, i=i: e.activation(
                        out=ysb[i][:, c0:c0 + n], in_=ps[pb][:, 0:n], func=AF.Copy), r=[PSK[pb]], w=[("ysb", i)])
                    P.op('act', lambda e, pb=pb, c0=c0, n=n, i=i: e.activation(
                        out=ysq[i][:, c0:c0 + n], in_=ps[pb][:, 0:n], func=AF.Square), r=[PSK[pb]], w=[("ysq", i)])
                for gi, (c0, n) in enumerate(GROUPS):
                    P.op('pe', lambda e, gi=gi, c0=c0, n=n, i=i, dc=dc: e.matmul(
                        ps[gi][:, 0:n], lhsT=ones_bf, rhs=ysq[i][:, c0:c0 + n], start=(dc == 0), stop=(dc == KC - 1)),
                        r=[("ysq", i), "ones"], w=[PSK[gi]])
                P.op('act', lambda e, i=i, dc=dc: e.dma_start(out=yT_d[dc, :, :], in_=ysb[i]),
                     r=[("ysb", i)], w=[("yT_d", dc)], dma=True)
        for gi, (c0, n) in enumerate(GROUPS):
            P.op('act', lambda e, c0=c0, n=n, gi=gi: e.activation(
                out=rstd[:, c0:c0 + n], in_=ps[gi][:, 0:n], func=AF.Sqrt, bias=epsb, scale=1.0 / D),
                r=[PSK[gi], "epsb"], w=["rstd2"])
        P.op('dve', lambda e: e.reciprocal(out=rstd, in_=rstd), r=["rstd2"], w=["rstd2"])
        residual_update(s, rstd, ysb, hsb)

    def residual_update(s, rstd, ysb, hsb, ncols=NT):
        ranges = [(0, 0, NL)] + ([(1, NL, NT - NL)] if ncols == NT else [])
        for dc in range(KC):
            i = dc % 2
            yb, hb = ysb[i], hsb[i]
            P.op('sp', lambda e, yb=yb, dc=dc: e.dma_start(out=yb[:, 0:ncols], in_=yT_d[dc, :, 0:ncols]),
                 r=[("yT_d", dc)], w=[("ysb", i)], dma=True)
            P.op('sp', lambda e, hb=hb, dc=dc: e.dma_start(out=hb[:, 0:ncols], in_=hT_d[dc, :, 0:ncols]),
                 r=[("hT_d", dc)], w=[("hsb", i)], dma=True)
            P.op('dve', lambda e, yb=yb: e.tensor_tensor(out=yb[:, 0:ncols], in0=yb[:, 0:ncols], in1=rstd[:, 0:ncols],
                                                         op=ALU.mult),
                 r=[("ysb", i), "rstd2"], w=[("ysb", i)])
            for (r, c0, n) in ranges:
                P.op('dve', lambda e, yb=yb, hb=hb, dc=dc, r=r, c0=c0, n=n: e.scalar_tensor_tensor(
                    out=hb[:, c0:c0 + n], in0=yb[:, c0:c0 + n], scalar=Gmod[:, r, s, dc:dc + 1],
                    in1=hb[:, c0:c0 + n], op0=ALU.mult, op1=ALU.add),
                    r=[("ysb", i), "Gmod", ("hsb", i)], w=[("hsb", i)])
            P.op('pool', lambda e, hb=hb, dc=dc: e.dma_start(out=hT_d[dc, :, 0:ncols], in_=hb[:, 0:ncols]),
                 r=[("hsb", i)], w=[("hT_d", dc)], dma=True)
        P.barrier()

    def prenorm_stage(s, from_x):
        A.reset(0)
        hm = A.bf16(KC * NT).rearrange("p (k t) -> p k t", k=KC)
        HT = A.f32(KC * NT).rearrange("p (k t) -> p k t", k=KC)
        if from_x:
            xt = [A.f32(D) for _ in range(2)]
            tiles = [(i * 128, 128) for i in range(8)] + [(NL, 64)]
            slot = 0
            for ti, (t0, pn) in enumerate(tiles):
                xb = xt[ti % 2]
                P.op('sp', lambda e, xb=xb, t0=t0, pn=pn: e.dma_start(out=xb[0:pn, :], in_=xin[t0:t0 + pn, :]),
                     w=[("xt", ti % 2)], dma=True)
                for kq in range(4):
                    pb = 4 + (slot % 4)
                    slot += 1
                    for kk in range(4):
                        k = kq * 4 + kk
                        P.op('pe', lambda e, xb=xb, pn=pn, pb=pb, kk=kk, k=k: e.transpose(
                            ps[pb][:, kk * 128:kk * 128 + pn], xb[0:pn, k * 128:(k + 1) * 128], ident[0:pn, 0:pn]),
                            r=[("xt", ti % 2), "ident"], w=[PSK[pb]])
                    src = ps[pb][:, :].rearrange("p (a b) -> p a b", a=4)[:, :, 0:pn]
                    dst = HT[:, kq * 4:(kq + 1) * 4, t0:t0 + pn]
                    if kq % 2 == 0:
                        P.op('act', lambda e, src=src, dst=dst: e.activation(out=dst, in_=src, func=AF.Copy),
                             r=[PSK[pb]], w=["HT"])
                    else:
                        P.op('dve', lambda e, src=src, dst=dst: e.tensor_copy(out=dst, in_=src),
                             r=[PSK[pb]], w=["HT"])
            for dc in range(KC):
                P.op('pool', lambda e, dc=dc: e.dma_start(out=hT_d[dc, :, :], in_=HT[:, dc, :]),
                     r=["HT"], w=[("hT_d", dc)], dma=True)
        else:
            for dc in range(KC):
                P.op('sp', lambda e, dc=dc: e.dma_start(out=HT[:, dc, :], in_=hT_d[dc, :, :]),
                     r=[("hT_d", dc)], w=["HT"], dma=True)
        prenorm_modulate(HT, s, hm)
        P.barrier()
        return hm

    def tok_groups():
        gl = [(0, 256, [(r, 1024, 64, r * 64) for r in range(4)])]
        for g in range(1, 9):
            gl.append((256 + 512 * (g - 1), 512, [((g - 1) // 2, ((g - 1) % 2) * 512, 512, 0)]))
        return gl

    def mixer_inproj():
        A.reset(0)
        NW = 2336
        WB = A.bf16(KC * NW).rearrange("p (k c) -> p k c", k=KC)
        wstg = [A.f32(NW) for _ in range(2)]
        wv_ = w_inh.ap().rearrange("(k p) c -> p k c", p=128)
        for k in range(KC):
            P.op('sp', lambda e, k=k: e.dma_start(out=wstg[k % 2], in_=wv_[:, k, :]), w=[("wstg", k % 2)], dma=True)
            P.op('pool', lambda e, k=k: e.tensor_copy(out=WB[:, k, :], in_=wstg[k % 2]),
                 r=[("wstg", k % 2)], w=["WB"])
        X = [A.bf16(KC * 512).rearrange("p (k t) -> p k t", k=KC) for _ in range(2)]
        ev = [A.f32(512) for _ in range(4)]
        evb = [A.bf16(512) for _ in range(4)]
        cnt = dict(pb=0, ev=0, evb=0)

        def nxt(kind, mod):
            v = cnt[kind] % mod
            cnt[kind] += 1
            return v

        FM = [(0, 128, 'q', 0), (128, 128, 'q', 1), (256, 128, 'k', 0), (384, 128, 'k', 1), (512, 32, 'g', 0),
              (544, 128, 'nq', 0), (672, 128, 'nq', 1), (800, 128, 'nk', 0), (928, 128, 'nk', 1)]
        TM = [(1056, 512, 'v'), (1568, 512, 'r'), (2080, 256, 'nv')]
        for gi, (n0, n, srcs) in enumerate(tok_groups()):
            Xb = X[gi % 2]
            xk = ("X", gi % 2)
            for (r, c0, nn, x0) in srcs:
                for c in range(8):
                    P.op('sp', lambda e, Xb=Xb, r=r, c0=c0, nn=nn, x0=x0, c=c: e.dma_start(
                        out=Xb[:, 2 * c:2 * c + 2, x0:x0 + nn],
                        in_=hmT_all[c * 1024 + r * 256:c * 1024 + (r + 1) * 256, c0:c0 + nn].rearrange(
                            "(k p) t -> p k t", p=128)), w=[xk], dma=True)
            for (c0, M, kind, idx) in FM:
                if gi == 0 and kind in ('q', 'nq'):
                    continue
                pb = nxt('pb', 8)
                for k in range(KC):
                    P.op('pe', lambda e, pb=pb, M=M, n=n, k=k, c0=c0, Xb=Xb: e.matmul(
                        ps[pb][0:M, 0:n], lhsT=WB[:, k, c0:c0 + M], rhs=Xb[:, k, 0:n],
                        start=(k == 0), stop=(k == KC - 1)), r=["WB", xk], w=[PSK[pb]])
                if kind in ('q', 'k', 'g'):
                    j = nxt('ev', 4)
                    dst = {'q': qT_d, 'k': kT_d, 'g': gT_d}[kind]
                    P.op('act', lambda e, pb=pb, M=M, n=n, j=j: e.activation(out=ev[j][0:M, 0:n], in_=ps[pb][0:M, 0:n],
                                                                           func=AF.Copy), r=[PSK[pb]], w=[("ev", j)])
                    P.op('pool', lambda e, dst=dst, idx=idx, M=M, n=n, n0=n0, j=j: e.dma_start(
                        out=dst[idx * 128:idx * 128 + M, n0:n0 + n], in_=ev[j][0:M, 0:n]),
                        r=[("ev", j)], w=["qkg_d"], dma=True)
                else:
                    j = nxt('evb', 4)
                    dst = {'nq': nqT_d, 'nk': nkT_d}[kind]
                    sc = 0.125 if kind == 'nq' else 1.0
                    P.op('act', lambda e, pb=pb, n=n, j=j, sc=sc: e.activation(
                        out=evb[j][:, 0:n], in_=ps[pb][:, 0:n], func=AF.Copy, scale=sc), r=[PSK[pb]], w=[("evb", j)])
                    P.op('pool', lambda e, dst=dst, idx=idx, n=n, n0=n0, j=j: e.dma_start(
                        out=dst[idx * 128:(idx + 1) * 128, n0:n0 + n], in_=evb[j][:, 0:n]),
                        r=[("evb", j)], w=["qkg_d"], dma=True)
            for tt in range(n // 128):
                for (c0, ncol, kind) in TM:
                    pb = nxt('pb', 8)
                    for k in range(KC):
                        P.op('pe', lambda e, pb=pb, ncol=ncol, k=k, c0=c0, Xb=Xb, tt=tt: e.matmul(
                            ps[pb][:, 0:ncol], lhsT=Xb[:, k, tt * 128:(tt + 1) * 128], rhs=WB[:, k, c0:c0 + ncol],
                            start=(k == 0), stop=(k == KC - 1)), r=["WB", xk], w=[PSK[pb]])
                    r0 = n0 + tt * 128
                    if kind == 'r':
                        j = nxt('ev', 4)
                        P.op('act', lambda e, pb=pb, j=j: e.activation(out=ev[j], in_=ps[pb][:, 0:512], func=AF.Silu),
                             r=[PSK[pb]], w=[("ev", j)])
                        P.op('pool', lambda e, r0=r0, j=j: e.dma_start(out=sr_d[r0:r0 + 128, :], in_=ev[j]),
                             r=[("ev", j)], w=["qkg_d"], dma=True)
                    else:
                        j = nxt('evb', 4)
                        dst = v_d if kind == 'v' else nv_d
                        P.op('dve', lambda e, pb=pb, j=j, ncol=ncol: e.tensor_copy(out=evb[j][:, 0:ncol],
                                                                                  in_=ps[pb][:, 0:ncol]),
                             r=[PSK[pb]], w=[("evb", j)])
                        P.op('pool', lambda e, dst=dst, r0=r0, j=j, ncol=ncol: e.dma_start(
                            out=dst[r0:r0 + 128, :], in_=evb[j][:, 0:ncol]), r=[("evb", j)], w=["qkg_d"], dma=True)
        P.barrier()

    def gla():
        S = A.f32(1024).rearrange("p (c v) -> p c v", c=2)
        Sb = A.bf16(1024).rearrange("p (c v) -> p c v", c=2)
        f3 = lambda: A.f32(1024).rearrange("p (c t) -> p c t", c=2)
        b3 = lambda: A.bf16(1024).rearrange("p (c t) -> p c t", c=2)
        qT, kT, Ct, St = f3(), f3(), f3(), f3()
        e_, sp, cum, d3, E1, E2, E3, t1, qr, kr = [f3() for _ in range(10)]
        qd, ki, ke = b3(), b3(), b3()
        gts = A.f32(512, parts=16)
        kend = A.bf16(8 * 256, parts=64).rearrange("p (c d) -> p c d", c=8)
        vb = A.bf16(8 * 512, parts=64).rearrange("p (c v) -> p c v", c=8)
        dch = A.f32(16).rearrange("p (c n) -> p c n", c=2)
        att_sb = [A.bf16(64, parts=64) for _ in range(2)]
        o_sb = [A.f32(512, parts=64) for _ in range(2)]
        of_g = A.f32(8 * 512, parts=64).rearrange("p (c v) -> p c v", c=8)
        sr_g = A.f32(8 * 512, parts=64).rearrange("p (c v) -> p c v", c=8)
        gcT = A.bf16(4 * 512).rearrange("p (a t) -> p a t", a=4)
        res = [A.bf16(512, parts=64) for _ in range(2)]
        junk = A.f32(512, parts=64)
        ssq = [A.f32(1, parts=64) for _ in range(2)]
        glist = tok_groups()
        ci = 0
        yield
        for dirn in (0, 1):
            P.op('pool', lambda e: e.memset(S, 0.0), w=["S"])
            P.op('pool', lambda e: e.memset(Sb, 0.0), w=["Sb"])
            order = [0] + (list(range(1, 9)) if dirn == 0 else list(range(8, 0, -1)))
            for g in order:
                n0, n, _ = glist[g]
                nch = n // 64
                lat = g > 0
                tok0 = n0 - 256
                P.op('sp', lambda e, n0=n0, n=n: e.dma_start(
                    out=kT[:, :, 0:n], in_=kT_d.ap().rearrange("(c p) t -> p c t", p=128)[:, :, n0:n0 + n]),
                    r=["qkg_d"], w=["kT"], dma=True)
                P.op('sp', lambda e, n0=n0, n=n, dirn=dirn: e.dma_start(
                    out=gts[:, 0:n], in_=gT_d[dirn * 16:(dirn + 1) * 16, n0:n0 + n]), r=["qkg_d"], w=["gts"], dma=True)
                P.op('sp', lambda e, n0=n0, n=n, nch=nch: e.dma_start(
                    out=vb[:, 0:nch, :], in_=v_d[n0:n0 + n, :].rearrange("(c p) v -> p c v", p=64)),
                    r=["qkg_d"], w=["vb"], dma=True)
                if lat:
                    P.op('sp', lambda e, n0=n0, n=n: e.dma_start(
                        out=qT[:, :, 0:n], in_=qT_d.ap().rearrange("(c p) t -> p c t", p=128)[:, :, n0:n0 + n]),
                        r=["qkg_d"], w=["qT"], dma=True)
                    P.op('sp', lambda e, tok0=tok0, n=n: e.dma_start(
                        out=Ct[:, :, 0:n], in_=ropeC.ap().rearrange("c p t -> p c t")[:, :, tok0:tok0 + n]),
                        w=["Ct"], dma=True)
                    P.op('sp', lambda e, tok0=tok0, n=n: e.dma_start(
                        out=St[:, :, 0:n], in_=ropeS.ap().rearrange("c p t -> p c t")[:, :, tok0:tok0 + n]),
                        w=["St"], dma=True)
                    if dirn == 1:
                        P.op('sp', lambda e, tok0=tok0, n=n: e.dma_start(
                            out=of_g, in_=of_d[tok0:tok0 + n, :].rearrange("(c p) v -> p c v", p=64)),
                            r=["of_d"], w=["of_g"], dma=True)
                        P.op('sp', lambda e, n0=n0, n=n: e.dma_start(
                            out=sr_g, in_=sr_d[n0:n0 + n, :].rearrange("(c p) v -> p c v", p=64)),
                            r=["qkg_d"], w=["sr_g"], dma=True)
                        for c in range(8):
                            P.op('dve', lambda e, c=c: e.tensor_tensor(out=sr_g[:, c, :], in0=sr_g[:, c, :], in1=gnb,
                                                                       op=ALU.mult), r=["sr_g", "gnb"], w=["sr_g"])
                for dc in range(2):
                    P.op('pe', lambda e, dc=dc, n=n, dirn=dirn: e.matmul(
                        ps[dc][:, 0:n], lhsT=wgT[:, dirn, dc * 128:(dc + 1) * 128], rhs=gts[:, 0:n],
                        start=True, stop=True), r=["gts", "wgT"], w=[PSK[dc]])
                    P.op('act', lambda e, dc=dc, n=n, dirn=dirn: e.activation(
                        out=e_[:, dc, 0:n], in_=ps[dc][:, 0:n], func=AF.Exp, scale=-1.0,
                        bias=nbg[:, dirn * 2 + dc:dirn * 2 + dc + 1]), r=[PSK[dc], "nbg"], w=["e_"])
                    P.op('act', lambda e, dc=dc, n=n: e.activation(
                        out=sp[:, dc, 0:n], in_=e_[:, dc, 0:n], func=AF.Ln, bias=one1, scale=1.0),
                        r=["e_", "one1"], w=["sp"])
                    P.op('dve', lambda e, dc=dc, n=n: e.tensor_tensor_scan(
                        out=cum[:, dc, 0:n], data0=cmask[:, 0:n], data1=sp[:, dc, 0:n], initial=0.0,
                        op0=ALU.mult, op1=ALU.add), r=["sp", "cmask"], w=["cum"])
                cum4 = cum[:, :, 0:n].rearrange("p c (h t) -> p c h t", t=64)
                tot = cum4[:, :, :, 63:64]
                for dc in range(2):
                    P.op('dve', lambda e, dc=dc, n=n, nch=nch: e.tensor_tensor(
                        out=d3[:, dc, 0:n].rearrange("p (h t) -> p h t", t=64),
                        in0=cum[:, dc, 0:n].rearrange("p (h t) -> p h t", t=64),
                        in1=cum[:, dc, 0:n].rearrange("p (h t) -> p h t", t=64)[:, :, 63:64].to_broadcast([128, nch, 64]),
                        op=ALU.subtract), r=["cum"], w=["d3"])
                    P.op('act', lambda e, dc=dc, n=n, nch=nch: e.activation(
                        out=dch[:, dc, 0:nch], in_=cum[:, dc, 0:n].rearrange("p (h t) -> p h t", t=64)[:, :, 63],
                        func=AF.Exp, scale=-1.0 / 16), r=["cum"], w=["dch"])
                if dirn == 0:
                    P.op('act', lambda e, n=n: e.activation(out=E1[:, :, 0:n], in_=cum[:, :, 0:n], func=AF.Exp,
                                                            scale=-1.0 / 16), r=["cum"], w=["E1"])
                    P.op('act', lambda e, n=n: e.activation(out=E2[:, :, 0:n], in_=cum[:, :, 0:n], func=AF.Exp,
                                                            scale=1.0 / 16), r=["cum"], w=["E2"])
                    P.op('act', lambda e, n=n: e.activation(out=E3[:, :, 0:n], in_=d3[:, :, 0:n], func=AF.Exp,
                                                            scale=1.0 / 16), r=["d3"], w=["E3"])
                else:
                    P.op('dve', lambda e, n=n: e.tensor_tensor(out=t1[:, :, 0:n], in0=sp[:, :, 0:n], in1=d3[:, :, 0:n],
                                                               op=ALU.subtract), r=["sp", "d3"], w=["t1"])
                    P.op('act', lambda e, n=n: e.activation(out=E1[:, :, 0:n], in_=t1[:, :, 0:n], func=AF.Exp,
                                                            scale=-1.0 / 16), r=["t1"], w=["E1"])
                    P.op('act', lambda e, n=n: e.activation(out=E2[:, :, 0:n], in_=t1[:, :, 0:n], func=AF.Exp,
                                                            scale=1.0 / 16), r=["t1"], w=["E2"])
                    P.op('dve', lambda e, n=n: e.tensor_tensor(out=d3[:, :, 0:n], in0=cum[:, :, 0:n], in1=sp[:, :, 0:n],
                                                               op=ALU.subtract), r=["sp", "cum", "E1"], w=["d3"])
                    P.op('act', lambda e, n=n: e.activation(out=E3[:, :, 0:n], in_=d3[:, :, 0:n], func=AF.Exp,
                                                            scale=-1.0 / 16), r=["d3"], w=["E3"])
                if lat:
                    for (src, dstr, nm) in ((qT, qr, "q"), (kT, kr, "k")):
                        for dc in range(2):
                            pb = dc
                            P.op('pe', lambda e, pb=pb, dc=dc, n=n, src=src: e.matmul(
                                ps[pb][:, 0:n], lhsT=rotT, rhs=src[:, dc, 0:n], start=True, stop=True),
                                r=[nm + "T", "rotT"], w=[PSK[pb]])
                            P.op('dve', lambda e, pb=pb, dc=dc, n=n: e.tensor_tensor(
                                out=t1[:, dc, 0:n], in0=ps[pb][:, 0:n], in1=St[:, dc, 0:n], op=ALU.mult),
                                r=[PSK[pb], "St", "E1", "E2"], w=["t1"])
                            P.op('pool', lambda e, dc=dc, n=n, src=src, dstr=dstr: e.tensor_tensor(
                                out=dstr[:, dc, 0:n], in0=src[:, dc, 0:n], in1=Ct[:, dc, 0:n], op=ALU.mult),
                                r=[nm + "T", "Ct"], w=[nm + "r"])
                            P.op('dve', lambda e, dc=dc, n=n, dstr=dstr: e.tensor_tensor(
                                out=dstr[:, dc, 0:n], in0=dstr[:, dc, 0:n], in1=t1[:, dc, 0:n], op=ALU.add),
                                r=[nm + "r", "t1"], w=[nm + "r"])
                    ksrc, kname = kr, "kr"
                    for dc in range(2):
                        P.op('dve', lambda e, dc=dc, n=n: e.scalar_tensor_tensor(
                            out=qd[:, dc, 0:n], in0=qr[:, dc, 0:n], scalar=0.0625, in1=E1[:, dc, 0:n],
                            op0=ALU.mult, op1=ALU.mult), r=["qr", "E1"], w=["qd"])
                        P.op('dve', lambda e, dc=dc, n=n: e.tensor_tensor(
                            out=ki[:, dc, 0:n], in0=kr[:, dc, 0:n], in1=E2[:, dc, 0:n], op=ALU.mult),
                            r=["kr", "E2"], w=["ki"])
                else:
                    ksrc, kname = kT, "kT"
                for dc in range(2):
                    P.op('pool', lambda e, dc=dc, n=n, ksrc=ksrc: e.tensor_tensor(
                        out=ke[:, dc, 0:n], in0=ksrc[:, dc, 0:n], in1=E3[:, dc, 0:n], op=ALU.mult),
                        r=[kname, "E3"], w=["ke"])
                for half in range((nch + 3) // 4):
                    pb = half
                    pv = ps[pb][0:64, :].bitcast(BF16).rearrange("p (c d) -> p c d", c=4)
                    for cc in range(4):
                        c = half * 4 + cc
                        for dc in range(2):
                            P.op('pe', lambda e, pv=pv, cc=cc, c=c, dc=dc: e.transpose(
                                pv[:, cc, dc * 128:(dc + 1) * 128], ke[:, dc, c * 64:(c + 1) * 64], identb),
                                r=["ke", "identb"], w=[PSK[pb]])
                    P.op('act', lambda e, pv=pv, half=half: e.activation(out=kend[:, half * 4:half * 4 + 4, :], in_=pv,
                                                                        func=AF.Copy), r=[PSK[pb]], w=["kend"])
                yield
                clist = list(range(nch)) if dirn == 0 else list(range(nch - 1, -1, -1))
                for c in clist:
                    yield
                    cs = slice(c * 64, (c + 1) * 64)
                    j = ci % 2
                    ci += 1
                    pa, po = 2, 3
                    if lat:
                        for dc in range(2):
                            P.op('pe', lambda e, pa=pa, dc=dc, cs=cs: e.matmul(
                                ps[pa][0:64, 0:64], lhsT=ki[:, dc, cs], rhs=qd[:, dc, cs],
                                start=(dc == 0), stop=(dc == 1)), r=["ki", "qd"], w=[PSK[pa]])
                        P.op('dve', lambda e, pa=pa, j=j, dirn=dirn: e.tensor_tensor(
                            out=att_sb[j], in0=ps[pa][0:64, 0:64], in1=trm[:, dirn, :], op=ALU.mult),
                            r=[PSK[pa], "trm"], w=[("att", j)])
                        P.op('pe', lambda e, po=po, j=j, c=c: e.matmul(
                            ps[po][0:64, 0:512], lhsT=att_sb[j], rhs=vb[:, c, :], start=True, stop=False),
                            r=[("att", j), "vb"], w=[PSK[po]])
                        for dc in range(2):
                            P.op('pe', lambda e, po=po, dc=dc, cs=cs: e.matmul(
                                ps[po][0:64, 0:512], lhsT=qd[:, dc, cs], rhs=Sb[:, dc, :],
                                start=False, stop=(dc == 1)), r=["qd", "Sb"], w=[PSK[po]])
                    for dc in range(2):
                        P.op('pe', lambda e, dc=dc, c=c: e.matmul(
                            ps[dc][:, 0:512], lhsT=kend[:, c, dc * 128:(dc + 1) * 128], rhs=vb[:, c, :],
                            start=True, stop=True), r=["kend", "vb"], w=[PSK[dc]])
                        P.op('dve', lambda e, dc=dc, c=c: e.scalar_tensor_tensor(
                            out=S[:, dc, :], in0=S[:, dc, :], scalar=dch[:, dc, c:c + 1], in1=ps[dc][:, 0:512],
                            op0=ALU.mult, op1=ALU.add), r=["S", "dch", PSK[dc]], w=["S"])
                        P.op('act', lambda e, dc=dc: e.activation(out=Sb[:, dc, :], in_=S[:, dc, :], func=AF.Copy),
                             r=["S"], w=["Sb"])
                    if lat and dirn == 0:
                        P.op('act', lambda e, po=po, j=j: e.activation(out=o_sb[j], in_=ps[po][0:64, 0:512],
                                                                      func=AF.Copy), r=[PSK[po]], w=[("o_sb", j)])
                        P.op('pool', lambda e, j=j, tok0=tok0, c=c: e.dma_start(
                            out=of_d[tok0 + c * 64:tok0 + (c + 1) * 64, :], in_=o_sb[j]),
                            r=[("o_sb", j)], w=["of_d"], dma=True)
                    elif lat:
                        P.op('dve', lambda e, po=po, j=j, c=c: e.tensor_tensor(
                            out=o_sb[j], in0=ps[po][0:64, 0:512], in1=of_g[:, c, :], op=ALU.add),
                            r=[PSK[po], "of_g"], w=[("o_sb", j)])
                        P.op('act', lambda e, j=j: e.activation(out=junk, in_=o_sb[j], func=AF.Square,
                                                                accum_out=ssq[j]), r=[("o_sb", j)], w=["junk", ("ssq", j)])
                        P.op('act', lambda e, j=j: e.activation(out=ssq[j], in_=ssq[j], func=AF.Sqrt, bias=epsb[0:64, :],
                                                                scale=1.0 / 512), r=[("ssq", j), "epsb"], w=[("ssq", j)])
                        P.op('dve', lambda e, j=j: e.reciprocal(out=ssq[j], in_=ssq[j]), r=[("ssq", j)], w=[("ssq", j)])
                        P.op('dve', lambda e, j=j, c=c: e.scalar_tensor_tensor(
                            out=res[j], in0=o_sb[j], scalar=ssq[j], in1=sr_g[:, c, :], op0=ALU.mult, op1=ALU.mult),
                            r=[("o_sb", j), ("ssq", j), "sr_g"], w=[("res", j)])
                        pv = ps[pa][:, 0:128].bitcast(BF16).rearrange("p (a t) -> p a t", a=4)
                        for a in range(4):
                            P.op('pe', lambda e, pv=pv, a=a, j=j: e.transpose(
                                pv[:, a, :], res[j][:, a * 128:(a + 1) * 128], identb[0:64, 0:64]),
                                r=[("res", j), "identb"], w=[PSK[pa]])
                        P.op('act', lambda e, pv=pv, cs=cs: e.activation(out=gcT[:, :, cs], in_=pv, func=AF.Copy),
                             r=[PSK[pa]], w=["gcT"])
                if lat and dirn == 1:
                    qt, c0 = tok0 // 1024, tok0 % 1024
                    P.op('pool', lambda e, qt=qt, c0=c0: e.dma_start(
                        out=mixT_d[qt * 768:qt * 768 + 512, c0:c0 + 512].rearrange("(a p) t -> p a t", p=128),
                        in_=gcT), r=["gcT"], w=["mixT_d"], dma=True)

    def na():
        nq = A.bf16(SEQ, parts=64)
        nk = A.bf16(NS, parts=64)
        Ve = A.bf16(34 * 64).rearrange("p (b d) -> p b d", b=34)
        Vo = A.bf16(31 * 64).rearrange("p (b d) -> p b d", b=31)
        TBh = A.f32(15 * 64, parts=64)
        naT = A.bf16(SEQ, parts=64)
        s_sb = [A.f32(512, parts=64) for _ in range(2)]
        Pm = [A.bf16(768, parts=64) for _ in range(2)]
        PT = [A.bf16(6 * 64).rearrange("p (b q) -> p b q", b=6) for _ in range(2)]
        osb = [A.bf16(64, parts=64) for _ in range(2)]
        sm = [A.f32(8, parts=64) for _ in range(2)]
        yield
        for hh in range(4):
            hs = slice(hh * 64, (hh + 1) * 64)
            P.op('sp', lambda e, hs=hs: e.dma_start(out=nq, in_=nqT_d[hs, 256:NS]), r=["qkg_d"], w=["nq"], dma=True)
            P.op('sp', lambda e, hs=hs: e.dma_start(out=nk, in_=nkT_d[hs, :]), r=["qkg_d"], w=["nk"], dma=True)
            P.op('sp', lambda e, hs=hs: e.dma_start(
                out=Ve, in_=nv_d[:, hs].rearrange("(b p) d -> p b d", p=128)), r=["qkg_d"], w=["Ve"], dma=True)
            P.op('sp', lambda e, hs=hs: e.dma_start(
                out=Vo, in_=nv_d[320:320 + 31 * 128, hs].rearrange("(b p) d -> p b d", p=128)),
                r=["qkg_d"], w=["Vo"], dma=True)
            P.op('sp', lambda e, hh=hh: e.dma_start(out=TBh, in_=tb_in[hh, :, :]), w=["TBh"], dma=True)
            for r in range(64):
                yield
                j = r % 2
                bl, bm = 4 + 2 * j, 5 + 2 * j
                KLk, KCk, KTk, KOk = ("nl", j), ("nc", j), ("nt", j), ("no", j)
                j0 = min(max(r - 4, 0), 56)
                s0 = j0 - r + 7
                qs = slice(r * 64, (r + 1) * 64)
                P.op('pe', lambda e, bl=bl, qs=qs, j0=j0: e.matmul(
                    ps[bl][0:64, 0:512], lhsT=nq[:, qs], rhs=nk[:, 256 + j0 * 64:256 + j0 * 64 + 512],
                    start=True, stop=True), r=["nq", "nk"], w=[KLk])
                P.op('pe', lambda e, bm=bm, qs=qs: e.matmul(
                    ps[bm][0:64, 0:256], lhsT=nq[:, qs], rhs=nk[:, 0:256], start=True, stop=True),
                    r=["nq", "nk"], w=[KCk])
                P.op('dve', lambda e, bl=bl, j=j, s0=s0: e.tensor_tensor(
                    out=s_sb[j], in0=ps[bl][0:64, 0:512], in1=TBh[:, s0 * 64:(s0 + 8) * 64], op=ALU.add),
                    r=[KLk, "TBh"], w=[("s_sb", j)])
                P.op('dve', lambda e, j=j: e.reduce_max(out=sm[j][:, 0:1], in_=s_sb[j], axis=mybir.AxisListType.X),
                     r=[("s_sb", j)], w=[("sm", j)])
                P.op('dve', lambda e, j=j, bm=bm: e.reduce_max(out=sm[j][:, 1:2], in_=ps[bm][0:64, 0:256],
                                                               axis=mybir.AxisListType.X),
                     r=[KCk], w=[("sm", j)])
                P.op('dve', lambda e, j=j: e.tensor_tensor(out=sm[j][:, 2:3], in0=sm[j][:, 0:1], in1=sm[j][:, 1:2],
                                                           op=ALU.max), r=[("sm", j)], w=[("sm", j)])
                P.op('dve', lambda e, j=j: e.tensor_scalar(out=sm[j][:, 3:4], in0=sm[j][:, 2:3], scalar1=-1.0,
                                                           scalar2=None, op0=ALU.mult), r=[("sm", j)], w=[("sm", j)])
                P.op('act', lambda e, j=j: e.activation(out=Pm[j][:, 0:512], in_=s_sb[j], func=AF.Exp,
                                                        bias=sm[j][:, 3:4], scale=1.0, accum_out=sm[j][:, 4:5]),
                     r=[("s_sb", j), ("sm", j)], w=[("Pm", j), ("sm2", j)])
                P.op('act', lambda e, j=j, bm=bm: e.activation(out=Pm[j][:, 512:768], in_=ps[bm][0:64, 0:256], func=AF.Exp,
                                                               bias=sm[j][:, 3:4], scale=1.0, accum_out=sm[j][:, 5:6]),
                     r=[KCk, ("sm", j)], w=[("Pm", j), ("sm2", j)])
                pv = ps[bm][:, 256:448].bitcast(BF16).rearrange("p (b q) -> p b q", b=6)
                for blk in range(6):
                    P.op('pe', lambda e, pv=pv, blk=blk, j=j: e.transpose(
                        pv[:, blk, :], Pm[j][:, blk * 128:(blk + 1) * 128], identb[0:64, 0:64]),
                        r=[("Pm", j), "identb"], w=[KTk])
                P.op('dve', lambda e, pv=pv, j=j: e.tensor_copy(out=PT[j], in_=pv), r=[KTk], w=[("PT", j)])
                for blk in range(6):
                    if blk < 4:
                        vsrc = Ve[:, 2 + j0 // 2 + blk, :] if j0 % 2 == 0 else Vo[:, (j0 - 1) // 2 + blk, :]
                    else:
                        vsrc = Ve[:, blk - 4, :]
                    P.op('pe', lambda e, bm=bm, blk=blk, j=j, vsrc=vsrc: e.matmul(
                        ps[bm][0:64, 448:512], lhsT=PT[j][:, blk, :], rhs=vsrc, start=(blk == 0), stop=(blk == 5)),
                        r=[("PT", j), "Ve", "Vo"], w=[KOk])
                P.op('dve', lambda e, j=j: e.tensor_tensor(out=sm[j][:, 6:7], in0=sm[j][:, 4:5], in1=sm[j][:, 5:6],
                                                           op=ALU.add), r=[("sm2", j)], w=[("sm3", j)])
                P.op('dve', lambda e, j=j: e.reciprocal(out=sm[j][:, 7:8], in_=sm[j][:, 6:7]),
                     r=[("sm3", j)], w=[("sm3", j)])
                P.op('dve', lambda e, j=j, bm=bm: e.tensor_scalar(out=osb[j], in0=ps[bm][0:64, 448:512],
                                                                  scalar1=sm[j][:, 7:8], scalar2=None, op0=ALU.mult),
                     r=[KOk, ("sm3", j)], w=[("osb", j)])
                pv2 = ps[bm][0:64, 256:288].bitcast(BF16)
                P.op('pe', lambda e, pv2=pv2, j=j: e.transpose(pv2, osb[j], identb[0:64, 0:64]),
                     r=[("osb", j), "identb"], w=[KTk])
                P.op('act', lambda e, pv2=pv2, qs=qs: e.activation(out=naT[:, qs], in_=pv2, func=AF.Copy),
                     r=[KTk], w=["naT"])
            for qt in range(4):
                P.op('pool', lambda e, qt=qt, hh=hh: e.dma_start(
                    out=mixT_d[qt * 768 + 512 + hh * 64:qt * 768 + 512 + (hh + 1) * 64, :],
                    in_=naT[:, qt * 1024:(qt + 1) * 1024]), r=["naT"], w=["mixT_d"], dma=True)

    def merge_out():
        A.reset(0)
        MX = A.bf16(24 * NL).rearrange("p (c t) -> p c t", c=24)
        hmo = A.bf16(KC * NL).rearrange("p (k t) -> p k t", k=KC)
        MG = A.bf16(KC * NL).rearrange("p (k t) -> p k t", k=KC)
        mark = A.off
        selS = A.f32(4)
        tmpx = [A.bf16(6 * NL) for _ in range(2)]
        P.op('sp', lambda e: e.dma_start(out=selS, in_=sel_in[:, :]), w=["selS"], dma=True)
        P.op('sp', lambda e: e.dma_start(
            out=hmo, in_=hmT_d.ap().rearrange("(k p) t -> p k t", p=128)[:, :, 0:NL]), w=["hmo"], dma=True)
        n = 0
        for r in range(4):
            dst = MX[:, r * 6:(r + 1) * 6, :].rearrange("p c t -> p (c t)")
            for q in range(4):
                tb_ = tmpx[n % 2]
                tk = ("tmpx", n % 2)
                n += 1
                for hf in range(2):
                    r0 = (2 * q + hf) * 1536 + r * 384
                    P.op('sp', lambda e, tb_=tb_, r0=r0, hf=hf: e.dma_start(
                        out=tb_.rearrange("p (c t) -> p c t", c=6)[:, 3 * hf:3 * hf + 3, :],
                        in_=mixin_d[r0:r0 + 384, :].rearrange("(c p) t -> p c t", p=128)), w=[tk], dma=True)
                if q == 0:
                    P.op('dve', lambda e, dst=dst, tb_=tb_, q=q: e.tensor_scalar(
                        out=dst, in0=tb_, scalar1=selS[:, q:q + 1], scalar2=None, op0=ALU.mult),
                        r=[tk, "selS"], w=["MX"])
                else:
                    P.op('dve', lambda e, dst=dst, tb_=tb_, q=q: e.scalar_tensor_tensor(
                        out=dst, in0=tb_, scalar=selS[:, q:q + 1], in1=dst, op0=ALU.mult, op1=ALU.add),
                        r=[tk, "selS", "MX"], w=["MX"])
        P.barrier()
        A.reset(mark)
        wsf = [A.f32(KC * 128).rearrange("p (k c) -> p k c", k=KC) for _ in range(4)]
        wsb = [A.bf16(KC * 128).rearrange("p (k c) -> p k c", k=KC) for _ in range(4)]
        sgm = [A.f32(512) for _ in range(4)]
        wgo = w_gla_o.ap().rearrange("(i p) d -> p i d", p=128)
        wno = w_na_o.ap().rearrange("(i p) d -> p i d", p=128)
        wmv = w_m.ap().rearrange("(k p) c -> p k c", p=128)
        it = 0
        for dc in range(KC):
            dsl = slice(dc * 128, (dc + 1) * 128)
            srcs = [(wgo[:, :, dsl], 16), (wno[:, :, dsl], 8), (wmv[:, :, dsl], 16),
                    (wmv[:, :, 2048 + dc * 128:2048 + (dc + 1) * 128], 16)]
            for wi, (src, ni) in enumerate(srcs):
                P.op('sp', lambda e, wi=wi, src=src, ni=ni: e.dma_start(out=wsf[wi][:, 0:ni, :], in_=src),
                     w=[("wsf", wi)], dma=True)
                P.op('pool', lambda e, wi=wi, ni=ni: e.tensor_copy(out=wsb[wi][:, 0:ni, :], in_=wsf[wi][:, 0:ni, :]),
                     r=[("wsf", wi)], w=[("wsb", wi)])
            for grp in range(2):
                gs = slice(grp * 512, (grp + 1) * 512)
                base = 4 * (it % 2)
                it += 1
                pa, pbk, p1, p2 = base, base + 1, base + 2, base + 3
                for i in range(16):
                    P.op('pe', lambda e, pa=pa, i=i, gs=gs: e.matmul(
                        ps[pa][:, :], lhsT=wsb[0][:, i, :], rhs=MX[:, (i // 4) * 6 + i % 4, gs],
                        start=(i == 0), stop=(i == 15)), r=[("wsb", 0), "MX"], w=[PSK[pa]])
                for i in range(8):
                    P.op('pe', lambda e, pbk=pbk, i=i, gs=gs: e.matmul(
                        ps[pbk][:, :], lhsT=wsb[1][:, i, :], rhs=MX[:, (i // 2) * 6 + 4 + i % 2, gs],
                        start=(i == 0), stop=(i == 7)), r=[("wsb", 1), "MX"], w=[PSK[pbk]])
                for (pm, wi) in ((p1, 2), (p2, 3)):
                    for k in range(KC):
                        P.op('pe', lambda e, pm=pm, wi=wi, k=k, gs=gs: e.matmul(
                            ps[pm][:, :], lhsT=wsb[wi][:, k, :], rhs=hmo[:, k, gs],
                            start=(k == 0), stop=(k == KC - 1)), r=[("wsb", wi), "hmo"], w=[PSK[pm]])
                P.op('act', lambda e, p1=p1: e.activation(out=sgm[0], in_=ps[p1][:, :], func=AF.Sigmoid),
                     r=[PSK[p1]], w=[("sgm", 0)])
                P.op('act', lambda e, p2=p2: e.activation(out=sgm[1], in_=ps[p2][:, :], func=AF.Sigmoid),
                     r=[PSK[p2]], w=[("sgm", 1)])
                P.op('dve', lambda e, pa=pa: e.tensor_tensor(out=sgm[2], in0=sgm[0], in1=ps[pa][:, :], op=ALU.mult),
                     r=[("sgm", 0), PSK[pa]], w=[("sgm", 2)])
                P.op('dve', lambda e, pbk=pbk: e.tensor_tensor(out=sgm[3], in0=sgm[1], in1=ps[pbk][:, :], op=ALU.mult),
                     r=[("sgm", 1), PSK[pbk]], w=[("sgm", 3)])
                P.op('pool', lambda e, dc=dc, gs=gs: e.tensor_tensor(out=MG[:, dc, gs], in0=sgm[2], in1=sgm[3],
                                                                      op=ALU.add),
                     r=[("sgm", 2), ("sgm", 3)], w=["MG"])
        P.barrier()
        A.reset(mark)
        wsf2 = [A.f32(KC * 128).rearrange("p (k c) -> p k c", k=KC) for _ in range(2)]
        wsb2 = [A.bf16(KC * 128).rearrange("p (k c) -> p k c", k=KC) for _ in range(2)]
        ysb = [A.f32(NT) for _ in range(2)]
        hsb = [A.f32(NT) for _ in range(2)]
        ysq = [A.bf16(NT) for _ in range(2)]
        rstd = A.f32(NT)
        wov = w_out.ap().rearrange("(k p) d -> p k d", p=128)
        it = 0
        for dc in range(KC):
            i = dc % 2
            P.op('sp', lambda e, i=i, dc=dc: e.dma_start(out=wsf2[i], in_=wov[:, :, dc * 128:(dc + 1) * 128]),
                 w=[("wsf", i)], dma=True)
            P.op('pool', lambda e, i=i: e.tensor_copy(out=wsb2[i], in_=wsf2[i]), r=[("wsf", i)], w=[("wsb", i)])
            for grp in range(2):
                gs = slice(grp * 512, (grp + 1) * 512)
                pb = 2 + (it % 6)
                it += 1
                for k in range(KC):
                    P.op('pe', lambda e, pb=pb, k=k, gs=gs, i=i: e.matmul(
                        ps[pb][:, :], lhsT=wsb2[i][:, k, :], rhs=MG[:, k, gs], start=(k == 0), stop=(k == KC - 1)),
                        r=[("wsb", i), "MG"], w=[PSK[pb]])
                P.op('act', lambda e, pb=pb, gs=gs, i=i: e.activation(out=ysb[i][:, gs], in_=ps[pb][:, :], func=AF.Copy),
                     r=[PSK[pb]], w=[("ysb", i)])
                P.op('act', lambda e, pb=pb, gs=gs, i=i: e.activation(out=ysq[i][:, gs], in_=ps[pb][:, :], func=AF.Square),
                     r=[PSK[pb]], w=[("ysq", i)])
            for grp in range(2):
                gs = slice(grp * 512, (grp + 1) * 512)
                P.op('pe', lambda e, grp=grp, gs=gs, i=i, dc=dc: e.matmul(
                    ps[grp][:, :], lhsT=ones_bf, rhs=ysq[i][:, gs], start=(dc == 0), stop=(dc == KC - 1)),
                    r=[("ysq", i), "ones"], w=[PSK[grp]])
            P.op('act', lambda e, i=i, dc=dc: e.dma_start(out=yT_d[dc, :, 0:NL], in_=ysb[i][:, 0:NL]),
                 r=[("ysb", i)], w=[("yT_d", dc)], dma=True)
        for grp in range(2):
            gs = slice(grp * 512, (grp + 1) * 512)
            P.op('act', lambda e, grp=grp, gs=gs: e.activation(
                out=rstd[:, gs], in_=ps[grp][:, :], func=AF.Sqrt, bias=epsb, scale=1.0 / D),
                r=[PSK[grp], "epsb"], w=["rstd2"])
        P.op('dve', lambda e: e.reciprocal(out=rstd[:, 0:NL], in_=rstd[:, 0:NL]), r=["rstd2"], w=["rstd2"])
        residual_update(1, rstd, ysb, hsb, ncols=NL)

    def final_out():
        A.reset(0)
        HT = A.f32(KC * NL).rearrange("p (k t) -> p k t", k=KC)
        xo = [A.f32(D) for _ in range(2)]
        for dc in range(KC):
            P.op('sp', lambda e, dc=dc: e.dma_start(out=HT[:, dc, :], in_=hT_d[dc, :, 0:NL]),
                 r=[("hT_d", dc)], w=["HT"], dma=True)
        it = 0
        for tt in range(8):
            xb = xo[tt % 2]
            xk = ("xo", tt % 2)
            for kq in range(4):
                pb = it % 8
                it += 1
                for kk in range(4):
                    k = kq * 4 + kk
                    P.op('pe', lambda e, pb=pb, kk=kk, k=k, tt=tt: e.transpose(
                        ps[pb][:, kk * 128:(kk + 1) * 128], HT[:, k, tt * 128:(tt + 1) * 128], ident),
                        r=["HT", "ident"], w=[PSK[pb]])
                if kq % 2 == 0:
                    P.op('act', lambda e, pb=pb, xb=xb, kq=kq: e.activation(
                        out=xb[:, kq * 512:(kq + 1) * 512], in_=ps[pb][:, :], func=AF.Copy), r=[PSK[pb]], w=[xk])
                else:
                    P.op('dve', lambda e, pb=pb, xb=xb, kq=kq: e.tensor_copy(
                        out=xb[:, kq * 512:(kq + 1) * 512], in_=ps[pb][:, :]), r=[PSK[pb]], w=[xk])
            P.op('sp', lambda e, xb=xb, tt=tt: e.dma_start(out=out[tt * 128:(tt + 1) * 128, :], in_=xb),
                 r=[xk], w=["out"], dma=True)

    if mode == 'full':
        prenorm_stage(0, True)
        ffn_core(0, 0)

    if 'h1T' in dbg:
        for dc in range(KC):
            P.op('sp', lambda e, dc=dc: e.dma_start(out=dbg['h1T'][dc, :, :], in_=hT_d[dc, :, :]),
                 r=[("hT_d", dc)], dma=True)

    if mode == 'full':
        hm = prenorm_stage(1, False)
    for c in range(8):
        if mode == 'full':
            P.op('sp', lambda e, c=c: e.dma_start(
                out=hmT_d[c * 256:(c + 1) * 256, :].rearrange("(k p) t -> p k t", p=128), in_=hm[:, 2 * c:2 * c + 2, :]),
                r=["hm"], w=[("hmT_d", c)], dma=True)
        else:
            P.op('sp', lambda e, c=c: e.dma_start(out=hmT_d[c * 256:(c + 1) * 256, :],
                                                   in_=hm_in[c * 256:(c + 1) * 256, :]), w=[("hmT_d", c)], dma=True)
        P.op('pool', lambda e, c=c: e.collective_compute(
            "AllGather", ALU.bypass, replica_groups=groups,
            ins=[hmT_d[c * 256:(c + 1) * 256, :].opt()], outs=[hmT_all[c * 1024:(c + 1) * 1024, :].opt()]),
            r=[("hmT_d", c)], w=["hmT_all"], cc=True)
    P.barrier()
    STG = os.environ.get("K_STAGES", "inproj,gla,na,ag2").split(",")
    if "inproj" in STG:
        mixer_inproj()
    A.reset(0)
    gens = []
    if "gla" in STG:
        gens.append([gla(), 1.0, 0.0])
    if "na" in STG:
        gens.append([na(), 1.66, 0.0])
    for g_ in gens:
        next(g_[0])
    while gens:
        for g_ in list(gens):
            g_[2] += g_[1]
            while g_[2] >= 1.0:
                g_[2] -= 1.0
                try:
                    next(g_[0])
                except StopIteration:
                    gens.remove(g_)
                    break
    P.barrier()
    if "ag2" in STG:
        for c in range(8):
            P.op('pool', lambda e, c=c: e.collective_compute(
                "AllGather", ALU.bypass, replica_groups=groups,
                ins=[mixT_d[c * 384:(c + 1) * 384, :].opt()], outs=[mixin_d[c * 1536:(c + 1) * 1536, :].opt()]),
                w=["mixin"], cc=True)
    P.barrier()
    if 'mixT' in dbg:
        P.op('sp', lambda e: e.dma_start(out=dbg['mixT'][:, :], in_=mixT_d[:, :]), dma=True)
        P.barrier()
    if mode == 'full':
        merge_out()
        prenorm_stage(2, False)
        ffn_core(2, 1)
        final_out()
    P.barrier()
    P.emit(nc, st)
    return nc, st


def _consts():
    th = 10000.0
    freqs = th ** (-np.arange(64, dtype=np.float32) / 64.0)
    t = np.arange(SEQ)
    row, col = (t // 64).astype(np.float32), (t % 64).astype(np.float32)
    ropeC = np.zeros((2, 128, SEQ), np.float32)
    ropeS = np.zeros((2, 128, SEQ), np.float32)
    for c, pos in enumerate((row, col)):
        ang = pos[None, :] * freqs[:, None]
        ropeC[c, :64], ropeC[c, 64:] = np.cos(ang), np.cos(ang)
        ropeS[c, :64], ropeS[c, 64:] = np.sin(ang), np.sin(ang)
    rotT = np.zeros((128, 128), np.float32)
    for m in range(64):
        rotT[m + 64, m] = -1.0
        rotT[m, m + 64] = 1.0
    cmask = np.ones((128, 512), np.float32)
    cmask[:, ::64] = 0.0
    ii = np.arange(64)
    trm = np.zeros((64, 2, 64), np.float32)
    trm[:, 0, :] = (ii[None, :] >= ii[:, None])
    trm[:, 1, :] = (ii[None, :] <= ii[:, None])
    return dict(ropeC=ropeC, ropeS=ropeS, rotT=rotT, cmask=cmask, trm=np.ascontiguousarray(trm.reshape(64, 128)),
                identf=np.eye(128, dtype=np.float32), identb=np.eye(128).astype(ml_dtypes.bfloat16))


def _prep_inputs(inputs, ncores=8, mode='full', hm_in=None):
    f = lambda k: np.asarray(inputs[k], np.float32)
    x, c, ctx, c_ctx = f('x'), f('c'), f('ctx'), f('c_ctx')
    w_in = f('w_in')[0]
    gla_wg, gla_bg, gng = f('gla_wg')[0], f('gla_bg')[0], f('gla_norm_g')[0]
    rpb = f('na_rpb')[0]
    shared = _consts()
    shared['w_m'] = np.ascontiguousarray(w_in[:, 9248:13344])
    shared['gng'] = np.ascontiguousarray(gng.reshape(1, 512))
    shared['w_gla_o'] = np.ascontiguousarray(f('w_gla_o')[0])
    shared['w_na_o'] = np.ascontiguousarray(f('w_na_o')[0])
    shared['w_out'] = np.ascontiguousarray(f('w_out')[0])
    w_ada_full = f('w_ada')[0]
    bT_full = np.ascontiguousarray(f('b_ada')[0].reshape(144, 128).T)
    shared['gT'] = np.ascontiguousarray(f('norm_g')[0].reshape(6, KC, 128).transpose(2, 0, 1))
    shared['ffn_wg'] = np.ascontiguousarray(f('ffn_wg')[0])
    shared['ffn_wu'] = np.ascontiguousarray(f('ffn_wu')[0])
    shared['ffn_wd'] = np.ascontiguousarray(f('ffn_wd')[0])
    q = np.arange(64)
    k = np.arange(64)
    dj = np.clip(k[None, :] - q[:, None], -15, 15) + 15
    cs = np.clip(q - 8, 0, 48)
    ok = (k[None, :] >= cs[:, None]) & (k[None, :] < cs[:, None] + 16)
    in_maps = []
    for core in range(ncores):
        b, h = core // 4, core % 4
        s = h
        m = dict(shared)
        xin = np.concatenate([x[b, s * 1024:(s + 1) * 1024], ctx[b, s * 64:(s + 1) * 64]], axis=0)
        m['xin'] = np.ascontiguousarray(xin)
        m['w_ada'] = np.ascontiguousarray(w_ada_full[:, s * 4608:(s + 1) * 4608])
        m['bT'] = np.ascontiguousarray(bT_full[:, s * 36:(s + 1) * 36])
        selv = np.zeros((128, 4), np.float32)
        selv[:, s] = 1.0
        m['sel'] = selv
        cin = np.stack([c[b], c_ctx], axis=0)
        m['cT'] = np.ascontiguousarray(cin.reshape(2, KC, 128).transpose(2, 1, 0))
        cols = np.concatenate([
            np.arange(h * 256, (h + 1) * 256), 1024 + np.arange(h * 256, (h + 1) * 256),
            np.arange(6144, 6176),
            6176 + np.arange(h * 256, (h + 1) * 256), 7200 + np.arange(h * 256, (h + 1) * 256),
            2048 + np.arange(h * 512, (h + 1) * 512), 4096 + np.arange(h * 512, (h + 1) * 512),
            8224 + np.arange(h * 256, (h + 1) * 256)])
        m['w_inh'] = np.ascontiguousarray(w_in[:, cols])
        m['gwg'] = np.ascontiguousarray(gla_wg[:, :, h * 256:(h + 1) * 256].transpose(1, 0, 2))
        m['gbgT'] = np.ascontiguousarray(gla_bg[:, h * 256:(h + 1) * 256].reshape(2, 2, 128).transpose(2, 0, 1).reshape(128, 4))
        tbl = np.empty((4, 64, 15, 64), np.float32)
        for hh in range(4):
            g = rpb[4 * h + hh][:, dj]
            tbl[hh] = np.where(ok[None], g, np.float32(-1e30)).transpose(1, 0, 2)
        m['tb'] = np.ascontiguousarray(tbl.reshape(4, 64, 15 * 64))
        if mode != 'full':
            m['hm_in'] = hm_in[core]
            for kk in ('xin', 'cT', 'w_ada', 'bT', 'ffn_wg', 'ffn_wu', 'ffn_wd'):
                m[kk] = np.zeros((1,), np.float32)
        in_maps.append(m)
    return in_maps


def kernel(**inputs):
    nc, st = build()
    in_maps = _prep_inputs(inputs)
    res = run_bass_kernel_spmd(nc, in_maps, core_ids=list(range(8)))
    outp = np.zeros((2, SEQ, D), np.float32)
    for core in range(8):
        b, s = core // 4, core % 4
        outp[b, s * 1024:(s + 1) * 1024] = res.results[core]["out"]
    return outp
```

```python
import os
import contextlib
import numpy as np
import ml_dtypes
import concourse.bass as bass
import concourse.mybir as mybir
from concourse.bass_utils import run_bass_kernel_spmd

F32 = mybir.dt.float32
BF16 = mybir.dt.bfloat16
AF = mybir.ActivationFunctionType
ALU = mybir.AluOpType

D = 2048
KC = 16
DFF = 5632
FC = 44
NT = 1088
NL = 1024
GROUPS = [(0, 512), (512, 512), (1024, 64)]
EPS = 1e-6
SEQ = 4096
CTX = 256
NS = SEQ + CTX


class Prog:
    def __init__(self):
        self.ops = []
        self.lastw = {}
        self.readers = {}

    def op(self, eng, fn, r=(), w=(), dma=False, cc=False):
        i = len(self.ops)
        deps = set()
        for k in r:
            if k in self.lastw:
                deps.add(self.lastw[k])
        for k in w:
            if k in self.lastw:
                deps.add(self.lastw[k])
            deps.update(self.readers.get(k, ()))
        for k in r:
            self.readers.setdefault(k, []).append(i)
        for k in w:
            self.lastw[k] = i
            self.readers[k] = []
        self.ops.append(dict(eng=eng, fn=fn, deps=deps, dma=dma, cc=cc, barrier=False))
        return i

    def barrier(self):
        self.ops.append(dict(barrier=True))
        self.lastw = {}
        self.readers = {}

    def emit(self, nc, st):
        ops = self.ops
        engs = ('pe', 'act', 'dve', 'pool', 'sp')
        needed = set()
        for o in ops:
            if not o['barrier']:
                needed |= o['deps']
        last = {}
        for i, o in enumerate(ops):
            if o['barrier']:
                for e in engs:
                    if e in last:
                        needed.add(last[e])
                last = {}
            elif not o['dma'] and not o['cc']:
                last[o['eng']] = i
        esem = {e: st.enter_context(nc.semaphore("es_" + e)) for e in engs}
        NQ = 20
        dpool = {q: [st.enter_context(nc.semaphore("dq_%s_%d" % (q, j))) for j in range(NQ)]
                 for q in ('sp', 'pool', 'act')}
        dcnt = {}
        rr = {q: 0 for q in dpool}
        tick = {e: 0 for e in engs}
        ccs = []
        for i, o in enumerate(ops):
            if o['barrier']:
                o['ticks'] = dict(tick)
                o['dmas'] = dict(dcnt)
                o['ccs'] = list(ccs)
            elif o['cc']:
                s = st.enter_context(nc.semaphore("cc_%d" % i))
                o['sem'] = s
                o['val'] = 1
                o['prev'] = 0
                ccs.append(s)
            elif o['dma']:
                q = o['eng']
                s = dpool[q][rr[q] % NQ]
                rr[q] += 1
                o['prev'] = dcnt.get(s, 0)
                dcnt[s] = o['prev'] + 16
                o['sem'] = s
                o['val'] = dcnt[s]
            elif i in needed:
                tick[o['eng']] += 1
                o['tick'] = tick[o['eng']]

        def run(E, e):
            waited = {}

            def w(sem, val):
                key = id(sem)
                if waited.get(key, 0) < val:
                    e.wait_ge(sem, val)
                    waited[key] = val

            for i, o in enumerate(ops):
                if o['barrier']:
                    for F in engs:
                        if o['ticks'][F] > 0:
                            w(esem[F], o['ticks'][F])
                    for s, c in o['dmas'].items():
                        w(s, c)
                    for s in o['ccs']:
                        w(s, 1)
                    continue
                if o['eng'] != E:
                    continue
                for d in sorted(o['deps']):
                    od = ops[d]
                    if od['dma'] or od['cc']:
                        w(od['sem'], od['val'])
                    else:
                        if od['eng'] == E and E == 'pe':
                            continue
                        w(esem[od['eng']], od['tick'])
                if (o['dma']) and o['prev'] > 0:
                    w(o['sem'], o['prev'])
                ins = o['fn'](e)
                if o['cc']:
                    ins.then_inc(o['sem'])
                elif o['dma']:
                    ins.then_inc(o['sem'], 16)
                elif 'tick' in o:
                    ins.then_inc(esem[E], 1)

        with nc.Block() as block:
            @block.tensor
            def _(e):
                run('pe', e)

            @block.scalar
            def _(e):
                run('act', e)

            @block.vector
            def _(e):
                run('dve', e)

            @block.gpsimd
            def _(e):
                run('pool', e)

            @block.sync
            def _(e):
                run('sp', e)


class Arena:
    def __init__(self, t, nwords):
        self.t = t
        self.n = nwords
        self.off = 0

    def reset(self, off=0):
        self.off = off

    def f32(self, n, parts=128):
        a = self.t[0:parts, self.off:self.off + n]
        self.off += n
        assert self.off <= self.n, ("arena overflow", self.off)
        return a

    def bf16(self, n, parts=128):
        words = (n + 1) // 2
        a = self.t[0:parts, self.off:self.off + words].bitcast(BF16)
        self.off += words
        assert self.off <= self.n, ("arena overflow", self.off)
        return a


def build(mode='full', debug=(), ncores=8):
    nc = bass.Bass("TRN2", target_bir_lowering=False)
    P = Prog()
    st = contextlib.ExitStack()

    def din(name, shape, dt=F32):
        return nc.dram_tensor(name, list(shape), dt, kind="ExternalInput")

    def dint(name, shape, dt=F32):
        return nc.dram_tensor(name, list(shape), dt)

    full = (mode == 'full')
    xin = din("xin", [NT, D] if full else [1])
    cT = din("cT", [128, KC, 2] if full else [1])
    w_ada = din("w_ada", [D, 4608] if full else [1])
    bT = din("bT", [128, 36] if full else [1])
    gT = din("gT", [128, 6, KC])
    ffn_wg = din("ffn_wg", [2, D, DFF] if full else [1])
    ffn_wu = din("ffn_wu", [2, D, DFF] if full else [1])
    ffn_wd = din("ffn_wd", [2, DFF, D] if full else [1])
    identf = din("identf", [128, 128])
    groups = [[0, 1, 2, 3], [4, 5, 6, 7]] if ncores == 8 else [[0, 1, 2, 3]]
    w_inh = din("w_inh", [D, 2336])
    w_m = din("w_m", [D, 4096])
    gwg = din("gwg", [16, 2, 256])
    gbgT = din("gbgT", [128, 4])
    gng = din("gng", [1, 512])
    ropeC = din("ropeC", [2, 128, SEQ])
    ropeS = din("ropeS", [2, 128, SEQ])
    rotT_in = din("rotT", [128, 128])
    cmask_in = din("cmask", [128, 512])
    trm_in = din("trm", [64, 128])
    tb_in = din("tb", [4, 64, 15 * 64])
    w_gla_o = din("w_gla_o", [D, D])
    w_na_o = din("w_na_o", [1024, D])
    w_out = din("w_out", [D, D])
    identb_in = din("identb", [128, 128], BF16)
    hm_in = din("hm_in", [D, NT], BF16) if mode != 'full' else None
    hmT_d = dint("hmT_d", [D, NT], BF16)
    modp_d = dint("modp_d", [128, 72])
    modall_d = dint("modall_d", [512, 72])
    hmT_all = dint("hmT_all", [8 * 1024, NT], BF16)
    qT_d = dint("qT_d", [256, NS])
    kT_d = dint("kT_d", [256, NS])
    gT_d = dint("gT_d", [32, NS])
    v_d = dint("v_d", [NS, 512], BF16)
    sr_d = dint("sr_d", [NS, 512])
    nv_d = dint("nv_d", [NS, 256], BF16)
    nqT_d = dint("nqT_d", [256, NS], BF16)
    nkT_d = dint("nkT_d", [256, NS], BF16)
    of_d = dint("of_d", [SEQ, 512])
    mixT_d = dint("mixT_d", [4 * 768, 1024], BF16)
    mixin_d = dint("mixin_d", [8 * 1536, 1024], BF16)
    sel_in = din("sel", [128, 4])
    out = nc.dram_tensor("out", [NL, D], F32, kind="ExternalOutput")

    hT_d = dint("hT_d", [KC, 128, NT])
    yT_d = dint("yT_d", [KC, 128, NT])
    dbg = {}
    for name, shape, dt in debug:
        dbg[name] = nc.dram_tensor("dbg_" + name, list(shape), dt, kind="ExternalOutput")

    AW = 50400
    arena_t = st.enter_context(nc.sbuf_tensor("arena", [128, AW], F32))
    A = Arena(arena_t, AW)
    cst = st.enter_context(nc.sbuf_tensor("cst", [128, 768], F32))
    ident = cst[:, 0:128]
    modT = cst[:, 128:128 + 288].rearrange("p (r j) -> p r j", r=2)
    Amod = cst[:, 416:416 + 96].rearrange("p (r s k) -> p r s k", r=2, s=3)
    Gmod = cst[:, 512:512 + 96].rearrange("p (r s k) -> p r s k", r=2, s=3)
    gTs = cst[:, 608:608 + 96].rearrange("p (s k) -> p s k", s=6)
    cst_b = st.enter_context(nc.sbuf_tensor("cst_b", [128, 256], BF16))
    ones_bf = cst_b[:, 0:128]
    identb = cst_b[:, 128:256]
    cst2 = st.enter_context(nc.sbuf_tensor("cst2", [128, 1800], F32))
    rotT = cst2[:, 0:128]
    cmask = cst2[:, 128:640]
    trm = cst2[0:64, 640:768].rearrange("p (a b) -> p a b", a=2)
    nbg = cst2[:, 768:772]
    one1 = cst2[:, 772:773]
    wgT = cst2[0:16, 776:776 + 512].rearrange("p (a b) -> p a b", a=2)
    gnb = cst2[0:64, 1288:1288 + 512]
    ps = [st.enter_context(nc.psum_tensor("ps%d" % i, [128, 512], F32)) for i in range(8)]
    PSK = [("ps", i) for i in range(8)]

    P.op('sp', lambda e: e.dma_start(out=ident, in_=identf[:, :]), w=["ident"], dma=True)
    P.op('sp', lambda e: e.dma_start(out=gTs, in_=gT[:, :, :]), w=["gTs"], dma=True)
    P.op('pool', lambda e: e.memset(ones_bf, 1.0), w=["ones"])
    epsb = cst[:, 704:705]
    P.op('pool', lambda e: e.memset(epsb, EPS), w=["epsb"])
    P.op('sp', lambda e: e.dma_start(out=identb, in_=identb_in[:, :]), w=["identb"], dma=True)
    P.op('sp', lambda e: e.dma_start(out=rotT, in_=rotT_in[:, :]), w=["rotT"], dma=True)
    P.op('sp', lambda e: e.dma_start(out=cmask, in_=cmask_in[:, :]), w=["cmask"], dma=True)
    P.op('sp', lambda e: e.dma_start(out=cst2[0:64, 640:768], in_=trm_in[:, :]), w=["trm"], dma=True)
    P.op('sp', lambda e: e.dma_start(out=nbg, in_=gbgT[:, :]), w=["nbg"], dma=True)
    P.op('dve', lambda e: e.tensor_scalar(out=nbg, in0=nbg, scalar1=-1.0, scalar2=None, op0=ALU.mult),
         r=["nbg"], w=["nbg"])
    P.op('pool', lambda e: e.memset(one1, 1.0), w=["one1"])
    P.op('sp', lambda e: e.dma_start(out=cst2[0:16, 776:776 + 512], in_=gwg.ap().rearrange("k a b -> k (a b)")),
         w=["wgT"], dma=True)
    P.op('sp', lambda e: e.dma_start(out=gnb, in_=gng.ap().broadcast_to([64, 512])), w=["gnb"], dma=True)
    if mode != 'full':
        P.barrier()
    if mode == 'full':
        A.reset()
        cTs = A.f32(32).rearrange("p (k r) -> p k r", r=2)
        sTs = A.f32(32).rearrange("p (k r) -> p k r", r=2)
        bTs = A.f32(36)
        modp = A.f32(72).rearrange("p (r j) -> p r j", r=2)
        P.op('sp', lambda e: e.dma_start(out=cTs, in_=cT[:, :, :]), w=["cTs"], dma=True)
        P.op('sp', lambda e: e.dma_start(out=bTs, in_=bT[:, :]), w=["bTs"], dma=True)
        P.op('act', lambda e: e.activation(out=sTs, in_=cTs, func=AF.Silu), r=["cTs"], w=["sTs"])
        wv = w_ada.ap().rearrange("(k p) c -> p k c", p=128)
        wst = [A.f32(KC * 512).rearrange("p (k c) -> p k c", k=KC) for _ in range(3)]
        modps = ps[0][:, 0:72].rearrange("p (j r) -> p j r", r=2)
        for cg in range(9):
            wb = wst[cg % 3]
            key = ("wst", cg % 3)
            P.op('sp', lambda e, wb=wb, cg=cg: e.dma_start(out=wb, in_=wv[:, :, cg * 512:(cg + 1) * 512]),
                 w=[key], dma=True)
            for jj in range(4):
                j = cg * 4 + jj
                for k in range(KC):
                    P.op('pe', lambda e, wb=wb, jj=jj, j=j, k=k: e.matmul(
                        modps[:, j, :], lhsT=wb[:, k, jj * 128:(jj + 1) * 128], rhs=sTs[:, k, :],
                        start=(k == 0), stop=(k == KC - 1)), r=[key, "sTs"], w=[PSK[0]])
        for r in range(2):
            P.op('dve', lambda e, r=r: e.tensor_tensor(out=modp[:, r, :], in0=modps[:, :, r], in1=bTs,
                                                        op=ALU.add), r=[PSK[0], "bTs"], w=["modp"])
        P.op('sp', lambda e: e.dma_start(out=modp_d[:, :], in_=modp.rearrange("p r j -> p (r j)")),
             r=["modp"], w=["modp_d"], dma=True)
        P.op('pool', lambda e: e.collective_compute("AllGather", ALU.bypass, replica_groups=groups,
                                                    ins=[modp_d.ap().opt()], outs=[modall_d.ap().opt()]),
             r=["modp_d"], w=["modall_d"], cc=True)
        for q in range(4):
            P.op('sp', lambda e, q=q: e.dma_start(
                out=modT[:, :, q * 36:(q + 1) * 36],
                in_=modall_d[q * 128:(q + 1) * 128, :].rearrange("p (r j) -> p r j", r=2)),
                r=["modall_d"], w=["modT"], dma=True)
        WRES = (0.5, 1.0, 0.5)
        for r in range(2):
            for s in range(3):
                P.op('dve', lambda e, r=r, s=s: e.scalar_tensor_tensor(
                    out=Amod[:, r, s, :], in0=modT[:, r, (3 * s + 1) * 16:(3 * s + 2) * 16], scalar=1.0,
                    in1=gTs[:, 2 * s, :], op0=ALU.add, op1=ALU.mult), r=["modT", "gTs"], w=["Amod"])
                P.op('dve', lambda e, r=r, s=s: e.scalar_tensor_tensor(
                    out=Gmod[:, r, s, :], in0=modT[:, r, (3 * s + 2) * 16:(3 * s + 3) * 16], scalar=WRES[s],
                    in1=gTs[:, 2 * s + 1, :], op0=ALU.mult, op1=ALU.mult), r=["modT", "gTs"], w=["Gmod"])
        P.barrier()

    def Bmod(r, s, k):
        return modT[:, r, (3 * s) * 16 + k:(3 * s) * 16 + k + 1]

    def prenorm_modulate(HT, s, hm, sqkey="hm"):
        rstd = A.f32(NT)
        tmp = [A.f32(NT) for _ in range(2)]
        for k in range(KC):
            P.op('act', lambda e, k=k: e.activation(out=hm[:, k, :], in_=HT[:, k, :], func=AF.Square),
                 r=["HT"], w=["hm"])
        for gi, (c0, n) in enumerate(GROUPS):
            for k in range(KC):
                P.op('pe', lambda e, k=k, c0=c0, n=n, gi=gi: e.matmul(
                    ps[gi][:, 0:n], lhsT=ones_bf, rhs=hm[:, k, c0:c0 + n], start=(k == 0), stop=(k == KC - 1)),
                    r=["hm", "ones"], w=[PSK[gi]])
            P.op('act', lambda e, c0=c0, n=n, gi=gi: e.activation(
                out=rstd[:, c0:c0 + n], in_=ps[gi][:, 0:n], func=AF.Sqrt, bias=epsb, scale=1.0 / D),
                r=[PSK[gi], "epsb"], w=["rstd"])
        P.op('dve', lambda e: e.reciprocal(out=rstd, in_=rstd), r=["rstd"], w=["rstd"])
        for k in range(KC):
            t = tmp[k % 2]
            tk = ("tmpn", k % 2)
            P.op('dve', lambda e, k=k, t=t: e.tensor_tensor(out=t, in0=HT[:, k, :], in1=rstd, op=ALU.mult),
                 r=["HT", "rstd"], w=[tk])
            for r, (c0, n) in enumerate([(0, NL), (NL, NT - NL)]):
                P.op('act', lambda e, k=k, t=t, r=r, c0=c0, n=n: e.activation(
                    out=hm[:, k, c0:c0 + n], in_=t[:, c0:c0 + n], func=AF.Identity,
                    bias=Bmod(r, s, k), scale=Amod[:, r, s, k:k + 1]),
                    r=[tk, "Amod", "modT"], w=["hm"])

    HM_W = KC * NT // 2
    ACT_W = FC * NT // 2

    def ffn_core(s, li):
        wg = ffn_wg.ap()[li].rearrange("(k p) f -> p k f", p=128)
        wu = ffn_wu.ap()[li].rearrange("(k p) f -> p k f", p=128)
        wd = ffn_wd.ap()[li].rearrange("(f p) d -> p f d", p=128)
        A.reset(0)
        hm = A.bf16(KC * NT).rearrange("p (k t) -> p k t", k=KC)
        actT = A.bf16(FC * NT).rearrange("p (f t) -> p f t", f=FC)
        mark = A.off
        wsf = [A.f32(KC * 256).rearrange("p (k c) -> p k c", k=KC) for _ in range(2)]
        wsb = [A.bf16(KC * 256).rearrange("p (k c) -> p k c", k=KC) for _ in range(4)]
        sg = [A.f32(512) for _ in range(2)]
        slot = 0
        nsg = 0
        nld = 0
        for fg in range(FC // 2):
            wb = []
            for gu, wsrc in enumerate((wg, wu)):
                i2 = nld % 2
                i = nld % 4
                nld += 1
                P.op('sp', lambda e, i2=i2, wsrc=wsrc, fg=fg: e.dma_start(
                    out=wsf[i2], in_=wsrc[:, :, fg * 256:(fg + 1) * 256]), w=[("wsf", i2)], dma=True)
                for hk in range(2):
                    P.op('pool', lambda e, i=i, i2=i2, hk=hk: e.tensor_copy(
                        out=wsb[i][:, hk * 8:(hk + 1) * 8, :], in_=wsf[i2][:, hk * 8:(hk + 1) * 8, :]),
                        r=[("wsf", i2)], w=[("wsb", i)])
                wb.append(i)
            for f2 in range(2):
                f = fg * 2 + f2
                for (c0, n) in GROUPS:
                    pg, pu = 2 + 2 * (slot % 3), 3 + 2 * (slot % 3)
                    slot += 1
                    for gu, pb in enumerate((pg, pu)):
                        for k in range(KC):
                            P.op('pe', lambda e, pb=pb, k=k, c0=c0, n=n, i=wb[gu], f2=f2: e.matmul(
                                ps[pb][:, 0:n], lhsT=wsb[i][:, k, f2 * 128:(f2 + 1) * 128], rhs=hm[:, k, c0:c0 + n],
                                start=(k == 0), stop=(k == KC - 1)),
                                r=[("wsb", wb[gu]), "hm"], w=[PSK[pb]])
                    sgi = nsg % 2
                    nsg += 1
                    P.op('act', lambda e, pg=pg, n=n, sgi=sgi: e.activation(out=sg[sgi][:, 0:n], in_=ps[pg][:, 0:n],
                                                                            func=AF.Silu),
                         r=[PSK[pg]], w=[("sg", sgi)])
                    P.op('dve', lambda e, pu=pu, n=n, sgi=sgi, f=f, c0=c0: e.tensor_tensor(
                        out=actT[:, f, c0:c0 + n], in0=sg[sgi][:, 0:n], in1=ps[pu][:, 0:n], op=ALU.mult),
                        r=[("sg", sgi), PSK[pu]], w=["act"])
        P.barrier()
        A.reset(0)
        ysb = [A.f32(NT) for _ in range(2)]
        hsb = [A.f32(NT) for _ in range(2)]
        ysq = [A.bf16(NT) for _ in range(2)]
        rstd = A.f32(NT)
        assert A.off <= HM_W
        A.reset(mark)
        wdf = [A.f32(11 * 256).rearrange("p (f c) -> p f c", f=11) for _ in range(2)]
        wdb = [A.bf16(FC * 256).rearrange("p (f c) -> p f c", f=FC) for _ in range(2)]
        slot = 0
        nst = 0
        for dcg in range(KC // 2):
            t = dcg % 2
            for qtr in range(4):
                qi = nst % 2
                nst += 1
                P.op('sp', lambda e, qi=qi, dcg=dcg, qtr=qtr: e.dma_start(
                    out=wdf[qi], in_=wd[:, qtr * 11:(qtr + 1) * 11, dcg * 256:(dcg + 1) * 256]),
                    w=[("wdf", qi)], dma=True)
                P.op('pool', lambda e, qi=qi, t=t, qtr=qtr: e.tensor_copy(
                    out=wdb[t][:, qtr * 11:(qtr + 1) * 11, :], in_=wdf[qi]), r=[("wdf", qi)], w=[("wdb", t)])
            for dc2 in range(2):
                dc = dcg * 2 + dc2
                i = dc % 2
                for gi, (c0, n) in enumerate(GROUPS):
                    pb = 3 + (slot % 5)
                    slot += 1
                    for f in range(FC):
                        P.op('pe', lambda e, pb=pb, f=f, c0=c0, n=n, t=t, dc2=dc2: e.matmul(
                            ps[pb][:, 0:n], lhsT=wdb[t][:, f, dc2 * 128:(dc2 + 1) * 128], rhs=actT[:, f, c0:c0 + n],
                            start=(f == 0), stop=(f == FC - 1)),
                            r=[("wdb", t), "act"], w=[PSK[pb]])
                    P.op('act', lambda e, pb=pb, c0=c0, n=n, i=i: e.activation(
                        out=ysb[i][:, c0:c0 + n], in_=ps[pb][:, 0:n], func=AF.Copy), r=[PSK[pb]], w=[("ysb", i)])
                    P.op('act', lambda e, pb=pb, c0=c0, n=n, i=i: e.activation(
                        out=ysq[i][:, c0:c0 + n], in_=ps[pb][:, 0:n], func=AF.Square), r=[PSK[pb]], w=[("ysq", i)])
                for gi, (c0, n) in enumerate(GROUPS):
                    P.op('pe', lambda e, gi=gi, c0=c0, n=n, i=i, dc=dc: e.matmul(
                        ps[gi][:, 0:n], lhsT=ones_bf, rhs=ysq[i][:, c0:c0 + n], start=(dc == 0), stop=(dc == KC - 1)),
                        r=[("ysq", i), "ones"], w=[PSK[gi]])
                P.op('act', lambda e, i=i, dc=dc: e.dma_start(out=yT_d[dc, :, :], in_=ysb[i]),
                     r=[("ysb", i)], w=[("yT_d", dc)], dma=True)
        for gi, (c0, n) in enumerate(GROUPS):
            P.op('act', lambda e, c0=c0, n=n, gi=gi: e.activation(
                out=rstd[:, c0:c0 + n], in_=ps[gi][:, 0:n], func=AF.Sqrt, bias=epsb, scale=1.0 / D),
                r=[PSK[gi], "epsb"], w=["rstd2"])
        P.op('dve', lambda e: e.reciprocal(out=rstd, in_=rstd), r=["rstd2"], w=["rstd2"])
        residual_update(s, rstd, ysb, hsb)

    def residual_update(s, rstd, ysb, hsb, ncols=NT):
        ranges = [(0, 0, NL)] + ([(1, NL, NT - NL)] if ncols == NT else [])
        for dc in range(KC):
            i = dc % 2
            yb, hb = ysb[i], hsb[i]
            P.op('sp', lambda e, yb=yb, dc=dc: e.dma_start(out=yb[:, 0:ncols], in_=yT_d[dc, :, 0:ncols]),
                 r=[("yT_d", dc)], w=[("ysb", i)], dma=True)
            P.op('sp', lambda e, hb=hb, dc=dc: e.dma_start(out=hb[:, 0:ncols], in_=hT_d[dc, :, 0:ncols]),
                 r=[("hT_d", dc)], w=[("hsb", i)], dma=True)
            P.op('dve', lambda e, yb=yb: e.tensor_tensor(out=yb[:, 0:ncols], in0=yb[:, 0:ncols], in1=rstd[:, 0:ncols],
                                                         op=ALU.mult),
                 r=[("ysb", i), "rstd2"], w=[("ysb", i)])
            for (r, c0, n) in ranges:
                P.op('dve', lambda e, yb=yb, hb=hb, dc=dc, r=r, c0=c0, n=n: e.scalar_tensor_tensor(
                    out=hb[:, c0:c0 + n], in0=yb[:, c0:c0 + n], scalar=Gmod[:, r, s, dc:dc + 1],
                    in1=hb[:, c0:c0 + n], op0=ALU.mult, op1=ALU.add),
                    r=[("ysb", i), "Gmod", ("hsb", i)], w=[("hsb", i)])
            P.op('pool', lambda e, hb=hb, dc=dc: e.dma_start(out=hT_d[dc, :, 0:ncols], in_=hb[:, 0:ncols]),
                 r=[("hsb", i)], w=[("hT_d", dc)], dma=True)
        P.barrier()

    def prenorm_stage(s, from_x):
        A.reset(0)
        hm = A.bf16(KC * NT).rearrange("p (k t) -> p k t", k=KC)
        HT = A.f32(KC * NT).rearrange("p (k t) -> p k t", k=KC)
        if from_x:
            xt = [A.f32(D) for _ in range(2)]
            tiles = [(i * 128, 128) for i in range(8)] + [(NL, 64)]
            slot = 0
            for ti, (t0, pn) in enumerate(tiles):
                xb = xt[ti % 2]
                P.op('sp', lambda e, xb=xb, t0=t0, pn=pn: e.dma_start(out=xb[0:pn, :], in_=xin[t0:t0 + pn, :]),
                     w=[("xt", ti % 2)], dma=True)
                for kq in range(4):
                    pb = 4 + (slot % 4)
                    slot += 1
                    for kk in range(4):
                        k = kq * 4 + kk
                        P.op('pe', lambda e, xb=xb, pn=pn, pb=pb, kk=kk, k=k: e.transpose(
                            ps[pb][:, kk * 128:kk * 128 + pn], xb[0:pn, k * 128:(k + 1) * 128], ident[0:pn, 0:pn]),
                            r=[("xt", ti % 2), "ident"], w=[PSK[pb]])
                    src = ps[pb][:, :].rearrange("p (a b) -> p a b", a=4)[:, :, 0:pn]
                    dst = HT[:, kq * 4:(kq + 1) * 4, t0:t0 + pn]
                    if kq % 2 == 0:
                        P.op('act', lambda e, src=src, dst=dst: e.activation(out=dst, in_=src, func=AF.Copy),
                             r=[PSK[pb]], w=["HT"])
                    else:
                        P.op('dve', lambda e, src=src, dst=dst: e.tensor_copy(out=dst, in_=src),
                             r=[PSK[pb]], w=["HT"])
            for dc in range(KC):
                P.op('pool', lambda e, dc=dc: e.dma_start(out=hT_d[dc, :, :], in_=HT[:, dc, :]),
                     r=["HT"], w=[("hT_d", dc)], dma=True)
        else:
            for dc in range(KC):
                P.op('sp', lambda e, dc=dc: e.dma_start(out=HT[:, dc, :], in_=hT_d[dc, :, :]),
                     r=[("hT_d", dc)], w=["HT"], dma=True)
        prenorm_modulate(HT, s, hm)
        P.barrier()
        return hm

    def tok_groups():
        gl = [(0, 256, [(r, 1024, 64, r * 64) for r in range(4)])]
        for g in range(1, 9):
            gl.append((256 + 512 * (g - 1), 512, [((g - 1) // 2, ((g - 1) % 2) * 512, 512, 0)]))
        return gl

    def mixer_inproj():
        A.reset(0)
        NW = 2336
        WB = A.bf16(KC * NW).rearrange("p (k c) -> p k c", k=KC)
        wstg = [A.f32(NW) for _ in range(2)]
        wv_ = w_inh.ap().rearrange("(k p) c -> p k c", p=128)
        for k in range(KC):
            P.op('sp', lambda e, k=k: e.dma_start(out=wstg[k % 2], in_=wv_[:, k, :]), w=[("wstg", k % 2)], dma=True)
            P.op('pool', lambda e, k=k: e.tensor_copy(out=WB[:, k, :], in_=wstg[k % 2]),
                 r=[("wstg", k % 2)], w=["WB"])
        X = [A.bf16(KC * 512).rearrange("p (k t) -> p k t", k=KC) for _ in range(2)]
        ev = [A.f32(512) for _ in range(4)]
        evb = [A.bf16(512) for _ in range(4)]
        cnt = dict(pb=0, ev=0, evb=0)

        def nxt(kind, mod):
            v = cnt[kind] % mod
            cnt[kind] += 1
            return v

        FM = [(0, 128, 'q', 0), (128, 128, 'q', 1), (256, 128, 'k', 0), (384, 128, 'k', 1), (512, 32, 'g', 0),
              (544, 128, 'nq', 0), (672, 128, 'nq', 1), (800, 128, 'nk', 0), (928, 128, 'nk', 1)]
        TM = [(1056, 512, 'v'), (1568, 512, 'r'), (2080, 256, 'nv')]
        for gi, (n0, n, srcs) in enumerate(tok_groups()):
            Xb = X[gi % 2]
            xk = ("X", gi % 2)
            for (r, c0, nn, x0) in srcs:
                for c in range(8):
                    P.op('sp', lambda e, Xb=Xb, r=r, c0=c0, nn=nn, x0=x0, c=c: e.dma_start(
                        out=Xb[:, 2 * c:2 * c + 2, x0:x0 + nn],
                        in_=hmT_all[c * 1024 + r * 256:c * 1024 + (r + 1) * 256, c0:c0 + nn].rearrange(
                            "(k p) t -> p k t", p=128)), w=[xk], dma=True)
            for (c0, M, kind, idx) in FM:
                if gi == 0 and kind in ('q', 'nq'):
                    continue
                pb = nxt('pb', 8)
                for k in range(KC):
                    P.op('pe', lambda e, pb=pb, M=M, n=n, k=k, c0=c0, Xb=Xb: e.matmul(
                        ps[pb][0:M, 0:n], lhsT=WB[:, k, c0:c0 + M], rhs=Xb[:, k, 0:n],
                        start=(k == 0), stop=(k == KC - 1)), r=["WB", xk], w=[PSK[pb]])
                if kind in ('q', 'k', 'g'):
                    j = nxt('ev', 4)
                    dst = {'q': qT_d, 'k': kT_d, 'g': gT_d}[kind]
                    P.op('act', lambda e, pb=pb, M=M, n=n, j=j: e.activation(out=ev[j][0:M, 0:n], in_=ps[pb][0:M, 0:n],
                                                                           func=AF.Copy), r=[PSK[pb]], w=[("ev", j)])
                    P.op('pool', lambda e, dst=dst, idx=idx, M=M, n=n, n0=n0, j=j: e.dma_start(
                        out=dst[idx * 128:idx * 128 + M, n0:n0 + n], in_=ev[j][0:M, 0:n]),
                        r=[("ev", j)], w=["qkg_d"], dma=True)
                else:
                    j = nxt('evb', 4)
                    dst = {'nq': nqT_d, 'nk': nkT_d}[kind]
                    sc = 0.125 if kind == 'nq' else 1.0
                    P.op('act', lambda e, pb=pb, n=n, j=j, sc=sc: e.activation(
                        out=evb[j][:, 0:n], in_=ps[pb][:, 0:n], func=AF.Copy, scale=sc), r=[PSK[pb]], w=[("evb", j)])
                    P.op('pool', lambda e, dst=dst, idx=idx, n=n, n0=n0, j=j: e.dma_start(
                        out=dst[idx * 128:(idx + 1) * 128, n0:n0 + n], in_=evb[j][:, 0:n]),
                        r=[("evb", j)], w=["qkg_d"], dma=True)
            for tt in range(n // 128):
                for (c0, ncol, kind) in TM:
                    pb = nxt('pb', 8)
                    for k in range(KC):
                        P.op('pe', lambda e, pb=pb, ncol=ncol, k=k, c0=c0, Xb=Xb, tt=tt: e.matmul(
                            ps[pb][:, 0:ncol], lhsT=Xb[:, k, tt * 128:(tt + 1) * 128], rhs=WB[:, k, c0:c0 + ncol],
                            start=(k == 0), stop=(k == KC - 1)), r=["WB", xk], w=[PSK[pb]])
                    r0 = n0 + tt * 128
                    if kind == 'r':
                        j = nxt('ev', 4)
                        P.op('act', lambda e, pb=pb, j=j: e.activation(out=ev[j], in_=ps[pb][:, 0:512], func=AF.Silu),
                             r=[PSK[pb]], w=[("ev", j)])
                        P.op('pool', lambda e, r0=r0, j=j: e.dma_start(out=sr_d[r0:r0 + 128, :], in_=ev[j]),
                             r=[("ev", j)], w=["qkg_d"], dma=True)
                    else:
                        j = nxt('evb', 4)
                        dst = v_d if kind == 'v' else nv_d
                        P.op('dve', lambda e, pb=pb, j=j, ncol=ncol: e.tensor_copy(out=evb[j][:, 0:ncol],
                                                                                  in_=ps[pb][:, 0:ncol]),
                             r=[PSK[pb]], w=[("evb", j)])
                        P.op('pool', lambda e, dst=dst, r0=r0, j=j, ncol=ncol: e.dma_start(
                            out=dst[r0:r0 + 128, :], in_=evb[j][:, 0:ncol]), r=[("evb", j)], w=["qkg_d"], dma=True)
        P.barrier()

    def gla():
        S = A.f32(1024).rearrange("p (c v) -> p c v", c=2)
        Sb = A.bf16(1024).rearrange("p (c v) -> p c v", c=2)
        f3 = lambda: A.f32(1024).rearrange("p (c t) -> p c t", c=2)
        b3 = lambda: A.bf16(1024).rearrange("p (c t) -> p c t", c=2)
        qT, kT, Ct, St = f3(), f3(), f3(), f3()
        e_, sp, cum, d3, E1, E2, E3, t1, qr, kr = [f3() for _ in range(10)]
        qd, ki, ke = b3(), b3(), b3()
        gts = A.f32(512, parts=16)
        kend = A.bf16(8 * 256, parts=64).rearrange("p (c d) -> p c d", c=8)
        vb = A.bf16(8 * 512, parts=64).rearrange("p (c v) -> p c v", c=8)
        dch = A.f32(16).rearrange("p (c n) -> p c n", c=2)
        att_sb = [A.bf16(64, parts=64) for _ in range(2)]
        o_sb = [A.f32(512, parts=64) for _ in range(2)]
        of_g = A.f32(8 * 512, parts=64).rearrange("p (c v) -> p c v", c=8)
        sr_g = A.f32(8 * 512, parts=64).rearrange("p (c v) -> p c v", c=8)
        gcT = A.bf16(4 * 512).rearrange("p (a t) -> p a t", a=4)
        res = [A.bf16(512, parts=64) for _ in range(2)]
        junk = A.f32(512, parts=64)
        ssq = [A.f32(1, parts=64) for _ in range(2)]
        glist = tok_groups()
        ci = 0
        yield
        for dirn in (0, 1):
            P.op('pool', lambda e: e.memset(S, 0.0), w=["S"])
            P.op('pool', lambda e: e.memset(Sb, 0.0), w=["Sb"])
            order = [0] + (list(range(1, 9)) if dirn == 0 else list(range(8, 0, -1)))
            for g in order:
                n0, n, _ = glist[g]
                nch = n // 64
                lat = g > 0
                tok0 = n0 - 256
                P.op('sp', lambda e, n0=n0, n=n: e.dma_start(
                    out=kT[:, :, 0:n], in_=kT_d.ap().rearrange("(c p) t -> p c t", p=128)[:, :, n0:n0 + n]),
                    r=["qkg_d"], w=["kT"], dma=True)
                P.op('sp', lambda e, n0=n0, n=n, dirn=dirn: e.dma_start(
                    out=gts[:, 0:n], in_=gT_d[dirn * 16:(dirn + 1) * 16, n0:n0 + n]), r=["qkg_d"], w=["gts"], dma=True)
                P.op('sp', lambda e, n0=n0, n=n, nch=nch: e.dma_start(
                    out=vb[:, 0:nch, :], in_=v_d[n0:n0 + n, :].rearrange("(c p) v -> p c v", p=64)),
                    r=["qkg_d"], w=["vb"], dma=True)
                if lat:
                    P.op('sp', lambda e, n0=n0, n=n: e.dma_start(
                        out=qT[:, :, 0:n], in_=qT_d.ap().rearrange("(c p) t -> p c t", p=128)[:, :, n0:n0 + n]),
                        r=["qkg_d"], w=["qT"], dma=True)
                    P.op('sp', lambda e, tok0=tok0, n=n: e.dma_start(
                        out=Ct[:, :, 0:n], in_=ropeC.ap().rearrange("c p t -> p c t")[:, :, tok0:tok0 + n]),
                        w=["Ct"], dma=True)
                    P.op('sp', lambda e, tok0=tok0, n=n: e.dma_start(
                        out=St[:, :, 0:n], in_=ropeS.ap().rearrange("c p t -> p c t")[:, :, tok0:tok0 + n]),
                        w=["St"], dma=True)
                    if dirn == 1:
                        P.op('sp', lambda e, tok0=tok0, n=n: e.dma_start(
                            out=of_g, in_=of_d[tok0:tok0 + n, :].rearrange("(c p) v -> p c v", p=64)),
                            r=["of_d"], w=["of_g"], dma=True)
                        P.op('sp', lambda e, n0=n0, n=n: e.dma_start(
                            out=sr_g, in_=sr_d[n0:n0 + n, :].rearrange("(c p) v -> p c v", p=64)),
                            r=["qkg_d"], w=["sr_g"], dma=True)
                        for c in range(8):
                            P.op('dve', lambda e, c=c: e.tensor_tensor(out=sr_g[:, c, :], in0=sr_g[:, c, :], in1=gnb,
                                                                       op=ALU.mult), r=["sr_g", "gnb"], w=["sr_g"])
                for dc in range(2):
                    P.op('pe', lambda e, dc=dc, n=n, dirn=dirn: e.matmul(
                        ps[dc][:, 0:n], lhsT=wgT[:, dirn, dc * 128:(dc + 1) * 128], rhs=gts[:, 0:n],
                        start=True, stop=True), r=["gts", "wgT"], w=[PSK[dc]])
                    P.op('act', lambda e, dc=dc, n=n, dirn=dirn: e.activation(
                        out=e_[:, dc, 0:n], in_=ps[dc][:, 0:n], func=AF.Exp, scale=-1.0,
                        bias=nbg[:, dirn * 2 + dc:dirn * 2 + dc + 1]), r=[PSK[dc], "nbg"], w=["e_"])
                    P.op('act', lambda e, dc=dc, n=n: e.activation(
                        out=sp[:, dc, 0:n], in_=e_[:, dc, 0:n], func=AF.Ln, bias=one1, scale=1.0),
                        r=["e_", "one1"], w=["sp"])
                    P.op('dve', lambda e, dc=dc, n=n: e.tensor_tensor_scan(
                        out=cum[:, dc, 0:n], data0=cmask[:, 0:n], data1=sp[:, dc, 0:n], initial=0.0,
                        op0=ALU.mult, op1=ALU.add), r=["sp", "cmask"], w=["cum"])
                cum4 = cum[:, :, 0:n].rearrange("p c (h t) -> p c h t", t=64)
                tot = cum4[:, :, :, 63:64]
                for dc in range(2):
                    P.op('dve', lambda e, dc=dc, n=n, nch=nch: e.tensor_tensor(
                        out=d3[:, dc, 0:n].rearrange("p (h t) -> p h t", t=64),
                        in0=cum[:, dc, 0:n].rearrange("p (h t) -> p h t", t=64),
                        in1=cum[:, dc, 0:n].rearrange("p (h t) -> p h t", t=64)[:, :, 63:64].to_broadcast([128, nch, 64]),
                        op=ALU.subtract), r=["cum"], w=["d3"])
                    P.op('act', lambda e, dc=dc, n=n, nch=nch: e.activation(
                        out=dch[:, dc, 0:nch], in_=cum[:, dc, 0:n].rearrange("p (h t) -> p h t", t=64)[:, :, 63],
                        func=AF.Exp, scale=-1.0 / 16), r=["cum"], w=["dch"])
                if dirn == 0:
                    P.op('act', lambda e, n=n: e.activation(out=E1[:, :, 0:n], in_=cum[:, :, 0:n], func=AF.Exp,
                                                            scale=-1.0 / 16), r=["cum"], w=["E1"])
                    P.op('act', lambda e, n=n: e.activation(out=E2[:, :, 0:n], in_=cum[:, :, 0:n], func=AF.Exp,
                                                            scale=1.0 / 16), r=["cum"], w=["E2"])
                    P.op('act', lambda e, n=n: e.activation(out=E3[:, :, 0:n], in_=d3[:, :, 0:n], func=AF.Exp,
                                                            scale=1.0 / 16), r=["d3"], w=["E3"])
                else:
                    P.op('dve', lambda e, n=n: e.tensor_tensor(out=t1[:, :, 0:n], in0=sp[:, :, 0:n], in1=d3[:, :, 0:n],
                                                               op=ALU.subtract), r=["sp", "d3"], w=["t1"])
                    P.op('act', lambda e, n=n: e.activation(out=E1[:, :, 0:n], in_=t1[:, :, 0:n], func=AF.Exp,
                                                            scale=-1.0 / 16), r=["t1"], w=["E1"])
                    P.op('act', lambda e, n=n: e.activation(out=E2[:, :, 0:n], in_=t1[:, :, 0:n], func=AF.Exp,
                                                            scale=1.0 / 16), r=["t1"], w=["E2"])
                    P.op('dve', lambda e, n=n: e.tensor_tensor(out=d3[:, :, 0:n], in0=cum[:, :, 0:n], in1=sp[:, :, 0:n],
                                                               op=ALU.subtract), r=["sp", "cum", "E1"], w=["d3"])
                    P.op('act', lambda e, n=n: e.activation(out=E3[:, :, 0:n], in_=d3[:, :, 0:n], func=AF.Exp,
                                                            scale=-1.0 / 16), r=["d3"], w=["E3"])
                if lat:
                    for (src, dstr, nm) in ((qT, qr, "q"), (kT, kr, "k")):
                        for dc in range(2):
                            pb = dc
                            P.op('pe', lambda e, pb=pb, dc=dc, n=n, src=src: e.matmul(
                                ps[pb][:, 0:n], lhsT=rotT, rhs=src[:, dc, 0:n], start=True, stop=True),
                                r=[nm + "T", "rotT"], w=[PSK[pb]])
                            P.op('dve', lambda e, pb=pb, dc=dc, n=n: e.tensor_tensor(
                                out=t1[:, dc, 0:n], in0=ps[pb][:, 0:n], in1=St[:, dc, 0:n], op=ALU.mult),
                                r=[PSK[pb], "St", "E1", "E2"], w=["t1"])
                            P.op('pool', lambda e, dc=dc, n=n, src=src, dstr=dstr: e.tensor_tensor(
                                out=dstr[:, dc, 0:n], in0=src[:, dc, 0:n], in1=Ct[:, dc, 0:n], op=ALU.mult),
                                r=[nm + "T", "Ct"], w=[nm + "r"])
                            P.op('dve', lambda e, dc=dc, n=n, dstr=dstr: e.tensor_tensor(
                                out=dstr[:, dc, 0:n], in0=dstr[:, dc, 0:n], in1=t1[:, dc, 0:n], op=ALU.add),
                                r=[nm + "r", "t1"], w=[nm + "r"])
                    ksrc, kname = kr, "kr"
                    for dc in range(2):
                        P.op('dve', lambda e, dc=dc, n=n: e.scalar_tensor_tensor(
                            out=qd[:, dc, 0:n], in0=qr[:, dc, 0:n], scalar=0.0625, in1=E1[:, dc, 0:n],
                            op0=ALU.mult, op1=ALU.mult), r=["qr", "E1"], w=["qd"])
                        P.op('dve', lambda e, dc=dc, n=n: e.tensor_tensor(
                            out=ki[:, dc, 0:n], in0=kr[:, dc, 0:n], in1=E2[:, dc, 0:n], op=ALU.mult),
                            r=["kr", "E2"], w=["ki"])
                else:
                    ksrc, kname = kT, "kT"
                for dc in range(2):
                    P.op('pool', lambda e, dc=dc, n=n, ksrc=ksrc: e.tensor_tensor(
                        out=ke[:, dc, 0:n], in0=ksrc[:, dc, 0:n], in1=E3[:, dc, 0:n], op=ALU.mult),
                        r=[kname, "E3"], w=["ke"])
                for half in range((nch + 3) // 4):
                    pb = half
                    pv = ps[pb][0:64, :].bitcast(BF16).rearrange("p (c d) -> p c d", c=4)
                    for cc in range(4):
                        c = half * 4 + cc
                        for dc in range(2):
                            P.op('pe', lambda e, pv=pv, cc=cc, c=c, dc=dc: e.transpose(
                                pv[:, cc, dc * 128:(dc + 1) * 128], ke[:, dc, c * 64:(c + 1) * 64], identb),
                                r=["ke", "identb"], w=[PSK[pb]])
                    P.op('act', lambda e, pv=pv, half=half: e.activation(out=kend[:, half * 4:half * 4 + 4, :], in_=pv,
                                                                        func=AF.Copy), r=[PSK[pb]], w=["kend"])
                yield
                clist = list(range(nch)) if dirn == 0 else list(range(nch - 1, -1, -1))
                for c in clist:
                    yield
                    cs = slice(c * 64, (c + 1) * 64)
                    j = ci % 2
                    ci += 1
                    pa, po = (2, 3) if j == 0 else (6, 7)
                    if lat:
                        for dc in range(2):
                            P.op('pe', lambda e, pa=pa, dc=dc, cs=cs: e.matmul(
                                ps[pa][0:64, 0:64], lhsT=ki[:, dc, cs], rhs=qd[:, dc, cs],
                                start=(dc == 0), stop=(dc == 1)), r=["ki", "qd"], w=[PSK[pa]])
                        P.op('dve', lambda e, pa=pa, j=j, dirn=dirn: e.tensor_tensor(
                            out=att_sb[j], in0=ps[pa][0:64, 0:64], in1=trm[:, dirn, :], op=ALU.mult),
                            r=[PSK[pa], "trm"], w=[("att", j)])
                        P.op('pe', lambda e, po=po, j=j, c=c: e.matmul(
                            ps[po][0:64, 0:512], lhsT=att_sb[j], rhs=vb[:, c, :], start=True, stop=False),
                            r=[("att", j), "vb"], w=[PSK[po]])
                        for dc in range(2):
                            P.op('pe', lambda e, po=po, dc=dc, cs=cs: e.matmul(
                                ps[po][0:64, 0:512], lhsT=qd[:, dc, cs], rhs=Sb[:, dc, :],
                                start=False, stop=(dc == 1)), r=["qd", "Sb"], w=[PSK[po]])
                    for dc in range(2):
                        P.op('pe', lambda e, dc=dc, c=c: e.matmul(
                            ps[4 + dc][:, 0:512], lhsT=kend[:, c, dc * 128:(dc + 1) * 128], rhs=vb[:, c, :],
                            start=True, stop=True), r=["kend", "vb"], w=[PSK[4 + dc]])
                        P.op('dve', lambda e, dc=dc, c=c: e.scalar_tensor_tensor(
                            out=S[:, dc, :], in0=S[:, dc, :], scalar=dch[:, dc, c:c + 1], in1=ps[4 + dc][:, 0:512],
                            op0=ALU.mult, op1=ALU.add), r=["S", "dch", PSK[4 + dc]], w=["S"])
                        P.op('act', lambda e, dc=dc: e.activation(out=Sb[:, dc, :], in_=S[:, dc, :], func=AF.Copy),
                             r=["S"], w=["Sb"])
                    if lat and dirn == 0:
                        P.op('act', lambda e, po=po, j=j: e.activation(out=o_sb[j], in_=ps[po][0:64, 0:512],
                                                                      func=AF.Copy), r=[PSK[po]], w=[("o_sb", j)])
                        P.op('pool', lambda e, j=j, tok0=tok0, c=c: e.dma_start(
                            out=of_d[tok0 + c * 64:tok0 + (c + 1) * 64, :], in_=o_sb[j]),
                            r=[("o_sb", j)], w=["of_d"], dma=True)
                    elif lat:
                        P.op('dve', lambda e, po=po, j=j, c=c: e.tensor_tensor(
                            out=o_sb[j], in0=ps[po][0:64, 0:512], in1=of_g[:, c, :], op=ALU.add),
                            r=[PSK[po], "of_g"], w=[("o_sb", j)])
                        P.op('act', lambda e, j=j: e.activation(out=junk, in_=o_sb[j], func=AF.Square,
                                                                accum_out=ssq[j]), r=[("o_sb", j)], w=["junk", ("ssq", j)])
                        P.op('act', lambda e, j=j: e.activation(out=ssq[j], in_=ssq[j], func=AF.Sqrt, bias=epsb[0:64, :],
                                                                scale=1.0 / 512), r=[("ssq", j), "epsb"], w=[("ssq", j)])
                        P.op('dve', lambda e, j=j: e.reciprocal(out=ssq[j], in_=ssq[j]), r=[("ssq", j)], w=[("ssq", j)])
                        P.op('dve', lambda e, j=j, c=c: e.scalar_tensor_tensor(
                            out=res[j], in0=o_sb[j], scalar=ssq[j], in1=sr_g[:, c, :], op0=ALU.mult, op1=ALU.mult),
                            r=[("o_sb", j), ("ssq", j), "sr_g"], w=[("res", j)])
                        pv = ps[pa][:, 0:128].bitcast(BF16).rearrange("p (a t) -> p a t", a=4)
                        for a in range(4):
                            P.op('pe', lambda e, pv=pv, a=a, j=j: e.transpose(
                                pv[:, a, :], res[j][:, a * 128:(a + 1) * 128], identb[0:64, 0:64]),
                                r=[("res", j), "identb"], w=[PSK[pa]])
                        P.op('act', lambda e, pv=pv, cs=cs: e.activation(out=gcT[:, :, cs], in_=pv, func=AF.Copy),
                             r=[PSK[pa]], w=["gcT"])
                if lat and dirn == 1:
                    qt, c0 = tok0 // 1024, tok0 % 1024
                    P.op('pool', lambda e, qt=qt, c0=c0: e.dma_start(
                        out=mixT_d[qt * 768:qt * 768 + 512, c0:c0 + 512].rearrange("(a p) t -> p a t", p=128),
                        in_=gcT), r=["gcT"], w=["mixT_d"], dma=True)

    def na():
        NB = 4
        nq = A.bf16(SEQ, parts=64)
        nk = A.bf16(NS, parts=64)
        Ve = A.bf16(34 * 64).rearrange("p (b d) -> p b d", b=34)
        Vo = A.bf16(31 * 64).rearrange("p (b d) -> p b d", b=31)
        TBh = A.f32(15 * 64, parts=64)
        naT = A.bf16(SEQ, parts=64)
        s_all = [A.f32(768, parts=64) for _ in range(NB)]
        Pm = [A.bf16(768, parts=64) for _ in range(NB)]
        PT = [A.bf16(6 * 64).rearrange("p (b q) -> p b q", b=6) for _ in range(NB)]
        osb = [A.bf16(64, parts=64) for _ in range(NB)]
        sm = [A.f32(8, parts=64) for _ in range(NB)]
        yield
        for hh in range(4):
            hs = slice(hh * 64, (hh + 1) * 64)
            P.op('sp', lambda e, hs=hs: e.dma_start(out=nq, in_=nqT_d[hs, 256:NS]), r=["qkg_d"], w=["nq"], dma=True)
            P.op('sp', lambda e, hs=hs: e.dma_start(out=nk, in_=nkT_d[hs, :]), r=["qkg_d"], w=["nk"], dma=True)
            P.op('sp', lambda e, hs=hs: e.dma_start(
                out=Ve, in_=nv_d[:, hs].rearrange("(b p) d -> p b d", p=128)), r=["qkg_d"], w=["Ve"], dma=True)
            P.op('sp', lambda e, hs=hs: e.dma_start(
                out=Vo, in_=nv_d[320:320 + 31 * 128, hs].rearrange("(b p) d -> p b d", p=128)),
                r=["qkg_d"], w=["Vo"], dma=True)
            P.op('sp', lambda e, hh=hh: e.dma_start(out=TBh, in_=tb_in[hh, :, :]), w=["TBh"], dma=True)
            for rb in range(0, 64, NB):
                yield
                rows = []
                for j in range(NB):
                    r = rb + j
                    j0 = min(max(r - 4, 0), 56)
                    rows.append(dict(j=j, r=r, j0=j0, s0=j0 - r + 7, qs=slice(r * 64, (r + 1) * 64),
                                     bl=2 * j, bm=2 * j + 1, KL=("nl", j), KC=("nc", j), KT=("nt", j), KO=("no", j)))
                for R in rows:
                    P.op('pe', lambda e, R=R: e.matmul(
                        ps[R['bl']][0:64, 0:512], lhsT=nq[:, R['qs']],
                        rhs=nk[:, 256 + R['j0'] * 64:256 + R['j0'] * 64 + 512], start=True, stop=True),
                        r=["nq", "nk"], w=[R['KL']])
                    P.op('pe', lambda e, R=R: e.matmul(
                        ps[R['bm']][0:64, 0:256], lhsT=nq[:, R['qs']], rhs=nk[:, 0:256], start=True, stop=True),
                        r=["nq", "nk"], w=[R['KC'], R['KT'], R['KO']])
                for R in rows:
                    j = R['j']
                    P.op('dve', lambda e, R=R, j=j: e.tensor_tensor(
                        out=s_all[j][:, 0:512], in0=ps[R['bl']][0:64, 0:512],
                        in1=TBh[:, R['s0'] * 64:(R['s0'] + 8) * 64], op=ALU.add),
                        r=[R['KL'], "TBh"], w=[("s_lat", j)])
                    P.op('act', lambda e, R=R, j=j: e.activation(
                        out=s_all[j][:, 512:768], in_=ps[R['bm']][0:64, 0:256], func=AF.Copy),
                        r=[R['KC']], w=[("s_ctx", j)])
                for R in rows:
                    j = R['j']
                    P.op('dve', lambda e, j=j: e.reduce_max(out=sm[j][:, 0:1], in_=s_all[j], axis=mybir.AxisListType.X),
                         r=[("s_lat", j), ("s_ctx", j)], w=[("sm", j)])
                for R in rows:
                    j = R['j']
                    P.op('dve', lambda e, j=j: e.tensor_scalar(out=sm[j][:, 1:2], in0=sm[j][:, 0:1], scalar1=-1.0,
                                                               scalar2=None, op0=ALU.mult), r=[("sm", j)], w=[("nm", j)])
                for R in rows:
                    j = R['j']
                    P.op('act', lambda e, j=j: e.activation(out=Pm[j], in_=s_all[j], func=AF.Exp,
                                                            bias=sm[j][:, 1:2], scale=1.0, accum_out=sm[j][:, 2:3]),
                         r=[("s_lat", j), ("s_ctx", j), ("nm", j)], w=[("Pm", j), ("sum", j)])
                for R in rows:
                    j = R['j']
                    pv = ps[R['bm']][:, 256:448].bitcast(BF16).rearrange("p (b q) -> p b q", b=6)
                    R['pv'] = pv
                    for blk in range(6):
                        P.op('pe', lambda e, pv=pv, blk=blk, j=j: e.transpose(
                            pv[:, blk, :], Pm[j][:, blk * 128:(blk + 1) * 128], identb[0:64, 0:64]),
                            r=[("Pm", j), "identb"], w=[R['KT']])
                for R in rows:
                    j = R['j']
                    P.op('dve', lambda e, pv=R['pv'], j=j: e.tensor_copy(out=PT[j], in_=pv), r=[R['KT']], w=[("PT", j)])
                for R in rows:
                    j, j0 = R['j'], R['j0']
                    for blk in range(6):
                        if blk < 4:
                            vsrc = Ve[:, 2 + j0 // 2 + blk, :] if j0 % 2 == 0 else Vo[:, (j0 - 1) // 2 + blk, :]
                        else:
                            vsrc = Ve[:, blk - 4, :]
                        P.op('pe', lambda e, R=R, blk=blk, j=j, vsrc=vsrc: e.matmul(
                            ps[R['bm']][0:64, 448:512], lhsT=PT[j][:, blk, :], rhs=vsrc, start=(blk == 0), stop=(blk == 5)),
                            r=[("PT", j), "Ve", "Vo"], w=[R['KO']])
                for R in rows:
                    j = R['j']
                    P.op('dve', lambda e, j=j: e.reciprocal(out=sm[j][:, 3:4], in_=sm[j][:, 2:3]),
                         r=[("sum", j)], w=[("rs", j)])
                for R in rows:
                    j = R['j']
                    P.op('dve', lambda e, j=j, R=R: e.tensor_scalar(out=osb[j], in0=ps[R['bm']][0:64, 448:512],
                                                                   scalar1=sm[j][:, 3:4], scalar2=None, op0=ALU.mult),
                         r=[R['KO'], ("rs", j)], w=[("osb", j)])
                for R in rows:
                    j = R['j']
                    pv2 = ps[R['bm']][0:64, 256:288].bitcast(BF16)
                    R['pv2'] = pv2
                    P.op('pe', lambda e, pv2=pv2, j=j: e.transpose(pv2, osb[j], identb[0:64, 0:64]),
                         r=[("osb", j), "identb"], w=[R['KT']])
                for R in rows:
                    P.op('act', lambda e, R=R: e.activation(out=naT[:, R['qs']], in_=R['pv2'], func=AF.Copy),
                         r=[R['KT']], w=["naT"])
            for qt in range(4):
                P.op('pool', lambda e, qt=qt, hh=hh: e.dma_start(
                    out=mixT_d[qt * 768 + 512 + hh * 64:qt * 768 + 512 + (hh + 1) * 64, :],
                    in_=naT[:, qt * 1024:(qt + 1) * 1024]), r=["naT"], w=["mixT_d"], dma=True)

    def merge_out():
        A.reset(0)
        MX = A.bf16(24 * NL).rearrange("p (c t) -> p c t", c=24)
        hmo = A.bf16(KC * NL).rearrange("p (k t) -> p k t", k=KC)
        MG = A.bf16(KC * NL).rearrange("p (k t) -> p k t", k=KC)
        mark = A.off
        selS = A.f32(4)
        tmpx = [A.bf16(6 * NL) for _ in range(2)]
        P.op('sp', lambda e: e.dma_start(out=selS, in_=sel_in[:, :]), w=["selS"], dma=True)
        P.op('sp', lambda e: e.dma_start(
            out=hmo, in_=hmT_d.ap().rearrange("(k p) t -> p k t", p=128)[:, :, 0:NL]), w=["hmo"], dma=True)
        n = 0
        for r in range(4):
            dst = MX[:, r * 6:(r + 1) * 6, :].rearrange("p c t -> p (c t)")
            for q in range(4):
                tb_ = tmpx[n % 2]
                tk = ("tmpx", n % 2)
                n += 1
                for hf in range(2):
                    r0 = (2 * q + hf) * 1536 + r * 384
                    P.op('sp', lambda e, tb_=tb_, r0=r0, hf=hf: e.dma_start(
                        out=tb_.rearrange("p (c t) -> p c t", c=6)[:, 3 * hf:3 * hf + 3, :],
                        in_=mixin_d[r0:r0 + 384, :].rearrange("(c p) t -> p c t", p=128)), w=[tk], dma=True)
                if q == 0:
                    P.op('dve', lambda e, dst=dst, tb_=tb_, q=q: e.tensor_scalar(
                        out=dst, in0=tb_, scalar1=selS[:, q:q + 1], scalar2=None, op0=ALU.mult),
                        r=[tk, "selS"], w=["MX"])
                else:
                    P.op('dve', lambda e, dst=dst, tb_=tb_, q=q: e.scalar_tensor_tensor(
                        out=dst, in0=tb_, scalar=selS[:, q:q + 1], in1=dst, op0=ALU.mult, op1=ALU.add),
                        r=[tk, "selS", "MX"], w=["MX"])
        P.barrier()
        A.reset(mark)
        wsf = [A.f32(KC * 128).rearrange("p (k c) -> p k c", k=KC) for _ in range(4)]
        wsb = [A.bf16(KC * 128).rearrange("p (k c) -> p k c", k=KC) for _ in range(4)]
        sgm = [A.f32(512) for _ in range(4)]
        wgo = w_gla_o.ap().rearrange("(i p) d -> p i d", p=128)
        wno = w_na_o.ap().rearrange("(i p) d -> p i d", p=128)
        wmv = w_m.ap().rearrange("(k p) c -> p k c", p=128)
        it = 0
        for dc in range(KC):
            dsl = slice(dc * 128, (dc + 1) * 128)
            srcs = [(wgo[:, :, dsl], 16), (wno[:, :, dsl], 8), (wmv[:, :, dsl], 16),
                    (wmv[:, :, 2048 + dc * 128:2048 + (dc + 1) * 128], 16)]
            for wi, (src, ni) in enumerate(srcs):
                P.op('sp', lambda e, wi=wi, src=src, ni=ni: e.dma_start(out=wsf[wi][:, 0:ni, :], in_=src),
                     w=[("wsf", wi)], dma=True)
                P.op('pool', lambda e, wi=wi, ni=ni: e.tensor_copy(out=wsb[wi][:, 0:ni, :], in_=wsf[wi][:, 0:ni, :]),
                     r=[("wsf", wi)], w=[("wsb", wi)])
            for grp in range(2):
                gs = slice(grp * 512, (grp + 1) * 512)
                base = 4 * (it % 2)
                it += 1
                pa, pbk, p1, p2 = base, base + 1, base + 2, base + 3
                for i in range(16):
                    P.op('pe', lambda e, pa=pa, i=i, gs=gs: e.matmul(
                        ps[pa][:, :], lhsT=wsb[0][:, i, :], rhs=MX[:, (i // 4) * 6 + i % 4, gs],
                        start=(i == 0), stop=(i == 15)), r=[("wsb", 0), "MX"], w=[PSK[pa]])
                for i in range(8):
                    P.op('pe', lambda e, pbk=pbk, i=i, gs=gs: e.matmul(
                        ps[pbk][:, :], lhsT=wsb[1][:, i, :], rhs=MX[:, (i // 2) * 6 + 4 + i % 2, gs],
                        start=(i == 0), stop=(i == 7)), r=[("wsb", 1), "MX"], w=[PSK[pbk]])
                for (pm, wi) in ((p1, 2), (p2, 3)):
                    for k in range(KC):
                        P.op('pe', lambda e, pm=pm, wi=wi, k=k, gs=gs: e.matmul(
                            ps[pm][:, :], lhsT=wsb[wi][:, k, :], rhs=hmo[:, k, gs],
                            start=(k == 0), stop=(k == KC - 1)), r=[("wsb", wi), "hmo"], w=[PSK[pm]])
                P.op('act', lambda e, p1=p1: e.activation(out=sgm[0], in_=ps[p1][:, :], func=AF.Sigmoid),
                     r=[PSK[p1]], w=[("sgm", 0)])
                P.op('act', lambda e, p2=p2: e.activation(out=sgm[1], in_=ps[p2][:, :], func=AF.Sigmoid),
                     r=[PSK[p2]], w=[("sgm", 1)])
                P.op('dve', lambda e, pa=pa: e.tensor_tensor(out=sgm[2], in0=sgm[0], in1=ps[pa][:, :], op=ALU.mult),
                     r=[("sgm", 0), PSK[pa]], w=[("sgm", 2)])
                P.op('dve', lambda e, pbk=pbk: e.tensor_tensor(out=sgm[3], in0=sgm[1], in1=ps[pbk][:, :], op=ALU.mult),
                     r=[("sgm", 1), PSK[pbk]], w=[("sgm", 3)])
                P.op('pool', lambda e, dc=dc, gs=gs: e.tensor_tensor(out=MG[:, dc, gs], in0=sgm[2], in1=sgm[3],
                                                                      op=ALU.add),
                     r=[("sgm", 2), ("sgm", 3)], w=["MG"])
        P.barrier()
        A.reset(mark)
        wsf2 = [A.f32(KC * 128).rearrange("p (k c) -> p k c", k=KC) for _ in range(2)]
        wsb2 = [A.bf16(KC * 128).rearrange("p (k c) -> p k c", k=KC) for _ in range(2)]
        ysb = [A.f32(NT) for _ in range(2)]
        hsb = [A.f32(NT) for _ in range(2)]
        ysq = [A.bf16(NT) for _ in range(2)]
        rstd = A.f32(NT)
        wov = w_out.ap().rearrange("(k p) d -> p k d", p=128)
        it = 0
        for dc in range(KC):
            i = dc % 2
            P.op('sp', lambda e, i=i, dc=dc: e.dma_start(out=wsf2[i], in_=wov[:, :, dc * 128:(dc + 1) * 128]),
                 w=[("wsf", i)], dma=True)
            P.op('pool', lambda e, i=i: e.tensor_copy(out=wsb2[i], in_=wsf2[i]), r=[("wsf", i)], w=[("wsb", i)])
            for grp in range(2):
                gs = slice(grp * 512, (grp + 1) * 512)
                pb = 2 + (it % 6)
                it += 1
                for k in range(KC):
                    P.op('pe', lambda e, pb=pb, k=k, gs=gs, i=i: e.matmul(
                        ps[pb][:, :], lhsT=wsb2[i][:, k, :], rhs=MG[:, k, gs], start=(k == 0), stop=(k == KC - 1)),
                        r=[("wsb", i), "MG"], w=[PSK[pb]])
                P.op('act', lambda e, pb=pb, gs=gs, i=i: e.activation(out=ysb[i][:, gs], in_=ps[pb][:, :], func=AF.Copy),
                     r=[PSK[pb]], w=[("ysb", i)])
                P.op('act', lambda e, pb=pb, gs=gs, i=i: e.activation(out=ysq[i][:, gs], in_=ps[pb][:, :], func=AF.Square),
                     r=[PSK[pb]], w=[("ysq", i)])
            for grp in range(2):
                gs = slice(grp * 512, (grp + 1) * 512)
                P.op('pe', lambda e, grp=grp, gs=gs, i=i, dc=dc: e.matmul(
                    ps[grp][:, :], lhsT=ones_bf, rhs=ysq[i][:, gs], start=(dc == 0), stop=(dc == KC - 1)),
                    r=[("ysq", i), "ones"], w=[PSK[grp]])
            P.op('act', lambda e, i=i, dc=dc: e.dma_start(out=yT_d[dc, :, 0:NL], in_=ysb[i][:, 0:NL]),
                 r=[("ysb", i)], w=[("yT_d", dc)], dma=True)
        for grp in range(2):
            gs = slice(grp * 512, (grp + 1) * 512)
            P.op('act', lambda e, grp=grp, gs=gs: e.activation(
                out=rstd[:, gs], in_=ps[grp][:, :], func=AF.Sqrt, bias=epsb, scale=1.0 / D),
                r=[PSK[grp], "epsb"], w=["rstd2"])
        P.op('dve', lambda e: e.reciprocal(out=rstd[:, 0:NL], in_=rstd[:, 0:NL]), r=["rstd2"], w=["rstd2"])
        residual_update(1, rstd, ysb, hsb, ncols=NL)

    def final_out():
        A.reset(0)
        HT = A.f32(KC * NL).rearrange("p (k t) -> p k t", k=KC)
        xo = [A.f32(D) for _ in range(2)]
        for dc in range(KC):
            P.op('sp', lambda e, dc=dc: e.dma_start(out=HT[:, dc, :], in_=hT_d[dc, :, 0:NL]),
                 r=[("hT_d", dc)], w=["HT"], dma=True)
        it = 0
        for tt in range(8):
            xb = xo[tt % 2]
            xk = ("xo", tt % 2)
            for kq in range(4):
                pb = it % 8
                it += 1
                for kk in range(4):
                    k = kq * 4 + kk
                    P.op('pe', lambda e, pb=pb, kk=kk, k=k, tt=tt: e.transpose(
                        ps[pb][:, kk * 128:(kk + 1) * 128], HT[:, k, tt * 128:(tt + 1) * 128], ident),
                        r=["HT", "ident"], w=[PSK[pb]])
                if kq % 2 == 0:
                    P.op('act', lambda e, pb=pb, xb=xb, kq=kq: e.activation(
                        out=xb[:, kq * 512:(kq + 1) * 512], in_=ps[pb][:, :], func=AF.Copy), r=[PSK[pb]], w=[xk])
                else:
                    P.op('dve', lambda e, pb=pb, xb=xb, kq=kq: e.tensor_copy(
                        out=xb[:, kq * 512:(kq + 1) * 512], in_=ps[pb][:, :]), r=[PSK[pb]], w=[xk])
            P.op('sp', lambda e, xb=xb, tt=tt: e.dma_start(out=out[tt * 128:(tt + 1) * 128, :], in_=xb),
                 r=[xk], w=["out"], dma=True)

    if mode == 'full':
        prenorm_stage(0, True)
        ffn_core(0, 0)

    if 'h1T' in dbg:
        for dc in range(KC):
            P.op('sp', lambda e, dc=dc: e.dma_start(out=dbg['h1T'][dc, :, :], in_=hT_d[dc, :, :]),
                 r=[("hT_d", dc)], dma=True)

    if mode == 'full':
        hm = prenorm_stage(1, False)
    for c in range(8):
        if mode == 'full':
            P.op('sp', lambda e, c=c: e.dma_start(
                out=hmT_d[c * 256:(c + 1) * 256, :].rearrange("(k p) t -> p k t", p=128), in_=hm[:, 2 * c:2 * c + 2, :]),
                r=["hm"], w=[("hmT_d", c)], dma=True)
        else:
            P.op('sp', lambda e, c=c: e.dma_start(out=hmT_d[c * 256:(c + 1) * 256, :],
                                                   in_=hm_in[c * 256:(c + 1) * 256, :]), w=[("hmT_d", c)], dma=True)
        P.op('pool', lambda e, c=c: e.collective_compute(
            "AllGather", ALU.bypass, replica_groups=groups,
            ins=[hmT_d[c * 256:(c + 1) * 256, :].opt()], outs=[hmT_all[c * 1024:(c + 1) * 1024, :].opt()]),
            r=[("hmT_d", c)], w=["hmT_all"], cc=True)
    P.barrier()
    STG = os.environ.get("K_STAGES", "inproj,gla,na,ag2").split(",")
    if "inproj" in STG:
        mixer_inproj()
    for nm_, fn_ in (("gla", gla), ("na", na)):
        if nm_ in STG:
            A.reset(0)
            for _ in fn_():
                pass
            P.barrier()
    if "ag2" in STG:
        for c in range(8):
            P.op('pool', lambda e, c=c: e.collective_compute(
                "AllGather", ALU.bypass, replica_groups=groups,
                ins=[mixT_d[c * 384:(c + 1) * 384, :].opt()], outs=[mixin_d[c * 1536:(c + 1) * 1536, :].opt()]),
                w=["mixin"], cc=True)
    P.barrier()
    if 'mixT' in dbg:
        P.op('sp', lambda e: e.dma_start(out=dbg['mixT'][:, :], in_=mixT_d[:, :]), dma=True)
        P.barrier()
    if mode == 'full':
        merge_out()
        prenorm_stage(2, False)
        ffn_core(2, 1)
        final_out()
    P.barrier()
    P.emit(nc, st)
    return nc, st


def _consts():
    th = 10000.0
    freqs = th ** (-np.arange(64, dtype=np.float32) / 64.0)
    t = np.arange(SEQ)
    row, col = (t // 64).astype(np.float32), (t % 64).astype(np.float32)
    ropeC = np.zeros((2, 128, SEQ), np.float32)
    ropeS = np.zeros((2, 128, SEQ), np.float32)
    for c, pos in enumerate((row, col)):
        ang = pos[None, :] * freqs[:, None]
        ropeC[c, :64], ropeC[c, 64:] = np.cos(ang), np.cos(ang)
        ropeS[c, :64], ropeS[c, 64:] = np.sin(ang), np.sin(ang)
    rotT = np.zeros((128, 128), np.float32)
    for m in range(64):
        rotT[m + 64, m] = -1.0
        rotT[m, m + 64] = 1.0
    cmask = np.ones((128, 512), np.float32)
    cmask[:, ::64] = 0.0
    ii = np.arange(64)
    trm = np.zeros((64, 2, 64), np.float32)
    trm[:, 0, :] = (ii[None, :] >= ii[:, None])
    trm[:, 1, :] = (ii[None, :] <= ii[:, None])
    return dict(ropeC=ropeC, ropeS=ropeS, rotT=rotT, cmask=cmask, trm=np.ascontiguousarray(trm.reshape(64, 128)),
                identf=np.eye(128, dtype=np.float32), identb=np.eye(128).astype(ml_dtypes.bfloat16))


def _prep_inputs(inputs, ncores=8, mode='full', hm_in=None):
    f = lambda k: np.asarray(inputs[k], np.float32)
    x, c, ctx, c_ctx = f('x'), f('c'), f('ctx'), f('c_ctx')
    w_in = f('w_in')[0]
    gla_wg, gla_bg, gng = f('gla_wg')[0], f('gla_bg')[0], f('gla_norm_g')[0]
    rpb = f('na_rpb')[0]
    shared = _consts()
    shared['w_m'] = np.ascontiguousarray(w_in[:, 9248:13344])
    shared['gng'] = np.ascontiguousarray(gng.reshape(1, 512))
    shared['w_gla_o'] = np.ascontiguousarray(f('w_gla_o')[0])
    shared['w_na_o'] = np.ascontiguousarray(f('w_na_o')[0])
    shared['w_out'] = np.ascontiguousarray(f('w_out')[0])
    w_ada_full = f('w_ada')[0]
    bT_full = np.ascontiguousarray(f('b_ada')[0].reshape(144, 128).T)
    shared['gT'] = np.ascontiguousarray(f('norm_g')[0].reshape(6, KC, 128).transpose(2, 0, 1))
    shared['ffn_wg'] = np.ascontiguousarray(f('ffn_wg')[0])
    shared['ffn_wu'] = np.ascontiguousarray(f('ffn_wu')[0])
    shared['ffn_wd'] = np.ascontiguousarray(f('ffn_wd')[0])
    q = np.arange(64)
    k = np.arange(64)
    dj = np.clip(k[None, :] - q[:, None], -15, 15) + 15
    cs = np.clip(q - 8, 0, 48)
    ok = (k[None, :] >= cs[:, None]) & (k[None, :] < cs[:, None] + 16)
    in_maps = []
    for core in range(ncores):
        b, h = core // 4, core % 4
        s = h
        m = dict(shared)
        xin = np.concatenate([x[b, s * 1024:(s + 1) * 1024], ctx[b, s * 64:(s + 1) * 64]], axis=0)
        m['xin'] = np.ascontiguousarray(xin)
        m['w_ada'] = np.ascontiguousarray(w_ada_full[:, s * 4608:(s + 1) * 4608])
        m['bT'] = np.ascontiguousarray(bT_full[:, s * 36:(s + 1) * 36])
        selv = np.zeros((128, 4), np.float32)
        selv[:, s] = 1.0
        m['sel'] = selv
        cin = np.stack([c[b], c_ctx], axis=0)
        m['cT'] = np.ascontiguousarray(cin.reshape(2, KC, 128).transpose(2, 1, 0))
        cols = np.concatenate([
            np.arange(h * 256, (h + 1) * 256), 1024 + np.arange(h * 256, (h + 1) * 256),
            np.arange(6144, 6176),
            6176 + np.arange(h * 256, (h + 1) * 256), 7200 + np.arange(h * 256, (h + 1) * 256),
            2048 + np.arange(h * 512, (h + 1) * 512), 4096 + np.arange(h * 512, (h + 1) * 512),
            8224 + np.arange(h * 256, (h + 1) * 256)])
        m['w_inh'] = np.ascontiguousarray(w_in[:, cols])
        m['gwg'] = np.ascontiguousarray(gla_wg[:, :, h * 256:(h + 1) * 256].transpose(1, 0, 2))
        m['gbgT'] = np.ascontiguousarray(gla_bg[:, h * 256:(h + 1) * 256].reshape(2, 2, 128).transpose(2, 0, 1).reshape(128, 4))
        tbl = np.empty((4, 64, 15, 64), np.float32)
        for hh in range(4):
            g = rpb[4 * h + hh][:, dj]
            tbl[hh] = np.where(ok[None], g, np.float32(-1e30)).transpose(1, 0, 2)
        m['tb'] = np.ascontiguousarray(tbl.reshape(4, 64, 15 * 64))
        if mode != 'full':
            m['hm_in'] = hm_in[core]
            for kk in ('xin', 'cT', 'w_ada', 'bT', 'ffn_wg', 'ffn_wu', 'ffn_wd'):
                m[kk] = np.zeros((1,), np.float32)
        in_maps.append(m)
    return in_maps


def kernel(**inputs):
    nc, st = build()
    in_maps = _prep_inputs(inputs)
    res = run_bass_kernel_spmd(nc, in_maps, core_ids=list(range(8)))
    outp = np.zeros((2, SEQ, D), np.float32)
    for core in range(8):
        b, s = core // 4, core % 4
        outp[b, s * 1024:(s + 1) * 1024] = res.results[core]["out"]
    return outp
```

```python
import os
import contextlib
import numpy as np
import ml_dtypes
import concourse.bass as bass
import concourse.mybir as mybir
from concourse.bass_utils import run_bass_kernel_spmd

F32 = mybir.dt.float32
BF16 = mybir.dt.bfloat16
AF = mybir.ActivationFunctionType
ALU = mybir.AluOpType

D = 2048
KC = 16
DFF = 5632
FC = 44
NT = 1088
NL = 1024
GROUPS = [(0, 512), (512, 512), (1024, 64)]
EPS = 1e-6
SEQ = 4096
CTX = 256
NS = SEQ + CTX


class Prog:
    def __init__(self):
        self.ops = []
        self.lastw = {}
        self.readers = {}

    def op(self, eng, fn, r=(), w=(), dma=False, cc=False):
        i = len(self.ops)
        deps = set()
        for k in r:
            if k in self.lastw:
                deps.add(self.lastw[k])
        for k in w:
            if k in self.lastw:
                deps.add(self.lastw[k])
            deps.update(self.readers.get(k, ()))
        for k in r:
            self.readers.setdefault(k, []).append(i)
        for k in w:
            self.lastw[k] = i
            self.readers[k] = []
        self.ops.append(dict(eng=eng, fn=fn, deps=deps, dma=dma, cc=cc, barrier=False))
        return i

    def barrier(self):
        self.ops.append(dict(barrier=True))
        self.lastw = {}
        self.readers = {}

    def emit(self, nc, st):
        ops = self.ops
        engs = ('pe', 'act', 'dve', 'pool', 'sp')
        needed = set()
        for o in ops:
            if not o['barrier']:
                needed |= o['deps']
        last = {}
        for i, o in enumerate(ops):
            if o['barrier']:
                for e in engs:
                    if e in last:
                        needed.add(last[e])
                last = {}
            elif not o['dma'] and not o['cc']:
                last[o['eng']] = i
        esem = {e: st.enter_context(nc.semaphore("es_" + e)) for e in engs}
        NQ = 20
        dpool = {q: [st.enter_context(nc.semaphore("dq_%s_%d" % (q, j))) for j in range(NQ)]
                 for q in ('sp', 'pool', 'act')}
        dcnt = {}
        rr = {q: 0 for q in dpool}
        tick = {e: 0 for e in engs}
        ccs = []
        for i, o in enumerate(ops):
            if o['barrier']:
                o['ticks'] = dict(tick)
                o['dmas'] = dict(dcnt)
                o['ccs'] = list(ccs)
            elif o['cc']:
                s = st.enter_context(nc.semaphore("cc_%d" % i))
                o['sem'] = s
                o['val'] = 1
                o['prev'] = 0
                ccs.append(s)
            elif o['dma']:
                q = o['eng']
                s = dpool[q][rr[q] % NQ]
                rr[q] += 1
                o['prev'] = dcnt.get(s, 0)
                dcnt[s] = o['prev'] + 16
                o['sem'] = s
                o['val'] = dcnt[s]
            elif i in needed:
                tick[o['eng']] += 1
                o['tick'] = tick[o['eng']]

        def run(E, e):
            waited = {}

            def w(sem, val):
                key = id(sem)
                if waited.get(key, 0) < val:
                    e.wait_ge(sem, val)
                    waited[key] = val

            for i, o in enumerate(ops):
                if o['barrier']:
                    for F in engs:
                        if o['ticks'][F] > 0:
                            w(esem[F], o['ticks'][F])
                    for s, c in o['dmas'].items():
                        w(s, c)
                    for s in o['ccs']:
                        w(s, 1)
                    continue
                if o['eng'] != E:
                    continue
                for d in sorted(o['deps']):
                    od = ops[d]
                    if od['dma'] or od['cc']:
                        w(od['sem'], od['val'])
                    else:
                        if od['eng'] == E and E == 'pe':
                            continue
                        w(esem[od['eng']], od['tick'])
                if (o['dma']) and o['prev'] > 0:
                    w(o['sem'], o['prev'])
                ins = o['fn'](e)
                if o['cc']:
                    ins.then_inc(o['sem'])
                elif o['dma']:
                    ins.then_inc(o['sem'], 16)
                elif 'tick' in o:
                    ins.then_inc(esem[E], 1)

        with nc.Block() as block:
            @block.tensor
            def _(e):
                run('pe', e)

            @block.scalar
            def _(e):
                run('act', e)

            @block.vector
            def _(e):
                run('dve', e)

            @block.gpsimd
            def _(e):
                run('pool', e)

            @block.sync
            def _(e):
                run('sp', e)


class Arena:
    def __init__(self, t, nwords):
        self.t = t
        self.n = nwords
        self.off = 0

    def reset(self, off=0):
        self.off = off

    def f32(self, n, parts=128):
        a = self.t[0:parts, self.off:self.off + n]
        self.off += n
        assert self.off <= self.n, ("arena overflow", self.off)
        return a

    def bf16(self, n, parts=128):
        words = (n + 1) // 2
        a = self.t[0:parts, self.off:self.off + words].bitcast(BF16)
        self.off += words
        assert self.off <= self.n, ("arena overflow", self.off)
        return a


def build(mode='full', debug=(), ncores=8):
    nc = bass.Bass("TRN2", target_bir_lowering=False)
    P = Prog()
    st = contextlib.ExitStack()

    def din(name, shape, dt=F32):
        return nc.dram_tensor(name, list(shape), dt, kind="ExternalInput")

    def dint(name, shape, dt=F32):
        return nc.dram_tensor(name, list(shape), dt)

    full = (mode == 'full')
    xin = din("xin", [NT, D] if full else [1])
    cT = din("cT", [128, KC, 2] if full else [1])
    w_ada = din("w_ada", [D, 4608] if full else [1])
    bT = din("bT", [128, 36] if full else [1])
    gT = din("gT", [128, 6, KC])
    ffn_wg = din("ffn_wg", [2, D, DFF] if full else [1])
    ffn_wu = din("ffn_wu", [2, D, DFF] if full else [1])
    ffn_wd = din("ffn_wd", [2, DFF, D] if full else [1])
    identf = din("identf", [128, 128])
    groups = [[0, 1, 2, 3], [4, 5, 6, 7]] if ncores == 8 else [[0, 1, 2, 3]]
    w_inh = din("w_inh", [D, 2336])
    w_m = din("w_m", [D, 4096])
    gwg = din("gwg", [16, 2, 256])
    gbgT = din("gbgT", [128, 4])
    gng = din("gng", [1, 512])
    ropeC = din("ropeC", [2, 128, SEQ])
    ropeS = din("ropeS", [2, 128, SEQ])
    rotT_in = din("rotT", [128, 128])
    cmask_in = din("cmask", [128, 512])
    trm_in = din("trm", [64, 128])
    tb_in = din("tb", [4, 64, 15 * 64])
    w_gla_o = din("w_gla_o", [D, D])
    w_na_o = din("w_na_o", [1024, D])
    w_out = din("w_out", [D, D])
    identb_in = din("identb", [128, 128], BF16)
    hm_in = din("hm_in", [D, NT], BF16) if mode != 'full' else None
    hmT_d = dint("hmT_d", [D, NT], BF16)
    modp_d = dint("modp_d", [128, 72])
    modall_d = dint("modall_d", [512, 72])
    hmT_all = dint("hmT_all", [8 * 1024, NT], BF16)
    qT_d = dint("qT_d", [256, NS])
    kT_d = dint("kT_d", [256, NS])
    gT_d = dint("gT_d", [32, NS])
    v_d = dint("v_d", [NS, 512], BF16)
    sr_d = dint("sr_d", [NS, 512])
    nv_d = dint("nv_d", [NS, 256], BF16)
    nqT_d = dint("nqT_d", [256, NS], BF16)
    nkT_d = dint("nkT_d", [256, NS], BF16)
    of_d = dint("of_d", [SEQ, 512])
    mixT_d = dint("mixT_d", [4 * 768, 1024], BF16)
    mixin_d = dint("mixin_d", [8 * 1536, 1024], BF16)
    sel_in = din("sel", [128, 4])
    out = nc.dram_tensor("out", [NL, D], F32, kind="ExternalOutput")

    hT_d = dint("hT_d", [KC, 128, NT])
    yT_d = dint("yT_d", [KC, 128, NT])
    dbg = {}
    for name, shape, dt in debug:
        dbg[name] = nc.dram_tensor("dbg_" + name, list(shape), dt, kind="ExternalOutput")

    AW = 50400
    arena_t = st.enter_context(nc.sbuf_tensor("arena", [128, AW], F32))
    A = Arena(arena_t, AW)
    cst = st.enter_context(nc.sbuf_tensor("cst", [128, 768], F32))
    ident = cst[:, 0:128]
    modT = cst[:, 128:128 + 288].rearrange("p (r j) -> p r j", r=2)
    Amod = cst[:, 416:416 + 96].rearrange("p (r s k) -> p r s k", r=2, s=3)
    Gmod = cst[:, 512:512 + 96].rearrange("p (r s k) -> p r s k", r=2, s=3)
    gTs = cst[:, 608:608 + 96].rearrange("p (s k) -> p s k", s=6)
    cst_b = st.enter_context(nc.sbuf_tensor("cst_b", [128, 256], BF16))
    ones_bf = cst_b[:, 0:128]
    identb = cst_b[:, 128:256]
    cst2 = st.enter_context(nc.sbuf_tensor("cst2", [128, 1800], F32))
    rotT = cst2[:, 0:128]
    cmask = cst2[:, 128:640]
    trm = cst2[0:64, 640:768].rearrange("p (a b) -> p a b", a=2)
    nbg = cst2[:, 768:772]
    one1 = cst2[:, 772:773]
    wgT = cst2[0:16, 776:776 + 512].rearrange("p (a b) -> p a b", a=2)
    gnb = cst2[0:64, 1288:1288 + 512]
    ps = [st.enter_context(nc.psum_tensor("ps%d" % i, [128, 512], F32)) for i in range(8)]
    PSK = [("ps", i) for i in range(8)]

    P.op('sp', lambda e: e.dma_start(out=ident, in_=identf[:, :]), w=["ident"], dma=True)
    P.op('sp', lambda e: e.dma_start(out=gTs, in_=gT[:, :, :]), w=["gTs"], dma=True)
    P.op('pool', lambda e: e.memset(ones_bf, 1.0), w=["ones"])
    epsb = cst[:, 704:705]
    P.op('pool', lambda e: e.memset(epsb, EPS), w=["epsb"])
    P.op('sp', lambda e: e.dma_start(out=identb, in_=identb_in[:, :]), w=["identb"], dma=True)
    P.op('sp', lambda e: e.dma_start(out=rotT, in_=rotT_in[:, :]), w=["rotT"], dma=True)
    P.op('sp', lambda e: e.dma_start(out=cmask, in_=cmask_in[:, :]), w=["cmask"], dma=True)
    P.op('sp', lambda e: e.dma_start(out=cst2[0:64, 640:768], in_=trm_in[:, :]), w=["trm"], dma=True)
    P.op('sp', lambda e: e.dma_start(out=nbg, in_=gbgT[:, :]), w=["nbg"], dma=True)
    P.op('dve', lambda e: e.tensor_scalar(out=nbg, in0=nbg, scalar1=-1.0, scalar2=None, op0=ALU.mult),
         r=["nbg"], w=["nbg"])
    P.op('pool', lambda e: e.memset(one1, 1.0), w=["one1"])
    P.op('sp', lambda e: e.dma_start(out=cst2[0:16, 776:776 + 512], in_=gwg.ap().rearrange("k a b -> k (a b)")),
         w=["wgT"], dma=True)
    P.op('sp', lambda e: e.dma_start(out=gnb, in_=gng.ap().broadcast_to([64, 512])), w=["gnb"], dma=True)
    if mode != 'full':
        P.barrier()
    if mode == 'full':
        A.reset()
        cTs = A.f32(32).rearrange("p (k r) -> p k r", r=2)
        sTs = A.f32(32).rearrange("p (k r) -> p k r", r=2)
        bTs = A.f32(36)
        modp = A.f32(72).rearrange("p (r j) -> p r j", r=2)
        P.op('sp', lambda e: e.dma_start(out=cTs, in_=cT[:, :, :]), w=["cTs"], dma=True)
        P.op('sp', lambda e: e.dma_start(out=bTs, in_=bT[:, :]), w=["bTs"], dma=True)
        P.op('act', lambda e: e.activation(out=sTs, in_=cTs, func=AF.Silu), r=["cTs"], w=["sTs"])
        wv = w_ada.ap().rearrange("(k p) c -> p k c", p=128)
        wst = [A.f32(KC * 512).rearrange("p (k c) -> p k c", k=KC) for _ in range(3)]
        modps = ps[0][:, 0:72].rearrange("p (j r) -> p j r", r=2)
        for cg in range(9):
            wb = wst[cg % 3]
            key = ("wst", cg % 3)
            P.op('sp', lambda e, wb=wb, cg=cg: e.dma_start(out=wb, in_=wv[:, :, cg * 512:(cg + 1) * 512]),
                 w=[key], dma=True)
            for jj in range(4):
                j = cg * 4 + jj
                for k in range(KC):
                    P.op('pe', lambda e, wb=wb, jj=jj, j=j, k=k: e.matmul(
                        modps[:, j, :], lhsT=wb[:, k, jj * 128:(jj + 1) * 128], rhs=sTs[:, k, :],
                        start=(k == 0), stop=(k == KC - 1)), r=[key, "sTs"], w=[PSK[0]])
        for r in range(2):
            P.op('dve', lambda e, r=r: e.tensor_tensor(out=modp[:, r, :], in0=modps[:, :, r], in1=bTs,
                                                        op=ALU.add), r=[PSK[0], "bTs"], w=["modp"])
        P.op('sp', lambda e: e.dma_start(out=modp_d[:, :], in_=modp.rearrange("p r j -> p (r j)")),
             r=["modp"], w=["modp_d"], dma=True)
        P.op('pool', lambda e: e.collective_compute("AllGather", ALU.bypass, replica_groups=groups,
                                                    ins=[modp_d.ap().opt()], outs=[modall_d.ap().opt()]),
             r=["modp_d"], w=["modall_d"], cc=True)
        for q in range(4):
            P.op('sp', lambda e, q=q: e.dma_start(
                out=modT[:, :, q * 36:(q + 1) * 36],
                in_=modall_d[q * 128:(q + 1) * 128, :].rearrange("p (r j) -> p r j", r=2)),
                r=["modall_d"], w=["modT"], dma=True)
        WRES = (0.5, 1.0, 0.5)
        for r in range(2):
            for s in range(3):
                P.op('dve', lambda e, r=r, s=s: e.scalar_tensor_tensor(
                    out=Amod[:, r, s, :], in0=modT[:, r, (3 * s + 1) * 16:(3 * s + 2) * 16], scalar=1.0,
                    in1=gTs[:, 2 * s, :], op0=ALU.add, op1=ALU.mult), r=["modT", "gTs"], w=["Amod"])
                P.op('dve', lambda e, r=r, s=s: e.scalar_tensor_tensor(
                    out=Gmod[:, r, s, :], in0=modT[:, r, (3 * s + 2) * 16:(3 * s + 3) * 16], scalar=WRES[s],
                    in1=gTs[:, 2 * s + 1, :], op0=ALU.mult, op1=ALU.mult), r=["modT", "gTs"], w=["Gmod"])
        P.barrier()

    def Bmod(r, s, k):
        return modT[:, r, (3 * s) * 16 + k:(3 * s) * 16 + k + 1]

    def prenorm_modulate(HT, s, hm, sqkey="hm"):
        rstd = A.f32(NT)
        tmp = [A.f32(NT) for _ in range(2)]
        for k in range(KC):
            P.op('act', lambda e, k=k: e.activation(out=hm[:, k, :], in_=HT[:, k, :], func=AF.Square),
                 r=["HT"], w=["hm"])
        for gi, (c0, n) in enumerate(GROUPS):
            for k in range(KC):
                P.op('pe', lambda e, k=k, c0=c0, n=n, gi=gi: e.matmul(
                    ps[gi][:, 0:n], lhsT=ones_bf, rhs=hm[:, k, c0:c0 + n], start=(k == 0), stop=(k == KC - 1)),
                    r=["hm", "ones"], w=[PSK[gi]])
            P.op('act', lambda e, c0=c0, n=n, gi=gi: e.activation(
                out=rstd[:, c0:c0 + n], in_=ps[gi][:, 0:n], func=AF.Sqrt, bias=epsb, scale=1.0 / D),
                r=[PSK[gi], "epsb"], w=["rstd"])
        P.op('dve', lambda e: e.reciprocal(out=rstd, in_=rstd), r=["rstd"], w=["rstd"])
        for k in range(KC):
            t = tmp[k % 2]
            tk = ("tmpn", k % 2)
            P.op('dve', lambda e, k=k, t=t: e.tensor_tensor(out=t, in0=HT[:, k, :], in1=rstd, op=ALU.mult),
                 r=["HT", "rstd"], w=[tk])
            for r, (c0, n) in enumerate([(0, NL), (NL, NT - NL)]):
                P.op('act', lambda e, k=k, t=t, r=r, c0=c0, n=n: e.activation(
                    out=hm[:, k, c0:c0 + n], in_=t[:, c0:c0 + n], func=AF.Identity,
                    bias=Bmod(r, s, k), scale=Amod[:, r, s, k:k + 1]),
                    r=[tk, "Amod", "modT"], w=["hm"])

    HM_W = KC * NT // 2
    ACT_W = FC * NT // 2

    def ffn_core(s, li):
        wg = ffn_wg.ap()[li].rearrange("(k p) f -> p k f", p=128)
        wu = ffn_wu.ap()[li].rearrange("(k p) f -> p k f", p=128)
        wd = ffn_wd.ap()[li].rearrange("(f p) d -> p f d", p=128)
        A.reset(0)
        hm = A.bf16(KC * NT).rearrange("p (k t) -> p k t", k=KC)
        actT = A.bf16(FC * NT).rearrange("p (f t) -> p f t", f=FC)
        mark = A.off
        wsf = [A.f32(KC * 256).rearrange("p (k c) -> p k c", k=KC) for _ in range(2)]
        wsb = [A.bf16(KC * 256).rearrange("p (k c) -> p k c", k=KC) for _ in range(4)]
        sg = [A.f32(512) for _ in range(2)]
        slot = 0
        nsg = 0
        nld = 0
        for fg in range(FC // 2):
            wb = []
            for gu, wsrc in enumerate((wg, wu)):
                i2 = nld % 2
                i = nld % 4
                nld += 1
                P.op('sp', lambda e, i2=i2, wsrc=wsrc, fg=fg: e.dma_start(
                    out=wsf[i2], in_=wsrc[:, :, fg * 256:(fg + 1) * 256]), w=[("wsf", i2)], dma=True)
                for hk in range(2):
                    P.op('pool', lambda e, i=i, i2=i2, hk=hk: e.tensor_copy(
                        out=wsb[i][:, hk * 8:(hk + 1) * 8, :], in_=wsf[i2][:, hk * 8:(hk + 1) * 8, :]),
                        r=[("wsf", i2)], w=[("wsb", i)])
                wb.append(i)
            for f2 in range(2):
                f = fg * 2 + f2
                for (c0, n) in GROUPS:
                    pg, pu = 2 + 2 * (slot % 3), 3 + 2 * (slot % 3)
                    slot += 1
                    for gu, pb in enumerate((pg, pu)):
                        for k in range(KC):
                            P.op('pe', lambda e, pb=pb, k=k, c0=c0, n=n, i=wb[gu], f2=f2: e.matmul(
                                ps[pb][:, 0:n], lhsT=wsb[i][:, k, f2 * 128:(f2 + 1) * 128], rhs=hm[:, k, c0:c0 + n],
                                start=(k == 0), stop=(k == KC - 1)),
                                r=[("wsb", wb[gu]), "hm"], w=[PSK[pb]])
                    sgi = nsg % 2
                    nsg += 1
                    P.op('act', lambda e, pg=pg, n=n, sgi=sgi: e.activation(out=sg[sgi][:, 0:n], in_=ps[pg][:, 0:n],
                                                                            func=AF.Silu),
                         r=[PSK[pg]], w=[("sg", sgi)])
                    P.op('dve', lambda e, pu=pu, n=n, sgi=sgi, f=f, c0=c0: e.tensor_tensor(
                        out=actT[:, f, c0:c0 + n], in0=sg[sgi][:, 0:n], in1=ps[pu][:, 0:n], op=ALU.mult),
                        r=[("sg", sgi), PSK[pu]], w=["act"])
        P.barrier()
        A.reset(0)
        ysb = [A.f32(NT) for _ in range(2)]
        hsb = [A.f32(NT) for _ in range(2)]
        ysq = [A.bf16(NT) for _ in range(2)]
        rstd = A.f32(NT)
        assert A.off <= HM_W
        A.reset(mark)
        wdf = [A.f32(11 * 256).rearrange("p (f c) -> p f c", f=11) for _ in range(2)]
        wdb = [A.bf16(FC * 256).rearrange("p (f c) -> p f c", f=FC) for _ in range(2)]
        slot = 0
        nst = 0
        for dcg in range(KC // 2):
            t = dcg % 2
            for qtr in range(4):
                qi = nst % 2
                nst += 1
                P.op('sp', lambda e, qi=qi, dcg=dcg, qtr=qtr: e.dma_start(
                    out=wdf[qi], in_=wd[:, qtr * 11:(qtr + 1) * 11, dcg * 256:(dcg + 1) * 256]),
                    w=[("wdf", qi)], dma=True)
                P.op('pool', lambda e, qi=qi, t=t, qtr=qtr: e.tensor_copy(
                    out=wdb[t][:, qtr * 11:(qtr + 1) * 11, :], in_=wdf[qi]), r=[("wdf", qi)], w=[("wdb", t)])
            for dc2 in range(2):
                dc = dcg * 2 + dc2
                i = dc % 2
                for gi, (c0, n) in enumerate(GROUPS):
                    pb = 3 + (slot % 5)
                    slot += 1
                    for f in range(FC):
                        P.op('pe', lambda e, pb=pb, f=f, c0=c0, n=n, t=t, dc2=dc2: e.matmul(
                            ps[pb][:, 0:n], lhsT=wdb[t][:, f, dc2 * 128:(dc2 + 1) * 128], rhs=actT[:, f, c0:c0 + n],
                            start=(f == 0), stop=(f == FC - 1)),
                            r=[("wdb", t), "act"], w=[PSK[pb]])
                    P.op('act', lambda e, pb=pb, c0=c0, n=n, i=i: e.activation(
                        out=ysb[i][:, c0:c0 + n], in_=ps[pb][:, 0:n], func=AF.Copy), r=[PSK[pb]], w=[("ysb", i)])
                    P.op('act', lambda e, pb=pb, c0=c0, n=n, i=i: e.activation(
                        out=ysq[i][:, c0:c0 + n], in_=ps[pb][:, 0:n], func=AF.Square), r=[PSK[pb]], w=[("ysq", i)])
                for gi, (c0, n) in enumerate(GROUPS):
                    P.op('pe', lambda e, gi=gi, c0=c0, n=n, i=i, dc=dc: e.matmul(
                        ps[gi][:, 0:n], lhsT=ones_bf, rhs=ysq[i][:, c0:c0 + n], start=(dc == 0), stop=(dc == KC - 1)),
                        r=[("ysq", i), "ones"], w=[PSK[gi]])
                P.op('act', lambda e, i=i, dc=dc: e.dma_start(out=yT_d[dc, :, :], in_=ysb[i]),
                     r=[("ysb", i)], w=[("yT_d", dc)], dma=True)
        for gi, (c0, n) in enumerate(GROUPS):
            P.op('act', lambda e, c0=c0, n=n, gi=gi: e.activation(
                out=rstd[:, c0:c0 + n], in_=ps[gi][:, 0:n], func=AF.Sqrt, bias=epsb, scale=1.0 / D),
                r=[PSK[gi], "epsb"], w=["rstd2"])
        P.op('dve', lambda e: e.reciprocal(out=rstd, in_=rstd), r=["rstd2"], w=["rstd2"])
        residual_update(s, rstd, ysb, hsb)

    def residual_update(s, rstd, ysb, hsb, ncols=NT):
        ranges = [(0, 0, NL)] + ([(1, NL, NT - NL)] if ncols == NT else [])
        for dc in range(KC):
            i = dc % 2
            yb, hb = ysb[i], hsb[i]
            P.op('sp', lambda e, yb=yb, dc=dc: e.dma_start(out=yb[:, 0:ncols], in_=yT_d[dc, :, 0:ncols]),
                 r=[("yT_d", dc)], w=[("ysb", i)], dma=True)
            P.op('sp', lambda e, hb=hb, dc=dc: e.dma_start(out=hb[:, 0:ncols], in_=hT_d[dc, :, 0:ncols]),
                 r=[("hT_d", dc)], w=[("hsb", i)], dma=True)
            P.op('dve', lambda e, yb=yb: e.tensor_tensor(out=yb[:, 0:ncols], in0=yb[:, 0:ncols], in1=rstd[:, 0:ncols],
                                                         op=ALU.mult),
                 r=[("ysb", i), "rstd2"], w=[("ysb", i)])
            for (r, c0, n) in ranges:
                P.op('dve', lambda e, yb=yb, hb=hb, dc=dc, r=r, c0=c0, n=n: e.scalar_tensor_tensor(
                    out=hb[:, c0:c0 + n], in0=yb[:, c0:c0 + n], scalar=Gmod[:, r, s, dc:dc + 1],
                    in1=hb[:, c0:c0 + n], op0=ALU.mult, op1=ALU.add),
                    r=[("ysb", i), "Gmod", ("hsb", i)], w=[("hsb", i)])
            P.op('pool', lambda e, hb=hb, dc=dc: e.dma_start(out=hT_d[dc, :, 0:ncols], in_=hb[:, 0:ncols]),
                 r=[("hsb", i)], w=[("hT_d", dc)], dma=True)
        P.barrier()

    def prenorm_stage(s, from_x):
        A.reset(0)
        hm = A.bf16(KC * NT).rearrange("p (k t) -> p k t", k=KC)
        HT = A.f32(KC * NT).rearrange("p (k t) -> p k t", k=KC)
        if from_x:
            xt = [A.f32(D) for _ in range(2)]
            tiles = [(i * 128, 128) for i in range(8)] + [(NL, 64)]
            slot = 0
            for ti, (t0, pn) in enumerate(tiles):
                xb = xt[ti % 2]
                P.op('sp', lambda e, xb=xb, t0=t0, pn=pn: e.dma_start(out=xb[0:pn, :], in_=xin[t0:t0 + pn, :]),
                     w=[("xt", ti % 2)], dma=True)
                for kq in range(4):
                    pb = 4 + (slot % 4)
                    slot += 1
                    for kk in range(4):
                        k = kq * 4 + kk
                        P.op('pe', lambda e, xb=xb, pn=pn, pb=pb, kk=kk, k=k: e.transpose(
                            ps[pb][:, kk * 128:kk * 128 + pn], xb[0:pn, k * 128:(k + 1) * 128], ident[0:pn, 0:pn]),
                            r=[("xt", ti % 2), "ident"], w=[PSK[pb]])
                    src = ps[pb][:, :].rearrange("p (a b) -> p a b", a=4)[:, :, 0:pn]
                    dst = HT[:, kq * 4:(kq + 1) * 4, t0:t0 + pn]
                    if kq % 2 == 0:
                        P.op('act', lambda e, src=src, dst=dst: e.activation(out=dst, in_=src, func=AF.Copy),
                             r=[PSK[pb]], w=["HT"])
                    else:
                        P.op('dve', lambda e, src=src, dst=dst: e.tensor_copy(out=dst, in_=src),
                             r=[PSK[pb]], w=["HT"])
            for dc in range(KC):
                P.op('pool', lambda e, dc=dc: e.dma_start(out=hT_d[dc, :, :], in_=HT[:, dc, :]),
                     r=["HT"], w=[("hT_d", dc)], dma=True)
        else:
            for dc in range(KC):
                P.op('sp', lambda e, dc=dc: e.dma_start(out=HT[:, dc, :], in_=hT_d[dc, :, :]),
                     r=[("hT_d", dc)], w=["HT"], dma=True)
        prenorm_modulate(HT, s, hm)
        P.barrier()
        return hm

    def tok_groups():
        gl = [(0, 256, [(r, 1024, 64, r * 64) for r in range(4)])]
        for g in range(1, 9):
            gl.append((256 + 512 * (g - 1), 512, [((g - 1) // 2, ((g - 1) % 2) * 512, 512, 0)]))
        return gl

    def mixer_inproj():
        A.reset(0)
        NW = 2336
        WB = A.bf16(KC * NW).rearrange("p (k c) -> p k c", k=KC)
        wstg = [A.f32(NW) for _ in range(2)]
        wv_ = w_inh.ap().rearrange("(k p) c -> p k c", p=128)
        for k in range(KC):
            P.op('sp', lambda e, k=k: e.dma_start(out=wstg[k % 2], in_=wv_[:, k, :]), w=[("wstg", k % 2)], dma=True)
            P.op('pool', lambda e, k=k: e.tensor_copy(out=WB[:, k, :], in_=wstg[k % 2]),
                 r=[("wstg", k % 2)], w=["WB"])
        X = [A.bf16(KC * 512).rearrange("p (k t) -> p k t", k=KC) for _ in range(2)]
        ev = [A.f32(512) for _ in range(4)]
        evb = [A.bf16(512) for _ in range(4)]
        cnt = dict(pb=0, ev=0, evb=0)

        def nxt(kind, mod):
            v = cnt[kind] % mod
            cnt[kind] += 1
            return v

        FM = [(0, 128, 'q', 0), (128, 128, 'q', 1), (256, 128, 'k', 0), (384, 128, 'k', 1), (512, 32, 'g', 0),
              (544, 128, 'nq', 0), (672, 128, 'nq', 1), (800, 128, 'nk', 0), (928, 128, 'nk', 1)]
        TM = [(1056, 512, 'v'), (1568, 512, 'r'), (2080, 256, 'nv')]
        for gi, (n0, n, srcs) in enumerate(tok_groups()):
            Xb = X[gi % 2]
            xk = ("X", gi % 2)
            for (r, c0, nn, x0) in srcs:
                for c in range(8):
                    P.op('sp', lambda e, Xb=Xb, r=r, c0=c0, nn=nn, x0=x0, c=c: e.dma_start(
                        out=Xb[:, 2 * c:2 * c + 2, x0:x0 + nn],
                        in_=hmT_all[c * 1024 + r * 256:c * 1024 + (r + 1) * 256, c0:c0 + nn].rearrange(
                            "(k p) t -> p k t", p=128)), w=[xk], dma=True)
            for (c0, M, kind, idx) in FM:
                if gi == 0 and kind in ('q', 'nq'):
                    continue
                pb = nxt('pb', 8)
                for k in range(KC):
                    P.op('pe', lambda e, pb=pb, M=M, n=n, k=k, c0=c0, Xb=Xb: e.matmul(
                        ps[pb][0:M, 0:n], lhsT=WB[:, k, c0:c0 + M], rhs=Xb[:, k, 0:n],
                        start=(k == 0), stop=(k == KC - 1)), r=["WB", xk], w=[PSK[pb]])
                if kind in ('q', 'k', 'g'):
                    j = nxt('ev', 4)
                    dst = {'q': qT_d, 'k': kT_d, 'g': gT_d}[kind]
                    P.op('act', lambda e, pb=pb, M=M, n=n, j=j: e.activation(out=ev[j][0:M, 0:n], in_=ps[pb][0:M, 0:n],
                                                                           func=AF.Copy), r=[PSK[pb]], w=[("ev", j)])
                    P.op('pool', lambda e, dst=dst, idx=idx, M=M, n=n, n0=n0, j=j: e.dma_start(
                        out=dst[idx * 128:idx * 128 + M, n0:n0 + n], in_=ev[j][0:M, 0:n]),
                        r=[("ev", j)], w=["qkg_d"], dma=True)
                else:
                    j = nxt('evb', 4)
                    dst = {'nq': nqT_d, 'nk': nkT_d}[kind]
                    sc = 0.125 if kind == 'nq' else 1.0
                    P.op('act', lambda e, pb=pb, n=n, j=j, sc=sc: e.activation(
                        out=evb[j][:, 0:n], in_=ps[pb][:, 0:n], func=AF.Copy, scale=sc), r=[PSK[pb]], w=[("evb", j)])
                    P.op('pool', lambda e, dst=dst, idx=idx, n=n, n0=n0, j=j: e.dma_start(
                        out=dst[idx * 128:(idx + 1) * 128, n0:n0 + n], in_=evb[j][:, 0:n]),
                        r=[("evb", j)], w=["qkg_d"], dma=True)
            for tt in range(n // 128):
                for (c0, ncol, kind) in TM:
                    pb = nxt('pb', 8)
                    for k in range(KC):
                        P.op('pe', lambda e, pb=pb, ncol=ncol, k=k, c0=c0, Xb=Xb, tt=tt: e.matmul(
                            ps[pb][:, 0:ncol], lhsT=Xb[:, k, tt * 128:(tt + 1) * 128], rhs=WB[:, k, c0:c0 + ncol],
                            start=(k == 0), stop=(k == KC - 1)), r=["WB", xk], w=[PSK[pb]])
                    r0 = n0 + tt * 128
                    if kind == 'r':
                        j = nxt('ev', 4)
                        P.op('act', lambda e, pb=pb, j=j: e.activation(out=ev[j], in_=ps[pb][:, 0:512], func=AF.Silu),
                             r=[PSK[pb]], w=[("ev", j)])
                        P.op('pool', lambda e, r0=r0, j=j: e.dma_start(out=sr_d[r0:r0 + 128, :], in_=ev[j]),
                             r=[("ev", j)], w=["qkg_d"], dma=True)
                    else:
                        j = nxt('evb', 4)
                        dst = v_d if kind == 'v' else nv_d
                        P.op('dve', lambda e, pb=pb, j=j, ncol=ncol: e.tensor_copy(out=evb[j][:, 0:ncol],
                                                                                  in_=ps[pb][:, 0:ncol]),
                             r=[PSK[pb]], w=[("evb", j)])
                        P.op('pool', lambda e, dst=dst, r0=r0, j=j, ncol=ncol: e.dma_start(
                            out=dst[r0:r0 + 128, :], in_=evb[j][:, 0:ncol]), r=[("evb", j)], w=["qkg_d"], dma=True)
        P.barrier()

    def gla():
        S = A.f32(1024).rearrange("p (c v) -> p c v", c=2)
        Sbr = [A.bf16(1024).rearrange("p (c v) -> p c v", c=2) for _ in range(2)]
        f3 = lambda: A.f32(1024).rearrange("p (c t) -> p c t", c=2)
        b3 = lambda: A.bf16(1024).rearrange("p (c t) -> p c t", c=2)
        qT, kT, Ct, St = f3(), f3(), f3(), f3()
        e_, sp, cum, d3, E1, E2, E3, t1, qr, kr = [f3() for _ in range(10)]
        qd, ki, ke = b3(), b3(), b3()
        gts = A.f32(512, parts=16)
        kend = A.bf16(8 * 256, parts=64).rearrange("p (c d) -> p c d", c=8)
        vb = A.bf16(8 * 512, parts=64).rearrange("p (c v) -> p c v", c=8)
        dch = A.f32(16).rearrange("p (c n) -> p c n", c=2)
        att_sb = [A.bf16(64, parts=64) for _ in range(2)]
        o_sb = [A.f32(512, parts=64) for _ in range(2)]
        of_g = A.f32(8 * 512, parts=64).rearrange("p (c v) -> p c v", c=8)
        sr_g = A.f32(8 * 512, parts=64).rearrange("p (c v) -> p c v", c=8)
        gcT = A.bf16(4 * 512).rearrange("p (a t) -> p a t", a=4)
        res = [A.bf16(512, parts=64) for _ in range(2)]
        junk = A.f32(512, parts=64)
        ssq = [A.f32(1, parts=64) for _ in range(2)]
        glist = tok_groups()
        ci = 0
        yield
        for dirn in (0, 1):
            P.op('pool', lambda e: e.memset(S, 0.0), w=["S"])
            for jj_ in range(2):
                P.op('pool', lambda e, jj_=jj_: e.memset(Sbr[jj_], 0.0), w=[("Sb", jj_)])
            order = [0] + (list(range(1, 9)) if dirn == 0 else list(range(8, 0, -1)))
            for g in order:
                n0, n, _ = glist[g]
                nch = n // 64
                lat = g > 0
                tok0 = n0 - 256
                P.op('sp', lambda e, n0=n0, n=n: e.dma_start(
                    out=kT[:, :, 0:n], in_=kT_d.ap().rearrange("(c p) t -> p c t", p=128)[:, :, n0:n0 + n]),
                    r=["qkg_d"], w=["kT"], dma=True)
                P.op('sp', lambda e, n0=n0, n=n, dirn=dirn: e.dma_start(
                    out=gts[:, 0:n], in_=gT_d[dirn * 16:(dirn + 1) * 16, n0:n0 + n]), r=["qkg_d"], w=["gts"], dma=True)
                P.op('sp', lambda e, n0=n0, n=n, nch=nch: e.dma_start(
                    out=vb[:, 0:nch, :], in_=v_d[n0:n0 + n, :].rearrange("(c p) v -> p c v", p=64)),
                    r=["qkg_d"], w=["vb"], dma=True)
                if lat:
                    P.op('sp', lambda e, n0=n0, n=n: e.dma_start(
                        out=qT[:, :, 0:n], in_=qT_d.ap().rearrange("(c p) t -> p c t", p=128)[:, :, n0:n0 + n]),
                        r=["qkg_d"], w=["qT"], dma=True)
                    P.op('sp', lambda e, tok0=tok0, n=n: e.dma_start(
                        out=Ct[:, :, 0:n], in_=ropeC.ap().rearrange("c p t -> p c t")[:, :, tok0:tok0 + n]),
                        w=["Ct"], dma=True)
                    P.op('sp', lambda e, tok0=tok0, n=n: e.dma_start(
                        out=St[:, :, 0:n], in_=ropeS.ap().rearrange("c p t -> p c t")[:, :, tok0:tok0 + n]),
                        w=["St"], dma=True)
                    if dirn == 1:
                        P.op('sp', lambda e, tok0=tok0, n=n: e.dma_start(
                            out=of_g, in_=of_d[tok0:tok0 + n, :].rearrange("(c p) v -> p c v", p=64)),
                            r=["of_d"], w=["of_g"], dma=True)
                        P.op('sp', lambda e, n0=n0, n=n: e.dma_start(
                            out=sr_g, in_=sr_d[n0:n0 + n, :].rearrange("(c p) v -> p c v", p=64)),
                            r=["qkg_d"], w=["sr_g"], dma=True)
                        for c in range(8):
                            P.op('dve', lambda e, c=c: e.tensor_tensor(out=sr_g[:, c, :], in0=sr_g[:, c, :], in1=gnb,
                                                                       op=ALU.mult), r=["sr_g", "gnb"], w=["sr_g"])
                for dc in range(2):
                    P.op('pe', lambda e, dc=dc, n=n, dirn=dirn: e.matmul(
                        ps[dc][:, 0:n], lhsT=wgT[:, dirn, dc * 128:(dc + 1) * 128], rhs=gts[:, 0:n],
                        start=True, stop=True), r=["gts", "wgT"], w=[PSK[dc]])
                    P.op('act', lambda e, dc=dc, n=n, dirn=dirn: e.activation(
                        out=e_[:, dc, 0:n], in_=ps[dc][:, 0:n], func=AF.Exp, scale=-1.0,
                        bias=nbg[:, dirn * 2 + dc:dirn * 2 + dc + 1]), r=[PSK[dc], "nbg"], w=["e_"])
                    P.op('act', lambda e, dc=dc, n=n: e.activation(
                        out=sp[:, dc, 0:n], in_=e_[:, dc, 0:n], func=AF.Ln, bias=one1, scale=1.0),
                        r=["e_", "one1"], w=["sp"])
                    P.op('dve', lambda e, dc=dc, n=n: e.tensor_tensor_scan(
                        out=cum[:, dc, 0:n], data0=cmask[:, 0:n], data1=sp[:, dc, 0:n], initial=0.0,
                        op0=ALU.mult, op1=ALU.add), r=["sp", "cmask"], w=["cum"])
                cum4 = cum[:, :, 0:n].rearrange("p c (h t) -> p c h t", t=64)
                tot = cum4[:, :, :, 63:64]
                for dc in range(2):
                    P.op('dve', lambda e, dc=dc, n=n, nch=nch: e.tensor_tensor(
                        out=d3[:, dc, 0:n].rearrange("p (h t) -> p h t", t=64),
                        in0=cum[:, dc, 0:n].rearrange("p (h t) -> p h t", t=64),
                        in1=cum[:, dc, 0:n].rearrange("p (h t) -> p h t", t=64)[:, :, 63:64].to_broadcast([128, nch, 64]),
                        op=ALU.subtract), r=["cum"], w=["d3"])
                    P.op('act', lambda e, dc=dc, n=n, nch=nch: e.activation(
                        out=dch[:, dc, 0:nch], in_=cum[:, dc, 0:n].rearrange("p (h t) -> p h t", t=64)[:, :, 63],
                        func=AF.Exp, scale=-1.0 / 16), r=["cum"], w=["dch"])
                if dirn == 0:
                    P.op('act', lambda e, n=n: e.activation(out=E1[:, :, 0:n], in_=cum[:, :, 0:n], func=AF.Exp,
                                                            scale=-1.0 / 16), r=["cum"], w=["E1"])
                    P.op('act', lambda e, n=n: e.activation(out=E2[:, :, 0:n], in_=cum[:, :, 0:n], func=AF.Exp,
                                                            scale=1.0 / 16), r=["cum"], w=["E2"])
                    P.op('act', lambda e, n=n: e.activation(out=E3[:, :, 0:n], in_=d3[:, :, 0:n], func=AF.Exp,
                                                            scale=1.0 / 16), r=["d3"], w=["E3"])
                else:
                    P.op('dve', lambda e, n=n: e.tensor_tensor(out=t1[:, :, 0:n], in0=sp[:, :, 0:n], in1=d3[:, :, 0:n],
                                                               op=ALU.subtract), r=["sp", "d3"], w=["t1"])
                    P.op('act', lambda e, n=n: e.activation(out=E1[:, :, 0:n], in_=t1[:, :, 0:n], func=AF.Exp,
                                                            scale=-1.0 / 16), r=["t1"], w=["E1"])
                    P.op('act', lambda e, n=n: e.activation(out=E2[:, :, 0:n], in_=t1[:, :, 0:n], func=AF.Exp,
                                                            scale=1.0 / 16), r=["t1"], w=["E2"])
                    P.op('dve', lambda e, n=n: e.tensor_tensor(out=d3[:, :, 0:n], in0=cum[:, :, 0:n], in1=sp[:, :, 0:n],
                                                               op=ALU.subtract), r=["sp", "cum", "E1"], w=["d3"])
                    P.op('act', lambda e, n=n: e.activation(out=E3[:, :, 0:n], in_=d3[:, :, 0:n], func=AF.Exp,
                                                            scale=-1.0 / 16), r=["d3"], w=["E3"])
                if lat:
                    for (src, dstr, nm) in ((qT, qr, "q"), (kT, kr, "k")):
                        for dc in range(2):
                            pb = dc
                            P.op('pe', lambda e, pb=pb, dc=dc, n=n, src=src: e.matmul(
                                ps[pb][:, 0:n], lhsT=rotT, rhs=src[:, dc, 0:n], start=True, stop=True),
                                r=[nm + "T", "rotT"], w=[PSK[pb]])
                            P.op('dve', lambda e, pb=pb, dc=dc, n=n: e.tensor_tensor(
                                out=t1[:, dc, 0:n], in0=ps[pb][:, 0:n], in1=St[:, dc, 0:n], op=ALU.mult),
                                r=[PSK[pb], "St", "E1", "E2"], w=["t1"])
                            P.op('pool', lambda e, dc=dc, n=n, src=src, dstr=dstr: e.tensor_tensor(
                                out=dstr[:, dc, 0:n], in0=src[:, dc, 0:n], in1=Ct[:, dc, 0:n], op=ALU.mult),
                                r=[nm + "T", "Ct"], w=[nm + "r"])
                            P.op('dve', lambda e, dc=dc, n=n, dstr=dstr: e.tensor_tensor(
                                out=dstr[:, dc, 0:n], in0=dstr[:, dc, 0:n], in1=t1[:, dc, 0:n], op=ALU.add),
                                r=[nm + "r", "t1"], w=[nm + "r"])
                    ksrc, kname = kr, "kr"
                    for dc in range(2):
                        P.op('dve', lambda e, dc=dc, n=n: e.scalar_tensor_tensor(
                            out=qd[:, dc, 0:n], in0=qr[:, dc, 0:n], scalar=0.0625, in1=E1[:, dc, 0:n],
                            op0=ALU.mult, op1=ALU.mult), r=["qr", "E1"], w=["qd"])
                        P.op('dve', lambda e, dc=dc, n=n: e.tensor_tensor(
                            out=ki[:, dc, 0:n], in0=kr[:, dc, 0:n], in1=E2[:, dc, 0:n], op=ALU.mult),
                            r=["kr", "E2"], w=["ki"])
                else:
                    ksrc, kname = kT, "kT"
                for dc in range(2):
                    P.op('pool', lambda e, dc=dc, n=n, ksrc=ksrc: e.tensor_tensor(
                        out=ke[:, dc, 0:n], in0=ksrc[:, dc, 0:n], in1=E3[:, dc, 0:n], op=ALU.mult),
                        r=[kname, "E3"], w=["ke"])
                for half in range((nch + 3) // 4):
                    pb = half
                    pv = ps[pb][0:64, :].bitcast(BF16).rearrange("p (c d) -> p c d", c=4)
                    for cc in range(4):
                        c = half * 4 + cc
                        for dc in range(2):
                            P.op('pe', lambda e, pv=pv, cc=cc, c=c, dc=dc: e.transpose(
                                pv[:, cc, dc * 128:(dc + 1) * 128], ke[:, dc, c * 64:(c + 1) * 64], identb),
                                r=["ke", "identb"], w=[PSK[pb]])
                    P.op('act', lambda e, pv=pv, half=half: e.activation(out=kend[:, half * 4:half * 4 + 4, :], in_=pv,
                                                                        func=AF.Copy), r=[PSK[pb]], w=["kend"])
                yield
                clist = list(range(nch)) if dirn == 0 else list(range(nch - 1, -1, -1))
                for c in clist:
                    yield
                    cs = slice(c * 64, (c + 1) * 64)
                    j = ci % 2
                    ci += 1
                    pa, po = (2, 3) if j == 0 else (6, 7)
                    kb = (4, 5) if j == 0 else (0, 1)
                    if lat:
                        for dc in range(2):
                            P.op('pe', lambda e, pa=pa, dc=dc, cs=cs: e.matmul(
                                ps[pa][0:64, 0:64], lhsT=ki[:, dc, cs], rhs=qd[:, dc, cs],
                                start=(dc == 0), stop=(dc == 1)), r=["ki", "qd"], w=[PSK[pa]])
                    for dc in range(2):
                        P.op('pe', lambda e, dc=dc, c=c, kb=kb: e.matmul(
                            ps[kb[dc]][:, 0:512], lhsT=kend[:, c, dc * 128:(dc + 1) * 128], rhs=vb[:, c, :],
                            start=True, stop=True), r=["kend", "vb"], w=[PSK[kb[dc]]])
                    if lat:
                        P.op('dve', lambda e, pa=pa, j=j, dirn=dirn: e.tensor_tensor(
                            out=att_sb[j], in0=ps[pa][0:64, 0:64], in1=trm[:, dirn, :], op=ALU.mult),
                            r=[PSK[pa], "trm"], w=[("att", j)])
                    for dc in range(2):
                        P.op('dve', lambda e, dc=dc, c=c, kb=kb: e.scalar_tensor_tensor(
                            out=S[:, dc, :], in0=S[:, dc, :], scalar=dch[:, dc, c:c + 1], in1=ps[kb[dc]][:, 0:512],
                            op0=ALU.mult, op1=ALU.add), r=["S", "dch", PSK[kb[dc]]], w=["S"])
                        P.op('act', lambda e, dc=dc, j=j: e.activation(out=Sbr[1 - j][:, dc, :], in_=S[:, dc, :],
                                                                      func=AF.Copy),
                             r=["S"], w=[("Sb", 1 - j)])
                    if lat:
                        P.op('pe', lambda e, po=po, j=j, c=c: e.matmul(
                            ps[po][0:64, 0:512], lhsT=att_sb[j], rhs=vb[:, c, :], start=True, stop=False),
                            r=[("att", j), "vb"], w=[PSK[po]])
                        for dc in range(2):
                            P.op('pe', lambda e, po=po, dc=dc, cs=cs, j=j: e.matmul(
                                ps[po][0:64, 0:512], lhsT=qd[:, dc, cs], rhs=Sbr[j][:, dc, :],
                                start=False, stop=(dc == 1)), r=["qd", ("Sb", j)], w=[PSK[po]])
                    if lat and dirn == 0:
                        P.op('act', lambda e, po=po, j=j: e.activation(out=o_sb[j], in_=ps[po][0:64, 0:512],
                                                                      func=AF.Copy), r=[PSK[po]], w=[("o_sb", j)])
                        P.op('pool', lambda e, j=j, tok0=tok0, c=c: e.dma_start(
                            out=of_d[tok0 + c * 64:tok0 + (c + 1) * 64, :], in_=o_sb[j]),
                            r=[("o_sb", j)], w=["of_d"], dma=True)
                    elif lat:
                        P.op('dve', lambda e, po=po, j=j, c=c: e.tensor_tensor(
                            out=o_sb[j], in0=ps[po][0:64, 0:512], in1=of_g[:, c, :], op=ALU.add),
                            r=[PSK[po], "of_g"], w=[("o_sb", j)])
                        P.op('act', lambda e, j=j: e.activation(out=junk, in_=o_sb[j], func=AF.Square,
                                                                accum_out=ssq[j]), r=[("o_sb", j)], w=["junk", ("ssq", j)])
                        P.op('act', lambda e, j=j: e.activation(out=ssq[j], in_=ssq[j], func=AF.Sqrt, bias=epsb[0:64, :],
                                                                scale=1.0 / 512), r=[("ssq", j), "epsb"], w=[("ssq", j)])
                        P.op('dve', lambda e, j=j: e.reciprocal(out=ssq[j], in_=ssq[j]), r=[("ssq", j)], w=[("ssq", j)])
                        P.op('dve', lambda e, j=j, c=c: e.scalar_tensor_tensor(
                            out=res[j], in0=o_sb[j], scalar=ssq[j], in1=sr_g[:, c, :], op0=ALU.mult, op1=ALU.mult),
                            r=[("o_sb", j), ("ssq", j), "sr_g"], w=[("res", j)])
                        pv = ps[pa][:, 0:128].bitcast(BF16).rearrange("p (a t) -> p a t", a=4)
                        for a in range(4):
                            P.op('pe', lambda e, pv=pv, a=a, j=j: e.transpose(
                                pv[:, a, :], res[j][:, a * 128:(a + 1) * 128], identb[0:64, 0:64]),
                                r=[("res", j), "identb"], w=[PSK[pa]])
                        P.op('act', lambda e, pv=pv, cs=cs: e.activation(out=gcT[:, :, cs], in_=pv, func=AF.Copy),
                             r=[PSK[pa]], w=["gcT"])
                if lat and dirn == 1:
                    qt, c0 = tok0 // 1024, tok0 % 1024
                    P.op('pool', lambda e, qt=qt, c0=c0: e.dma_start(
                        out=mixT_d[qt * 768:qt * 768 + 512, c0:c0 + 512].rearrange("(a p) t -> p a t", p=128),
                        in_=gcT), r=["gcT"], w=["mixT_d"], dma=True)

    def na():
        NB = 4
        nq = A.bf16(SEQ, parts=64)
        nk = A.bf16(NS, parts=64)
        Ve = A.bf16(34 * 64).rearrange("p (b d) -> p b d", b=34)
        Vo = A.bf16(31 * 64).rearrange("p (b d) -> p b d", b=31)
        TBh = A.f32(15 * 64, parts=64)
        naT = A.bf16(SEQ, parts=64)
        s_all = [A.f32(768, parts=64) for _ in range(NB)]
        Pm = [A.bf16(768, parts=64) for _ in range(NB)]
        PT = [A.bf16(6 * 64).rearrange("p (b q) -> p b q", b=6) for _ in range(NB)]
        osb = [A.bf16(64, parts=64) for _ in range(NB)]
        sm = [A.f32(8, parts=64) for _ in range(NB)]
        yield
        for hh in range(4):
            hs = slice(hh * 64, (hh + 1) * 64)
            P.op('sp', lambda e, hs=hs: e.dma_start(out=nq, in_=nqT_d[hs, 256:NS]), r=["qkg_d"], w=["nq"], dma=True)
            P.op('sp', lambda e, hs=hs: e.dma_start(out=nk, in_=nkT_d[hs, :]), r=["qkg_d"], w=["nk"], dma=True)
            P.op('sp', lambda e, hs=hs: e.dma_start(
                out=Ve, in_=nv_d[:, hs].rearrange("(b p) d -> p b d", p=128)), r=["qkg_d"], w=["Ve"], dma=True)
            P.op('sp', lambda e, hs=hs: e.dma_start(
                out=Vo, in_=nv_d[320:320 + 31 * 128, hs].rearrange("(b p) d -> p b d", p=128)),
                r=["qkg_d"], w=["Vo"], dma=True)
            P.op('sp', lambda e, hh=hh: e.dma_start(out=TBh, in_=tb_in[hh, :, :]), w=["TBh"], dma=True)
            for rb in range(0, 64, NB):
                yield
                rows = []
                for j in range(NB):
                    r = rb + j
                    j0 = min(max(r - 4, 0), 56)
                    rows.append(dict(j=j, r=r, j0=j0, s0=j0 - r + 7, qs=slice(r * 64, (r + 1) * 64),
                                     bl=2 * j, bm=2 * j + 1, KL=("nl", j), KC=("nc", j), KT=("nt", j), KO=("no", j)))
                for R in rows:
                    P.op('pe', lambda e, R=R: e.matmul(
                        ps[R['bl']][0:64, 0:512], lhsT=nq[:, R['qs']],
                        rhs=nk[:, 256 + R['j0'] * 64:256 + R['j0'] * 64 + 512], start=True, stop=True),
                        r=["nq", "nk"], w=[R['KL']])
                    P.op('pe', lambda e, R=R: e.matmul(
                        ps[R['bm']][0:64, 0:256], lhsT=nq[:, R['qs']], rhs=nk[:, 0:256], start=True, stop=True),
                        r=["nq", "nk"], w=[R['KC'], R['KT'], R['KO']])
                for R in rows:
                    j = R['j']
                    P.op('dve', lambda e, R=R, j=j: e.tensor_tensor(
                        out=s_all[j][:, 0:512], in0=ps[R['bl']][0:64, 0:512],
                        in1=TBh[:, R['s0'] * 64:(R['s0'] + 8) * 64], op=ALU.add),
                        r=[R['KL'], "TBh"], w=[("s_lat", j)])
                    P.op('act', lambda e, R=R, j=j: e.activation(
                        out=s_all[j][:, 512:768], in_=ps[R['bm']][0:64, 0:256], func=AF.Copy),
                        r=[R['KC']], w=[("s_ctx", j)])
                for R in rows:
                    j = R['j']
                    P.op('dve', lambda e, j=j: e.reduce_max(out=sm[j][:, 0:1], in_=s_all[j], axis=mybir.AxisListType.X),
                         r=[("s_lat", j), ("s_ctx", j)], w=[("sm", j)])
                for R in rows:
                    j = R['j']
                    P.op('dve', lambda e, j=j: e.tensor_scalar(out=sm[j][:, 1:2], in0=sm[j][:, 0:1], scalar1=-1.0,
                                                               scalar2=None, op0=ALU.mult), r=[("sm", j)], w=[("nm", j)])
                for R in rows:
                    j = R['j']
                    P.op('act', lambda e, j=j: e.activation(out=Pm[j], in_=s_all[j], func=AF.Exp,
                                                            bias=sm[j][:, 1:2], scale=1.0, accum_out=sm[j][:, 2:3]),
                         r=[("s_lat", j), ("s_ctx", j), ("nm", j)], w=[("Pm", j), ("sum", j)])
                for R in rows:
                    j = R['j']
                    pv = ps[R['bm']][:, 256:448].bitcast(BF16).rearrange("p (b q) -> p b q", b=6)
                    R['pv'] = pv
                    for blk in range(6):
                        P.op('pe', lambda e, pv=pv, blk=blk, j=j: e.transpose(
                            pv[:, blk, :], Pm[j][:, blk * 128:(blk + 1) * 128], identb[0:64, 0:64]),
                            r=[("Pm", j), "identb"], w=[R['KT']])
                for R in rows:
                    j = R['j']
                    P.op('dve', lambda e, pv=R['pv'], j=j: e.tensor_copy(out=PT[j], in_=pv), r=[R['KT']], w=[("PT", j)])
                for R in rows:
                    j, j0 = R['j'], R['j0']
                    for blk in range(6):
                        if blk < 4:
                            vsrc = Ve[:, 2 + j0 // 2 + blk, :] if j0 % 2 == 0 else Vo[:, (j0 - 1) // 2 + blk, :]
                        else:
                            vsrc = Ve[:, blk - 4, :]
                        P.op('pe', lambda e, R=R, blk=blk, j=j, vsrc=vsrc: e.matmul(
                            ps[R['bm']][0:64, 448:512], lhsT=PT[j][:, blk, :], rhs=vsrc, start=(blk == 0), stop=(blk == 5)),
                            r=[("PT", j), "Ve", "Vo"], w=[R['KO']])
                for R in rows:
                    j = R['j']
                    P.op('dve', lambda e, j=j: e.reciprocal(out=sm[j][:, 3:4], in_=sm[j][:, 2:3]),
                         r=[("sum", j)], w=[("rs", j)])
                for R in rows:
                    j = R['j']
                    P.op('dve', lambda e, j=j, R=R: e.tensor_scalar(out=osb[j], in0=ps[R['bm']][0:64, 448:512],
                                                                   scalar1=sm[j][:, 3:4], scalar2=None, op0=ALU.mult),
                         r=[R['KO'], ("rs", j)], w=[("osb", j)])
                for R in rows:
                    j = R['j']
                    pv2 = ps[R['bm']][0:64, 256:288].bitcast(BF16)
                    R['pv2'] = pv2
                    P.op('pe', lambda e, pv2=pv2, j=j: e.transpose(pv2, osb[j], identb[0:64, 0:64]),
                         r=[("osb", j), "identb"], w=[R['KT']])
                for R in rows:
                    P.op('act', lambda e, R=R: e.activation(out=naT[:, R['qs']], in_=R['pv2'], func=AF.Copy),
                         r=[R['KT']], w=["naT"])
            for qt in range(4):
                P.op('pool', lambda e, qt=qt, hh=hh: e.dma_start(
                    out=mixT_d[qt * 768 + 512 + hh * 64:qt * 768 + 512 + (hh + 1) * 64, :],
                    in_=naT[:, qt * 1024:(qt + 1) * 1024]), r=["naT"], w=["mixT_d"], dma=True)

    def merge_out():
        A.reset(0)
        MX = A.bf16(24 * NL).rearrange("p (c t) -> p c t", c=24)
        hmo = A.bf16(KC * NL).rearrange("p (k t) -> p k t", k=KC)
        MG = A.bf16(KC * NL).rearrange("p (k t) -> p k t", k=KC)
        mark = A.off
        selS = A.f32(4)
        tmpx = [A.bf16(6 * NL) for _ in range(2)]
        P.op('sp', lambda e: e.dma_start(out=selS, in_=sel_in[:, :]), w=["selS"], dma=True)
        P.op('sp', lambda e: e.dma_start(
            out=hmo, in_=hmT_d.ap().rearrange("(k p) t -> p k t", p=128)[:, :, 0:NL]), w=["hmo"], dma=True)
        n = 0
        for r in range(4):
            dst = MX[:, r * 6:(r + 1) * 6, :].rearrange("p c t -> p (c t)")
            for q in range(4):
                tb_ = tmpx[n % 2]
                tk = ("tmpx", n % 2)
                n += 1
                for hf in range(2):
                    r0 = (2 * q + hf) * 1536 + r * 384
                    P.op('sp', lambda e, tb_=tb_, r0=r0, hf=hf: e.dma_start(
                        out=tb_.rearrange("p (c t) -> p c t", c=6)[:, 3 * hf:3 * hf + 3, :],
                        in_=mixin_d[r0:r0 + 384, :].rearrange("(c p) t -> p c t", p=128)), w=[tk], dma=True)
                if q == 0:
                    P.op('dve', lambda e, dst=dst, tb_=tb_, q=q: e.tensor_scalar(
                        out=dst, in0=tb_, scalar1=selS[:, q:q + 1], scalar2=None, op0=ALU.mult),
                        r=[tk, "selS"], w=["MX"])
                else:
                    P.op('dve', lambda e, dst=dst, tb_=tb_, q=q: e.scalar_tensor_tensor(
                        out=dst, in0=tb_, scalar=selS[:, q:q + 1], in1=dst, op0=ALU.mult, op1=ALU.add),
                        r=[tk, "selS", "MX"], w=["MX"])
        P.barrier()
        A.reset(mark)
        wsf = [A.f32(KC * 128).rearrange("p (k c) -> p k c", k=KC) for _ in range(4)]
        wsb = [A.bf16(KC * 128).rearrange("p (k c) -> p k c", k=KC) for _ in range(4)]
        sgm = [A.f32(512) for _ in range(4)]
        wgo = w_gla_o.ap().rearrange("(i p) d -> p i d", p=128)
        wno = w_na_o.ap().rearrange("(i p) d -> p i d", p=128)
        wmv = w_m.ap().rearrange("(k p) c -> p k c", p=128)
        it = 0
        for dc in range(KC):
            dsl = slice(dc * 128, (dc + 1) * 128)
            srcs = [(wgo[:, :, dsl], 16), (wno[:, :, dsl], 8), (wmv[:, :, dsl], 16),
                    (wmv[:, :, 2048 + dc * 128:2048 + (dc + 1) * 128], 16)]
            for wi, (src, ni) in enumerate(srcs):
                P.op('sp', lambda e, wi=wi, src=src, ni=ni: e.dma_start(out=wsf[wi][:, 0:ni, :], in_=src),
                     w=[("wsf", wi)], dma=True)
                P.op('pool', lambda e, wi=wi, ni=ni: e.tensor_copy(out=wsb[wi][:, 0:ni, :], in_=wsf[wi][:, 0:ni, :]),
                     r=[("wsf", wi)], w=[("wsb", wi)])
            for grp in range(2):
                gs = slice(grp * 512, (grp + 1) * 512)
                base = 4 * (it % 2)
                it += 1
                pa, pbk, p1, p2 = base, base + 1, base + 2, base + 3
                for i in range(16):
                    P.op('pe', lambda e, pa=pa, i=i, gs=gs: e.matmul(
                        ps[pa][:, :], lhsT=wsb[0][:, i, :], rhs=MX[:, (i // 4) * 6 + i % 4, gs],
                        start=(i == 0), stop=(i == 15)), r=[("wsb", 0), "MX"], w=[PSK[pa]])
                for i in range(8):
                    P.op('pe', lambda e, pbk=pbk, i=i, gs=gs: e.matmul(
                        ps[pbk][:, :], lhsT=wsb[1][:, i, :], rhs=MX[:, (i // 2) * 6 + 4 + i % 2, gs],
                        start=(i == 0), stop=(i == 7)), r=[("wsb", 1), "MX"], w=[PSK[pbk]])
                for (pm, wi) in ((p1, 2), (p2, 3)):
                    for k in range(KC):
                        P.op('pe', lambda e, pm=pm, wi=wi, k=k, gs=gs: e.matmul(
                            ps[pm][:, :], lhsT=wsb[wi][:, k, :], rhs=hmo[:, k, gs],
                            start=(k == 0), stop=(k == KC - 1)), r=[("wsb", wi), "hmo"], w=[PSK[pm]])
                P.op('act', lambda e, p1=p1: e.activation(out=sgm[0], in_=ps[p1][:, :], func=AF.Sigmoid),
                     r=[PSK[p1]], w=[("sgm", 0)])
                P.op('act', lambda e, p2=p2: e.activation(out=sgm[1], in_=ps[p2][:, :], func=AF.Sigmoid),
                     r=[PSK[p2]], w=[("sgm", 1)])
                P.op('dve', lambda e, pa=pa: e.tensor_tensor(out=sgm[2], in0=sgm[0], in1=ps[pa][:, :], op=ALU.mult),
                     r=[("sgm", 0), PSK[pa]], w=[("sgm", 2)])
                P.op('dve', lambda e, pbk=pbk: e.tensor_tensor(out=sgm[3], in0=sgm[1], in1=ps[pbk][:, :], op=ALU.mult),
                     r=[("sgm", 1), PSK[pbk]], w=[("sgm", 3)])
                P.op('pool', lambda e, dc=dc, gs=gs: e.tensor_tensor(out=MG[:, dc, gs], in0=sgm[2], in1=sgm[3],
                                                                      op=ALU.add),
                     r=[("sgm", 2), ("sgm", 3)], w=["MG"])
        P.barrier()
        A.reset(mark)
        wsf2 = [A.f32(KC * 128).rearrange("p (k c) -> p k c", k=KC) for _ in range(2)]
        wsb2 = [A.bf16(KC * 128).rearrange("p (k c) -> p k c", k=KC) for _ in range(2)]
        ysb = [A.f32(NT) for _ in range(2)]
        hsb = [A.f32(NT) for _ in range(2)]
        ysq = [A.bf16(NT) for _ in range(2)]
        rstd = A.f32(NT)
        wov = w_out.ap().rearrange("(k p) d -> p k d", p=128)
        it = 0
        for dc in range(KC):
            i = dc % 2
            P.op('sp', lambda e, i=i, dc=dc: e.dma_start(out=wsf2[i], in_=wov[:, :, dc * 128:(dc + 1) * 128]),
                 w=[("wsf", i)], dma=True)
            P.op('pool', lambda e, i=i: e.tensor_copy(out=wsb2[i], in_=wsf2[i]), r=[("wsf", i)], w=[("wsb", i)])
            for grp in range(2):
                gs = slice(grp * 512, (grp + 1) * 512)
                pb = 2 + (it % 6)
                it += 1
                for k in range(KC):
                    P.op('pe', lambda e, pb=pb, k=k, gs=gs, i=i: e.matmul(
                        ps[pb][:, :], lhsT=wsb2[i][:, k, :], rhs=MG[:, k, gs], start=(k == 0), stop=(k == KC - 1)),
                        r=[("wsb", i), "MG"], w=[PSK[pb]])
                P.op('act', lambda e, pb=pb, gs=gs, i=i: e.activation(out=ysb[i][:, gs], in_=ps[pb][:, :], func=AF.Copy),
                     r=[PSK[pb]], w=[("ysb", i)])
                P.op('act', lambda e, pb=pb, gs=gs, i=i: e.activation(out=ysq[i][:, gs], in_=ps[pb][:, :], func=AF.Square),
                     r=[PSK[pb]], w=[("ysq", i)])
            for grp in range(2):
                gs = slice(grp * 512, (grp + 1) * 512)
                P.op('pe', lambda e, grp=grp, gs=gs, i=i, dc=dc: e.matmul(
                    ps[grp][:, :], lhsT=ones_bf, rhs=ysq[i][:, gs], start=(dc == 0), stop=(dc == KC - 1)),
                    r=[("ysq", i), "ones"], w=[PSK[grp]])
            P.op('act', lambda e, i=i, dc=dc: e.dma_start(out=yT_d[dc, :, 0:NL], in_=ysb[i][:, 0:NL]),
                 r=[("ysb", i)], w=[("yT_d", dc)], dma=True)
        for grp in range(2):
            gs = slice(grp * 512, (grp + 1) * 512)
            P.op('act', lambda e, grp=grp, gs=gs: e.activation(
                out=rstd[:, gs], in_=ps[grp][:, :], func=AF.Sqrt, bias=epsb, scale=1.0 / D),
                r=[PSK[grp], "epsb"], w=["rstd2"])
        P.op('dve', lambda e: e.reciprocal(out=rstd[:, 0:NL], in_=rstd[:, 0:NL]), r=["rstd2"], w=["rstd2"])
        residual_update(1, rstd, ysb, hsb, ncols=NL)

    def final_out():
        A.reset(0)
        HT = A.f32(KC * NL).rearrange("p (k t) -> p k t", k=KC)
        xo = [A.f32(D) for _ in range(2)]
        for dc in range(KC):
            P.op('sp', lambda e, dc=dc: e.dma_start(out=HT[:, dc, :], in_=hT_d[dc, :, 0:NL]),
                 r=[("hT_d", dc)], w=["HT"], dma=True)
        it = 0
        for tt in range(8):
            xb = xo[tt % 2]
            xk = ("xo", tt % 2)
            for kq in range(4):
                pb = it % 8
                it += 1
                for kk in range(4):
                    k = kq * 4 + kk
                    P.op('pe', lambda e, pb=pb, kk=kk, k=k, tt=tt: e.transpose(
                        ps[pb][:, kk * 128:(kk + 1) * 128], HT[:, k, tt * 128:(tt + 1) * 128], ident),
                        r=["HT", "ident"], w=[PSK[pb]])
                if kq % 2 == 0:
                    P.op('act', lambda e, pb=pb, xb=xb, kq=kq: e.activation(
                        out=xb[:, kq * 512:(kq + 1) * 512], in_=ps[pb][:, :], func=AF.Copy), r=[PSK[pb]], w=[xk])
                else:
                    P.op('dve', lambda e, pb=pb, xb=xb, kq=kq: e.tensor_copy(
                        out=xb[:, kq * 512:(kq + 1) * 512], in_=ps[pb][:, :]), r=[PSK[pb]], w=[xk])
            P.op('sp', lambda e, xb=xb, tt=tt: e.dma_start(out=out[tt * 128:(tt + 1) * 128, :], in_=xb),
                 r=[xk], w=["out"], dma=True)

    if mode == 'full':
        prenorm_stage(0, True)
        ffn_core(0, 0)

    if 'h1T' in dbg:
        for dc in range(KC):
            P.op('sp', lambda e, dc=dc: e.dma_start(out=dbg['h1T'][dc, :, :], in_=hT_d[dc, :, :]),
                 r=[("hT_d", dc)], dma=True)

    if mode == 'full':
        hm = prenorm_stage(1, False)
    for c in range(8):
        if mode == 'full':
            P.op('sp', lambda e, c=c: e.dma_start(
                out=hmT_d[c * 256:(c + 1) * 256, :].rearrange("(k p) t -> p k t", p=128), in_=hm[:, 2 * c:2 * c + 2, :]),
                r=["hm"], w=[("hmT_d", c)], dma=True)
        else:
            P.op('sp', lambda e, c=c: e.dma_start(out=hmT_d[c * 256:(c + 1) * 256, :],
                                                   in_=hm_in[c * 256:(c + 1) * 256, :]), w=[("hmT_d", c)], dma=True)
        P.op('pool', lambda e, c=c: e.collective_compute(
            "AllGather", ALU.bypass, replica_groups=groups,
            ins=[hmT_d[c * 256:(c + 1) * 256, :].opt()], outs=[hmT_all[c * 1024:(c + 1) * 1024, :].opt()]),
            r=[("hmT_d", c)], w=["hmT_all"], cc=True)
    P.barrier()
    STG = os.environ.get("K_STAGES", "inproj,gla,na,ag2").split(",")
    if "inproj" in STG:
        mixer_inproj()
    def ag2(cs_):
        for c in cs_:
            P.op('pool', lambda e, c=c: e.collective_compute(
                "AllGather", ALU.bypass, replica_groups=groups,
                ins=[mixT_d[c * 384:(c + 1) * 384, :].opt()], outs=[mixin_d[c * 1536:(c + 1) * 1536, :].opt()]),
                w=[("mixin", c)], cc=True)

    for nm_, fn_ in (("gla", gla), ("na", na)):
        if nm_ in STG:
            A.reset(0)
            for _ in fn_():
                pass
            P.barrier()
        if nm_ == "gla" and "ag2" in STG:
            ag2((0, 2, 4, 6))
    if "ag2" in STG:
        ag2((1, 3, 5, 7))
    P.barrier()
    if 'mixT' in dbg:
        P.op('sp', lambda e: e.dma_start(out=dbg['mixT'][:, :], in_=mixT_d[:, :]), dma=True)
        P.barrier()
    if mode == 'full':
        merge_out()
        prenorm_stage(2, False)
        ffn_core(2, 1)
        final_out()
    P.barrier()
    P.emit(nc, st)
    return nc, st


def _consts():
    th = 10000.0
    freqs = th ** (-np.arange(64, dtype=np.float32) / 64.0)
    t = np.arange(SEQ)
    row, col = (t // 64).astype(np.float32), (t % 64).astype(np.float32)
    ropeC = np.zeros((2, 128, SEQ), np.float32)
    ropeS = np.zeros((2, 128, SEQ), np.float32)
    for c, pos in enumerate((row, col)):
        ang = pos[None, :] * freqs[:, None]
        ropeC[c, :64], ropeC[c, 64:] = np.cos(ang), np.cos(ang)
        ropeS[c, :64], ropeS[c, 64:] = np.sin(ang), np.sin(ang)
    rotT = np.zeros((128, 128), np.float32)
    for m in range(64):
        rotT[m + 64, m] = -1.0
        rotT[m, m + 64] = 1.0
    cmask = np.ones((128, 512), np.float32)
    cmask[:, ::64] = 0.0
    ii = np.arange(64)
    trm = np.zeros((64, 2, 64), np.float32)
    trm[:, 0, :] = (ii[None, :] >= ii[:, None])
    trm[:, 1, :] = (ii[None, :] <= ii[:, None])
    return dict(ropeC=ropeC, ropeS=ropeS, rotT=rotT, cmask=cmask, trm=np.ascontiguousarray(trm.reshape(64, 128)),
                identf=np.eye(128, dtype=np.float32), identb=np.eye(128).astype(ml_dtypes.bfloat16))


def _prep_inputs(inputs, ncores=8, mode='full', hm_in=None):
    f = lambda k: np.asarray(inputs[k], np.float32)
    x, c, ctx, c_ctx = f('x'), f('c'), f('ctx'), f('c_ctx')
    w_in = f('w_in')[0]
    gla_wg, gla_bg, gng = f('gla_wg')[0], f('gla_bg')[0], f('gla_norm_g')[0]
    rpb = f('na_rpb')[0]
    shared = _consts()
    shared['w_m'] = np.ascontiguousarray(w_in[:, 9248:13344])
    shared['gng'] = np.ascontiguousarray(gng.reshape(1, 512))
    shared['w_gla_o'] = np.ascontiguousarray(f('w_gla_o')[0])
    shared['w_na_o'] = np.ascontiguousarray(f('w_na_o')[0])
    shared['w_out'] = np.ascontiguousarray(f('w_out')[0])
    w_ada_full = f('w_ada')[0]
    bT_full = np.ascontiguousarray(f('b_ada')[0].reshape(144, 128).T)
    shared['gT'] = np.ascontiguousarray(f('norm_g')[0].reshape(6, KC, 128).transpose(2, 0, 1))
    shared['ffn_wg'] = np.ascontiguousarray(f('ffn_wg')[0])
    shared['ffn_wu'] = np.ascontiguousarray(f('ffn_wu')[0])
    shared['ffn_wd'] = np.ascontiguousarray(f('ffn_wd')[0])
    q = np.arange(64)
    k = np.arange(64)
    dj = np.clip(k[None, :] - q[:, None], -15, 15) + 15
    cs = np.clip(q - 8, 0, 48)
    ok = (k[None, :] >= cs[:, None]) & (k[None, :] < cs[:, None] + 16)
    in_maps = []
    for core in range(ncores):
        b, h = core // 4, core % 4
        s = h
        m = dict(shared)
        xin = np.concatenate([x[b, s * 1024:(s + 1) * 1024], ctx[b, s * 64:(s + 1) * 64]], axis=0)
        m['xin'] = np.ascontiguousarray(xin)
        m['w_ada'] = np.ascontiguousarray(w_ada_full[:, s * 4608:(s + 1) * 4608])
        m['bT'] = np.ascontiguousarray(bT_full[:, s * 36:(s + 1) * 36])
        selv = np.zeros((128, 4), np.float32)
        selv[:, s] = 1.0
        m['sel'] = selv
        cin = np.stack([c[b], c_ctx], axis=0)
        m['cT'] = np.ascontiguousarray(cin.reshape(2, KC, 128).transpose(2, 1, 0))
        cols = np.concatenate([
            np.arange(h * 256, (h + 1) * 256), 1024 + np.arange(h * 256, (h + 1) * 256),
            np.arange(6144, 6176),
            6176 + np.arange(h * 256, (h + 1) * 256), 7200 + np.arange(h * 256, (h + 1) * 256),
            2048 + np.arange(h * 512, (h + 1) * 512), 4096 + np.arange(h * 512, (h + 1) * 512),
            8224 + np.arange(h * 256, (h + 1) * 256)])
        m['w_inh'] = np.ascontiguousarray(w_in[:, cols])
        m['gwg'] = np.ascontiguousarray(gla_wg[:, :, h * 256:(h + 1) * 256].transpose(1, 0, 2))
        m['gbgT'] = np.ascontiguousarray(gla_bg[:, h * 256:(h + 1) * 256].reshape(2, 2, 128).transpose(2, 0, 1).reshape(128, 4))
        tbl = np.empty((4, 64, 15, 64), np.float32)
        for hh in range(4):
            g = rpb[4 * h + hh][:, dj]
            tbl[hh] = np.where(ok[None], g, np.float32(-1e30)).transpose(1, 0, 2)
        m['tb'] = np.ascontiguousarray(tbl.reshape(4, 64, 15 * 64))
        if mode != 'full':
            m['hm_in'] = hm_in[core]
            for kk in ('xin', 'cT', 'w_ada', 'bT', 'ffn_wg', 'ffn_wu', 'ffn_wd'):
                m[kk] = np.zeros((1,), np.float32)
        in_maps.append(m)
    return in_maps


def kernel(**inputs):
    nc, st = build()
    in_maps = _prep_inputs(inputs)
    res = run_bass_kernel_spmd(nc, in_maps, core_ids=list(range(8)))
    outp = np.zeros((2, SEQ, D), np.float32)
    for core in range(8):
        b, s = core // 4, core % 4
        outp[b, s * 1024:(s + 1) * 1024] = res.results[core]["out"]
    return outp
```
